# Optimizing a Trainium2 kernel written in Bass

```python
import math
import jax
import jax.numpy as jnp
from jax import lax
import numpy as np

D_MODEL = 4096
BATCH = 4
SEQ = 4096
DEPTH = 1
DEC_BATCH = 1
DEC_SEQ = 16384
PAST_LEN = 128

MEM_LEN = 256
HEAD_DIM = 128
MIX_WIDTH = D_MODEL
CONV_WIDTH = MIX_WIDTH // 2
ATTN_WIDTH = MIX_WIDTH - CONV_WIDTH
N_HEADS = ATTN_WIDTH // HEAD_DIM
N_KV_HEADS = 4
GROUP = N_HEADS // N_KV_HEADS
KV_WIDTH = N_KV_HEADS * HEAD_DIM
CONV_K = 31
WINDOW = 128
BLOCK = 128
N_BUCKETS = 32
MAX_DISTANCE = 128
X_HEADS = 4
X_WIDTH = X_HEADS * HEAD_DIM
D_FF = -(-8 * D_MODEL // (3 * 256)) * 256
IN_COLS = 2 * CONV_WIDTH + ATTN_WIDTH + 2 * KV_WIDTH
ALPHA = (2 * DEPTH) ** 0.25
BETA = (8 * DEPTH) ** -0.25
LN_EPS = 1e-5
NEG_INF = -1e30

kernel_name = "hybrid_conv_swa_memory_encoder"


def layer_norm(x, g, b):
    xf = x.astype(jnp.float32)
    mu = jnp.mean(xf, axis=-1, keepdims=True)
    var = jnp.mean(jnp.square(xf - mu), axis=-1, keepdims=True)
    return ((xf - mu) * lax.rsqrt(var + LN_EPS) * g + b).astype(x.dtype)


def t5_bucket(rel):
    nb = N_BUCKETS // 2
    max_exact = nb // 2
    ret = jnp.where(rel > 0, nb, 0)
    n = jnp.abs(rel)
    nf = jnp.maximum(n, 1).astype(jnp.float32)
    large = max_exact + (jnp.log(nf / max_exact) / math.log(MAX_DISTANCE / max_exact)
                         * (nb - max_exact)).astype(jnp.int32)
    large = jnp.minimum(large, nb - 1)
    return ret + jnp.where(n < max_exact, n, large)


def conv_group(c_val, c_gate, conv_w, conv_b, ln_g, ln_b):
    u = c_val * jax.nn.sigmoid(c_gate)
    h = lax.conv_general_dilated(
        u, conv_w[:, None, :].astype(u.dtype), window_strides=(1,),
        padding=[(CONV_K // 2, CONV_K // 2)],
        dimension_numbers=('NWC', 'WIO', 'NWC'),
        feature_group_count=CONV_WIDTH) + conv_b
    h = layer_norm(h, ln_g, ln_b)
    return jax.nn.silu(h)


def window_attention(q, k, v, sink, rel_bias):
    B, S = q.shape[0], q.shape[1]
    nb = S // BLOCK
    qb = q.reshape(B, nb, BLOCK, N_KV_HEADS, GROUP, HEAD_DIM)
    pad = ((0, 0), (BLOCK, BLOCK), (0, 0), (0, 0))
    kp = jnp.pad(k, pad)
    vp = jnp.pad(v, pad)
    idx = jnp.arange(nb)[:, None] * BLOCK + jnp.arange(3 * BLOCK)[None, :]
    kb = kp[:, idx]
    vb = vp[:, idx]
    s = jnp.einsum('bnqkgd,bnskd->bnkgqs', qb, kb,
                   preferred_element_type=jnp.float32) * (HEAD_DIM ** -0.5)
    rel = jnp.arange(3 * BLOCK)[None, :] - BLOCK - jnp.arange(BLOCK)[:, None]
    bias = rel_bias[t5_bucket(rel)].astype(jnp.float32)
    bias = jnp.transpose(bias, (2, 0, 1)).reshape(N_KV_HEADS, GROUP, BLOCK, 3 * BLOCK)
    key_pos = idx - BLOCK
    key_ok = (key_pos >= 0) & (key_pos < S)
    mask = (jnp.abs(rel) <= WINDOW)[None] & key_ok[:, None, :]
    s = jnp.where(mask[None, :, None, None], s + bias, NEG_INF)
    sk = sink.astype(jnp.float32).reshape(N_KV_HEADS, GROUP, 1, 1)
    m = jnp.maximum(jnp.max(s, axis=-1, keepdims=True), sk)
    p = jnp.exp(s - m)
    denom = jnp.sum(p, axis=-1, keepdims=True) + jnp.exp(sk - m)
    p = (p / denom).astype(v.dtype)
    o = jnp.einsum('bnkgqs,bnskd->bnqkgd', p, vb)
    return o.reshape(B, S, ATTN_WIDTH)


def memory_cross_attention(x, mem, xq_w, xk_w, xv_w, xo_w):
    B, S = x.shape[0], x.shape[1]
    q = (x @ xq_w).reshape(B, S, X_HEADS, HEAD_DIM)
    k = (mem @ xk_w).reshape(B, MEM_LEN, X_HEADS, HEAD_DIM)
    v = (mem @ xv_w).reshape(B, MEM_LEN, X_HEADS, HEAD_DIM)
    s = jnp.einsum('bqhd,bmhd->bhqm', q, k,
                   preferred_element_type=jnp.float32) * (HEAD_DIM ** -0.5)
    p = jax.nn.softmax(s, axis=-1).astype(v.dtype)
    o = jnp.einsum('bhqm,bmhd->bqhd', p, v).reshape(B, S, X_WIDTH)
    return o @ xo_w


def encoder_layer(x, mem, rel_bias, w_in, conv_w, conv_b, conv_ln_g, conv_ln_b, sink,
                  w_out, ln1_g, ln1_b, xq_w, xk_w, xv_w, xo_w, ln2_g, ln2_b,
                  w_gate, w_up, w_down, ln3_g, ln3_b):
    B, S = x.shape[0], x.shape[1]
    h = x @ w_in
    cuts = [CONV_WIDTH, 2 * CONV_WIDTH, 2 * CONV_WIDTH + ATTN_WIDTH,
            2 * CONV_WIDTH + ATTN_WIDTH + KV_WIDTH]
    c_val, c_gate, q, k, v = jnp.split(h, cuts, axis=-1)
    conv_out = conv_group(c_val, c_gate, conv_w, conv_b, conv_ln_g, conv_ln_b)
    attn_out = window_attention(q.reshape(B, S, N_HEADS, HEAD_DIM),
                                k.reshape(B, S, N_KV_HEADS, HEAD_DIM),
                                v.reshape(B, S, N_KV_HEADS, HEAD_DIM), sink, rel_bias)
    mix = jnp.concatenate([conv_out, attn_out], axis=-1) @ w_out
    x = layer_norm(ALPHA * x + mix, ln1_g, ln1_b)
    x = layer_norm(ALPHA * x + memory_cross_attention(x, mem, xq_w, xk_w, xv_w, xo_w),
                   ln2_g, ln2_b)
    f = (jax.nn.silu(x @ w_gate) * (x @ w_up)) @ w_down
    return layer_norm(ALPHA * x + f, ln3_g, ln3_b)


def run_trunk(x, mem, rel_bias, w_in, conv_w, conv_b, conv_ln_g, conv_ln_b, sink, w_out,
              ln1_g, ln1_b, xq_w, xk_w, xv_w, xo_w, ln2_g, ln2_b, w_gate, w_up, w_down,
              ln3_g, ln3_b):
    for l in range(DEPTH):
        x = encoder_layer(x, mem, rel_bias, w_in[l], conv_w[l], conv_b[l], conv_ln_g[l],
                          conv_ln_b[l], sink[l], w_out[l], ln1_g[l], ln1_b[l], xq_w[l],
                          xk_w[l], xv_w[l], xo_w[l], ln2_g[l], ln2_b[l], w_gate[l],
                          w_up[l], w_down[l], ln3_g[l], ln3_b[l])
    return x


def setup_inputs(seed: int = 0) -> dict:
    key = jax.random.key(seed)
    ks = jax.random.split(key, 32)
    L = DEPTH

    def nrm(k, shape, scale):
        return jax.random.normal(k, shape, jnp.float32) * scale

    return {
        "x_prompt": nrm(ks[0], (BATCH, SEQ, D_MODEL), 1.0),
        "x_sample": nrm(ks[1], (DEC_BATCH, DEC_SEQ, D_MODEL), 1.0),
        "mem_prompt": nrm(ks[2], (BATCH, MEM_LEN, D_MODEL), 1.0),
        "mem_sample": nrm(ks[3], (DEC_BATCH, MEM_LEN, D_MODEL), 1.0),
        "rel_bias": nrm(ks[4], (N_BUCKETS, N_HEADS), 0.5),
        "w_in": nrm(ks[5], (L, D_MODEL, IN_COLS), D_MODEL ** -0.5),
        "conv_w": nrm(ks[6], (L, CONV_K, CONV_WIDTH), CONV_K ** -0.5),
        "conv_b": nrm(ks[7], (L, CONV_WIDTH), 0.02),
        "conv_ln_g": 1.0 + nrm(ks[8], (L, CONV_WIDTH), 0.02),
        "conv_ln_b": nrm(ks[9], (L, CONV_WIDTH), 0.02),
        "sink": nrm(ks[10], (L, N_HEADS), 0.5),
        "w_out": nrm(ks[11], (L, MIX_WIDTH, D_MODEL), MIX_WIDTH ** -0.5 * BETA),
        "ln1_g": 1.0 + nrm(ks[12], (L, D_MODEL), 0.02),
        "ln1_b": nrm(ks[13], (L, D_MODEL), 0.02),
        "xq_w": nrm(ks[14], (L, D_MODEL, X_WIDTH), D_MODEL ** -0.5),
        "xk_w": nrm(ks[15], (L, D_MODEL, X_WIDTH), D_MODEL ** -0.5),
        "xv_w": nrm(ks[16], (L, D_MODEL, X_WIDTH), D_MODEL ** -0.5 * BETA),
        "xo_w": nrm(ks[17], (L, X_WIDTH, D_MODEL), X_WIDTH ** -0.5 * BETA),
        "ln2_g": 1.0 + nrm(ks[18], (L, D_MODEL), 0.02),
        "ln2_b": nrm(ks[19], (L, D_MODEL), 0.02),
        "w_gate": nrm(ks[20], (L, D_MODEL, D_FF), D_MODEL ** -0.5),
        "w_up": nrm(ks[21], (L, D_MODEL, D_FF), D_MODEL ** -0.5 * BETA),
        "w_down": nrm(ks[22], (L, D_FF, D_MODEL), D_FF ** -0.5 * BETA),
        "ln3_g": 1.0 + nrm(ks[23], (L, D_MODEL), 0.02),
        "ln3_b": nrm(ks[24], (L, D_MODEL), 0.02),
    }


def reference(x_prompt, x_sample, mem_prompt, mem_sample, rel_bias, w_in, conv_w, conv_b,
              conv_ln_g, conv_ln_b, sink, w_out, ln1_g, ln1_b, xq_w, xk_w, xv_w, xo_w,
              ln2_g, ln2_b, w_gate, w_up, w_down, ln3_g, ln3_b):
    y_prompt = run_trunk(x_prompt, mem_prompt, rel_bias, w_in, conv_w, conv_b, conv_ln_g,
                         conv_ln_b, sink, w_out, ln1_g, ln1_b, xq_w, xk_w, xv_w, xo_w,
                         ln2_g, ln2_b, w_gate, w_up, w_down, ln3_g, ln3_b)
    y_sample = run_trunk(x_sample, mem_sample, rel_bias, w_in, conv_w, conv_b, conv_ln_g,
                         conv_ln_b, sink, w_out, ln1_g, ln1_b, xq_w, xk_w, xv_w, xo_w,
                         ln2_g, ln2_b, w_gate, w_up, w_down, ln3_g, ln3_b)
    return (y_prompt, y_sample)
```

```python
import math
from contextlib import ExitStack

import numpy as np
import concourse.bass as bass
import concourse.mybir as mybir
from concourse.bass_utils import run_bass_kernel_spmd

F32 = mybir.dt.float32
BF16 = mybir.dt.bfloat16
ALU = mybir.AluOpType
AF = mybir.ActivationFunctionType

D = 4096
KC = D // 128
CW = 2048
NCH = CW // 128
NH = 16
NKV = 4
XH = 4
MEM = 256
CONV_K = 31
T = 512
HALO = 128
TW = T + 2 * HALO
ALPHA = 2.0 ** 0.25
LN_EPS = 1e-5
SCALE = 128.0 ** -0.5
NEG = -30000.0
N_CORES = 8


class Res:
    __slots__ = ("w", "r")

    def __init__(self, seed=None):
        self.w = None
        self.r = dict(seed) if seed else {}


class Area:
    def __init__(self):
        self.cur = []

    def switch(self):
        seed = {}
        for res in self.cur:
            if res.w is not None:
                seed[("w", res.w)] = res.w
            for k, v in res.r.items():
                if k in seed:
                    seed[k] = max(seed[k], v)
                else:
                    seed[k] = v
        self.cur = []
        self.seed = seed
        return seed

    def res(self, extra_seed=None):
        s = dict(getattr(self, "seed", {}))
        if extra_seed:
            for k, v in extra_seed.items():
                s[k] = max(s.get(k, -1), v)
        r = Res(s)
        self.cur.append(r)
        return r

    def adopt(self, r):
        self.cur.append(r)


class Prog:
    def __init__(self):
        self.ins = []

    def op(self, eng, fn, reads=(), writes=(), lane=None):
        idx = len(self.ins)
        deps = set()
        for r in reads:
            if r.w is not None:
                deps.add(r.w)
        for w in writes:
            if w.w is not None:
                deps.add(w.w)
            deps.update(w.r.values())
        key = eng if lane is None else ("dma", idx)
        for r in reads:
            r.r[key] = idx
        for w in writes:
            w.w = idx
            w.r = {}
        self.ins.append([eng, fn, deps, lane, False, 0])
        return idx

    def emit(self, nc, es, final_wait_lanes):
        ins = self.ins
        engs = ("tensor", "vector", "scalar", "sync", "gpsimd")
        for rec in ins:
            for d in rec[2]:
                p = ins[d]
                if p[3] is None and not (p[0] == "tensor" and rec[0] == "tensor"):
                    p[4] = True
        cnt = {}
        lanes = {}
        for rec in ins:
            if rec[3] is not None:
                lanes[rec[3]] = lanes.get(rec[3], 0) + 16
                rec[5] = lanes[rec[3]]
            elif rec[4]:
                cnt[rec[0]] = cnt.get(rec[0], 0) + 1
                rec[5] = cnt[rec[0]]
        sem_e = {e: es.enter_context(nc.semaphore("se_" + e)) for e in ("tensor", "vector", "scalar", "gpsimd")}
        sem_l = {l: es.enter_context(nc.semaphore("sl_" + l)) for l in lanes}
        streams = {e: [] for e in engs}
        for i, rec in enumerate(ins):
            streams[rec[0]].append(i)
        block = es.enter_context(nc.Block())

        def run(e, name):
            waited = {}
            for i in streams[name]:
                rec = ins[i]
                need = {}
                for d in rec[2]:
                    p = ins[d]
                    if p[3] is not None:
                        k = ("l", p[3])
                    else:
                        if p[0] == "tensor" and name == "tensor":
                            continue
                        k = ("e", p[0])
                    if p[5] > need.get(k, 0):
                        need[k] = p[5]
                for k, v in need.items():
                    if waited.get(k, 0) >= v:
                        continue
                    waited[k] = v
                    e.wait_ge(sem_l[k[1]] if k[0] == "l" else sem_e[k[1]], v)
                inst = rec[1](e)
                if rec[3] is not None:
                    inst.then_inc(sem_l[rec[3]], 16)
                elif rec[4]:
                    inst.then_inc(sem_e[name], 1)
            if name == "sync":
                for l in final_wait_lanes:
                    if l in lanes:
                        e.wait_ge(sem_l[l], lanes[l])

        @block.tensor
        def _(e):
            run(e, "tensor")

        @block.vector
        def _(e):
            run(e, "vector")

        @block.scalar
        def _(e):
            run(e, "scalar")

        @block.gpsimd
        def _(e):
            run(e, "gpsimd")

        @block.sync
        def _(e):
            run(e, "sync")


DEBUG = False


def build_program(NT, DFF):
    NB = NT * 4 + 2
    NTOK = NT * T
    IN_COLS = 7168
    assert DFF % 256 == 0
    NFC = DFF // 128
    nc = bass.Bass("TRN2", target_bir_lowering=False)

    def din(name, shape):
        return nc.dram_tensor(name, list(shape), F32, kind="ExternalInput").ap()

    xpad = din("xpad", [NTOK + 2 * HALO, D])
    mem = din("mem", [MEM, D])
    kbias_d = din("kbias", [128, NB])
    w_in = din("w_in", [D, IN_COLS])
    w_out = din("w_out", [D, D])
    xq_w = din("xq_w", [D, 512])
    xk_w = din("xk_w", [D, 512])
    xv_w = din("xv_w", [D, 512])
    xo_w = din("xo_w", [512, D])
    w_gate = din("w_gate", [D, DFF])
    w_up = din("w_up", [D, DFF])
    w_down = din("w_down", [DFF, D])
    cpar_d = din("cpar", [34, CW])
    lnp = din("lnp", [6, D])
    relb_d = din("relb", [33, NH])
    sink_d = din("sink", [1, NH])
    ident_d = din("ident", [128, 128])
    jmat_d = din("jmat", [128, 128])
    oh_d = din("oh", [33, 512])
    y = nc.dram_tensor("y", [NTOK, D], F32, kind="ExternalOutput").ap()
    gtab_t = nc.dram_tensor("gtab", [16, 512], F32, kind="ExternalOutput" if DEBUG else "Internal")
    gtab_d = gtab_t.ap()
    bd_d = nc.dram_tensor("bd", [3, 128, 2048], F32, kind="ExternalOutput" if DEBUG else "Internal").ap()

    if DEBUG:
        dbg_mix = nc.dram_tensor("dbg_mix", [128, 16384], BF16, kind="ExternalOutput").ap()
        dbg_x1 = nc.dram_tensor("dbg_x1", [512, D], F32, kind="ExternalOutput").ap()
        dbg_x2 = nc.dram_tensor("dbg_x2", [512, D], F32, kind="ExternalOutput").ap()
        dbg_r0 = nc.dram_tensor("dbg_r0", [512, D], F32, kind="ExternalOutput").ap()
        dbg_R = nc.dram_tensor("dbg_R", [128, 16384], F32, kind="ExternalOutput").ap()
    P = Prog()
    es = ExitStack()
    E = es.enter_context

    Rt = E(nc.sbuf_tensor("Rt", [128, 16384], F32))
    Xt = E(nc.sbuf_tensor("Xt", [128, 24576], BF16))
    Mt = E(nc.sbuf_tensor("Mt", [128, 16384], BF16))
    slabs = [E(nc.sbuf_tensor(f"slab{i}", [128, 8192], BF16)) for i in range(3)]
    T4 = E(nc.sbuf_tensor("T4", [128, 1024], F32))
    ident = E(nc.sbuf_tensor("ident_s", [128, 128], F32))
    jmat = E(nc.sbuf_tensor("jmat_s", [128, 128], F32))
    ones_f = E(nc.sbuf_tensor("ones_f", [128, 128], F32))
    ones_b = E(nc.sbuf_tensor("ones_b", [128, 128], BF16))
    eps_t = E(nc.sbuf_tensor("eps_t", [128, 1], F32))
    cw = E(nc.sbuf_tensor("cw", [128, NCH, 34], F32))
    esk = E(nc.sbuf_tensor("esk", [128, NH], F32))
    kbias = E(nc.sbuf_tensor("kbias_s", [128, NB], F32))
    KxT = E(nc.sbuf_tensor("KxT", [128, XH, MEM], BF16))
    Vx = E(nc.sbuf_tensor("Vx", [128, 2, 512], BF16))
    stats = E(nc.sbuf_tensor("stats", [128, 4, 8, 6], F32))
    mv = E(nc.sbuf_tensor("mv", [128, 4, 2], F32))
    rstd = E(nc.sbuf_tensor("rstd", [128, 4], F32))
    nmr = E(nc.sbuf_tensor("nmr", [128, 4], F32))
    carry = E(nc.sbuf_tensor("carry", [128, NCH, 32], F32))
    gcol = E(nc.sbuf_tensor("gcol", [128, 128], F32))
    carryR = [Res() for _ in range(NCH)]
    banks = [E(nc.psum_tensor(f"bank{i}", [128, 512], F32)) for i in range(8)]

    Xf = Xt[:].bitcast(F32)
    Mf = Mt[:].bitcast(F32)
    Rb = Rt[:].bitcast(BF16)

    bankR = [Res() for _ in range(8)]
    slabR = [Res() for _ in range(3)]
    constR = Res()
    aR, aXA, aXB, aM, aT4 = Area(), Area(), Area(), Area(), Area()
    for a in (aR, aXA, aXB, aM, aT4):
        a.switch()

    st = {"bank": 0, "slab": 0, "ev": 0}

    st["nring"] = 6

    def ring():
        b = st["bank"] % st["nring"]
        st["bank"] = (b + 1) % st["nring"]
        return b

    def load_slab(src, shape3):
        s = st["slab"]
        st["slab"] = (s + 1) % 3
        a, b = shape3
        view = slabs[s][:, 0:a * b].rearrange("p (a b) -> p a b", a=a)
        P.op("gpsimd", lambda e, v=view, sr=src: e.dma_start(out=v, in_=sr), writes=[slabR[s]], lane=f"slab{s}")
        return view, slabR[s]

    def fm_slab(W, c0, ncols=256):
        return load_slab(W[:, c0:c0 + ncols].rearrange("(k p) c -> p k c", p=128), (KC, ncols))

    def mm(out, lhsT, rhs, start, stop, reads, writes):
        P.op("tensor", lambda e: e.matmul(out, lhsT, rhs, start=start, stop=stop), reads, writes)

    def tr(out, in_, idn, reads, writes):
        P.op("tensor", lambda e: e.transpose(out, in_, idn), reads, writes)

    def act(out, in_, func, reads, writes, bias=None, scale=None):
        kw = {}
        if bias is not None:
            kw["bias"] = bias
        if scale is not None:
            kw["scale"] = scale
        P.op("scalar", lambda e: e.activation(out, in_, func, **kw), reads, writes)

    def evac(out, in_, reads, writes):
        st["ev"] ^= 1
        if st["ev"]:
            P.op("scalar", lambda e: e.activation(out, in_, AF.Copy), reads, writes)
        else:
            P.op("vector", lambda e: e.tensor_copy(out=out, in_=in_), reads, writes)

    def vtt(out, in0, in1, op, reads, writes):
        P.op("vector", lambda e: e.tensor_tensor(out=out, in0=in0, in1=in1, op=op), reads, writes)

    def vts(out, in0, s1, s2, op0, op1, reads, writes):
        if op1 is None:
            P.op("vector", lambda e: e.tensor_scalar(out=out, in0=in0, scalar1=s1, scalar2=None, op0=op0), reads, writes)
        else:
            P.op("vector", lambda e: e.tensor_scalar(out=out, in0=in0, scalar1=s1, scalar2=s2, op0=op0, op1=op1), reads, writes)

    def vstt(out, in0, scalar, in1, op0, op1, reads, writes):
        P.op("vector", lambda e: e.scalar_tensor_tensor(out=out, in0=in0, scalar=scalar, in1=in1, op0=op0, op1=op1), reads, writes)

    def sdma(out, in_, reads, writes, lane):
        P.op("sync", lambda e: e.dma_start(out=out, in_=in_), reads, writes, lane=lane)

    sdma(ident[:], ident_d, [], [constR], "c0")
    sdma(jmat[:], jmat_d, [], [constR], "c1")
    sdma(kbias[:], kbias_d, [], [constR], "c2")
    P.op("vector", lambda e: e.memset(ones_f[:], 1.0), [], [constR])
    P.op("vector", lambda e: e.memset(ones_b[:], 1.0), [], [constR])
    P.op("vector", lambda e: e.memset(eps_t[:], LN_EPS), [], [constR])

    cparR = aM.res()
    cpar_s = Mf[0:34, 0:CW]
    sdma(cpar_s, cpar_d, [], [cparR], "c3")
    cwR = Res()
    for c in range(NCH):
        b = ring()
        tr(banks[b][:, 0:34], Mf[0:34, c * 128:(c + 1) * 128], ident[0:34, 0:34], [cparR, constR], [bankR[b]])
        evac(cw[:, c, :], banks[b][:, 0:34], [bankR[b]], [cwR])

    eskR = Res()
    sdma(esk[:], sink_d[0:1, :].broadcast_to([128, NH]), [], [eskR], "c4")
    act(esk[:], esk[:], AF.Exp, [eskR], [eskR])

    relbR = aM.res()
    relb_s = Mf[0:33, 4096:4096 + NH]
    oh_s = Mf[0:33, 4608:4608 + 512]
    sdma(relb_s, relb_d, [], [relbR], "c5")
    sdma(oh_s, oh_d, [], [relbR], "c6")
    b = ring()
    mm(banks[b][0:16, :], relb_s, oh_s, True, True, [relbR], [bankR[b]])
    gsbR = aM.res()
    gsb = Mf[0:16, 5120:5120 + 512]
    evac(gsb, banks[b][0:16, :], [bankR[b]], [gsbR])
    gtabR = Res()
    sdma(gtab_d, gsb, [gsbR], [gtabR], "c7")
    BdR = [Res() for _ in range(3)]
    for o in (-1, 0, 1):
        hkR = aR.res()
        hk = Rt[:, (o + 1) * 2048:(o + 2) * 2048].rearrange("p (h k) -> p h k", h=NH)
        src = bass.AP(tensor=gtab_t, offset=128 * o + 129, ap=[[1, 128], [512, NH], [1, 128]])
        sdma(hk, src, [gtabR], [hkR], f"hk{o + 1}")
        btR = aR.res()
        bt = Rt[:, 6144 + (o + 1) * 2048:6144 + (o + 2) * 2048]
        for hq in range(4):
            b = ring()
            for hh in range(4):
                mm(banks[b][:, hh * 128:(hh + 1) * 128], hk[:, hq * 4 + hh, :], jmat[:], True, True,
                   [hkR, constR], [bankR[b]])
            evac(bt[:, hq * 512:(hq + 1) * 512], banks[b][:], [bankR[b]], [btR])
        sdma(bd_d[o + 1], bt, [btR], [BdR[o + 1]], f"bdw{o + 1}")

    gcolR = Res()
    for li in range(2):
        lncR = aM.res()
        lnc_s = Mf[0:64, 6144 + li * 128:6144 + (li + 1) * 128]
        sdma(lnc_s, lnp[2 * li:2 * li + 2, :].rearrange("a (k p) -> (a k) p", p=128), [], [lncR], f"c{8 + li}")
        b = ring()
        tr(banks[b][:, 0:64], lnc_s, ident[0:64, 0:64], [lncR, constR], [bankR[b]])
        evac(gcol[:, li * 64:(li + 1) * 64], banks[b][:, 0:64], [bankR[b]], [gcolR])

    aM.switch()
    stgR = [aM.res(), aM.res()]
    stg = [Mf[:, 0:4096], Mf[:, 4096:8192]]
    aXA.switch()
    memTR = aXA.res()
    memT = Xt[:, 0:KC * MEM].rearrange("p (k m) -> p k m", k=KC)
    for mb in range(2):
        sdma(stg[mb], mem[mb * 128:(mb + 1) * 128, :], [], [stgR[mb]], f"stg{mb}")
        for kq in range(8):
            b = ring()
            for kk in range(4):
                k = kq * 4 + kk
                tr(banks[b][:, kk * 128:(kk + 1) * 128], stg[mb][:, k * 128:(k + 1) * 128], ident[:],
                   [stgR[mb], constR], [bankR[b]])
            evac(memT[:, kq * 4:kq * 4 + 4, mb * 128:(mb + 1) * 128],
                 banks[b][:].rearrange("p (a b) -> p a b", a=4), [bankR[b]], [memTR])
    KxR = Res()
    VxR = Res()
    for p in range(2):
        sl, slR = fm_slab(xk_w, p * 256)
        for j in range(2):
            h = 2 * p + j
            b = ring()
            for k in range(KC):
                mm(banks[b][:, 0:MEM], sl[:, k, j * 128:(j + 1) * 128], memT[:, k, :], k == 0, k == KC - 1,
                   [slR, memTR], [bankR[b]])
            evac(KxT[:, h, :], banks[b][:, 0:MEM], [bankR[b]], [KxR])
    for p in range(2):
        sl, slR = fm_slab(xv_w, p * 256)
        for mb in range(2):
            b = ring()
            for k in range(KC):
                mm(banks[b][:, 0:256], memT[:, k, mb * 128:(mb + 1) * 128], sl[:, k, :], k == 0, k == KC - 1,
                   [slR, memTR], [bankR[b]])
            evac(Vx[:, mb, p * 256:(p + 1) * 256], banks[b][:, 0:256], [bankR[b]], [VxR])

    smallR = [Res() for _ in range(4)]

    def ln_stats(resid, residR, t, n):
        P.op("vector", lambda e, t=t, n=n: e.bn_stats(out=stats[:, t, n, :], in_=resid[t][:, n * 512:(n + 1) * 512]),
             [residR[t][n]], [smallR[t]])

    def ln_tail():
        for t in range(4):
            P.op("vector", lambda e, t=t: e.bn_aggr(out=mv[:, t, :], in_=stats[:, t].rearrange("p a b -> p (a b)")),
                 [smallR[t]], [smallR[t]])
        act(rstd[:, 0:4], mv[:, :, 1], AF.Ln, smallR + [constR], smallR, bias=eps_t[:, 0:1])
        act(rstd[:, 0:4], rstd[:, 0:4], AF.Exp, smallR, smallR, scale=-0.5)
        vstt(nmr[:, 0:4], mv[:, :, 0], -1.0, rstd[:, 0:4], ALU.mult, ALU.mult, smallR, smallR)

    def layer_norm(resid, residR, li, gbR, gb):
        ln_tail()
        for cg in range(4):
            bf = cg % 2
            c0 = cg * 1024
            sdma(gb[bf][0], lnp[2 * li:2 * li + 1, c0:c0 + 1024].broadcast_to([128, 1024]), [], [gbR[bf][0]], f"gbg{bf}")
            sdma(gb[bf][1], lnp[2 * li + 1:2 * li + 2, c0:c0 + 1024].broadcast_to([128, 1024]), [], [gbR[bf][1]], f"gbb{bf}")
            for t in range(4):
                rr = [residR[t][2 * cg], residR[t][2 * cg + 1]]
                sl_ = resid[t][:, c0:c0 + 1024]
                act(sl_, sl_, AF.Identity, rr + [smallR[t]], rr, bias=nmr[:, t:t + 1], scale=rstd[:, t:t + 1])
                vtt(sl_, sl_, gb[bf][0], ALU.mult, rr + [gbR[bf][0]], rr)
                vtt(sl_, sl_, gb[bf][1], ALU.add, rr + [gbR[bf][1]], rr)

    def ln_norm_T(resid, residR, li, dstT, dstR):
        ln_tail()
        for cg in range(4):
            c0 = cg * 1024
            for t in range(4):
                rr = [residR[t][2 * cg], residR[t][2 * cg + 1]]
                sl_ = resid[t][:, c0:c0 + 1024]
                act(sl_, sl_, AF.Identity, rr + [smallR[t]], rr, bias=nmr[:, t:t + 1], scale=rstd[:, t:t + 1])
        for k in range(KC):
            b = ring()
            for t in range(4):
                tr(banks[b][:, t * 128:(t + 1) * 128], resid[t][:, k * 128:(k + 1) * 128], ident[:],
                   [residR[t][k // 4], constR], [bankR[b]])
            gc = gcol[:, li * 64 + k:li * 64 + k + 1]
            bc = gcol[:, li * 64 + 32 + k:li * 64 + 32 + k + 1]
            vts(dstT[:, k, :], banks[b][:], gc, bc, ALU.mult, ALU.add, [bankR[b], gcolR], [dstR[k]])

    def ln_affine_items(resid, residR, li, gbR, gb):
        items = []
        for cg in range(4):
            bf = cg % 2
            c0 = cg * 1024

            def load(cg=cg, bf=bf, c0=c0):
                sdma(gb[bf][0], lnp[2 * li:2 * li + 1, c0:c0 + 1024].broadcast_to([128, 1024]), [], [gbR[bf][0]], f"gbg{bf}")
                sdma(gb[bf][1], lnp[2 * li + 1:2 * li + 2, c0:c0 + 1024].broadcast_to([128, 1024]), [], [gbR[bf][1]], f"gbb{bf}")
            for t in range(4):
                def item(cg=cg, bf=bf, c0=c0, t=t, load=load):
                    if t == 0:
                        load()
                    rr = [residR[t][2 * cg], residR[t][2 * cg + 1]]
                    sl_ = resid[t][:, c0:c0 + 1024]
                    vtt(sl_, sl_, gb[bf][0], ALU.mult, rr + [gbR[bf][0]], rr)
                    vtt(sl_, sl_, gb[bf][1], ALU.add, rr + [gbR[bf][1]], rr)
                items.append(item)
        return items

    resid = [Rt[:, t * 4096:(t + 1) * 4096] for t in range(4)]
    gb = [[Xf[:, 8192 + bf * 2048:8192 + bf * 2048 + 1024], Xf[:, 8192 + bf * 2048 + 1024:8192 + (bf + 1) * 2048]]
          for bf in range(2)]
    y_lanes = [f"y{t}" for t in range(4)]

    pre_stgR = None
    for it in range(NT):
        tp0 = it * T
        if pre_stgR is None:
            aM.switch()
            stgR = [aM.res(), aM.res()]
            npre = 0
        else:
            stgR = pre_stgR
            npre = 2
        sa = aXA.switch()
        sb_ = aXB.switch()
        xTR = []
        mseed = dict(sa)
        for k_, v_ in sb_.items():
            mseed[k_] = max(mseed.get(k_, -1), v_)
        for blk in range(6):
            r = Res(mseed)
            aXA.adopt(r)
            aXB.adopt(r)
            xTR.append(r)
        xT = Xt[:, 0:KC * TW].rearrange("p (k t) -> p k t", k=KC)
        for blk in range(6):
            s = blk % 2
            if blk >= npre:
                sdma(stg[s], xpad[tp0 + blk * 128:tp0 + (blk + 1) * 128, :], [], [stgR[s]], f"stg{s}")
            for kq in range(8):
                b = ring()
                for kk in range(4):
                    k = kq * 4 + kk
                    tr(banks[b][:, kk * 128:(kk + 1) * 128], stg[s][:, k * 128:(k + 1) * 128], ident[:],
                       [stgR[s], constR], [bankR[b]])
                evac(xT[:, kq * 4:kq * 4 + 4, blk * 128:(blk + 1) * 128],
                     banks[b][:].rearrange("p (a b) -> p a b", a=4), [bankR[b]], [xTR[blk]])

        aM.switch()
        sigR = [aM.res(), aM.res()]
        uR = [aM.res(), aM.res()]
        sqR = [aM.res() for _ in range(4)]
        sig = [Mf[:, i * 544:(i + 1) * 544] for i in range(2)]
        u = [Mf[:, 1088 + i * 544:1088 + (i + 1) * 544] for i in range(2)]
        sq = [Mf[:, 2176 + i * 512:2176 + (i + 1) * 512] for i in range(4)]
        aR.switch()
        hTR = [aR.res() for _ in range(NCH)]
        qTR = [aR.res() for _ in range(NH)]
        kTR = [aR.res() for _ in range(NKV)]
        VR = [aR.res() for _ in range(6)]
        spR = aR.res()
        hT = Rt[:, 0:8192].rearrange("p (c t) -> p c t", c=NCH)
        qT = Rb[:, 16384:24576].rearrange("p (h t) -> p h t", h=NH)
        kT = Rb[:, 24576:24576 + NKV * TW].rearrange("p (g t) -> p g t", g=NKV)
        Vt = Rb[:, 27648:27648 + 6 * 512].rearrange("p (b c) -> p b c", b=6)
        sp0 = Rt[:, 15360:15872]
        sp1 = Rt[:, 15872:16384]
        SUMB, SQB = 6, 7
        allx = xTR

        def conv_stats(pcs):
            for j in range(2):
                c = 2 * pcs + j
                s4 = c % 4
                act(sq[s4], hT[:, c, :], AF.Square, [hTR[c]], [sqR[s4]])
                mm(banks[SUMB][:], ones_f[:], hT[:, c, :], c == 0, c == NCH - 1, [constR, hTR[c]], [bankR[SUMB]])
                mm(banks[SQB][:], ones_f[:], sq[s4], c == 0, c == NCH - 1, [constR, sqR[s4]], [bankR[SQB]])

        def evac_a(out, in_, reads, writes):
            P.op("scalar", lambda e: e.activation(out, in_, AF.Copy), reads, writes)

        def q_proj(pq):
            sl, slR = fm_slab(w_in, 2 * CW + pq * 256)
            for j in range(2):
                h = 2 * pq + j
                b = ring()
                for k in range(KC):
                    mm(banks[b][:], sl[:, k, j * 128:(j + 1) * 128], xT[:, k, 128:640], k == 0, k == KC - 1,
                       [slR] + allx, [bankR[b]])
                evac_a(qT[:, h, :], banks[b][:], [bankR[b]], [qTR[h]])

        def k_proj(pk):
            sl, slR = fm_slab(w_in, 2 * CW + 2048 + pk * 256)
            for j in range(2):
                g = 2 * pk + j
                bA = ring()
                bB = ring()
                for k in range(KC):
                    mm(banks[bA][:], sl[:, k, j * 128:(j + 1) * 128], xT[:, k, 0:512], k == 0, k == KC - 1,
                       [slR] + allx, [bankR[bA]])
                for k in range(KC):
                    mm(banks[bB][:, 0:256], sl[:, k, j * 128:(j + 1) * 128], xT[:, k, 512:768], k == 0, k == KC - 1,
                       [slR] + allx, [bankR[bB]])
                evac_a(kT[:, g, 0:512], banks[bA][:], [bankR[bA]], [kTR[g]])
                evac_a(kT[:, g, 512:768], banks[bB][:, 0:256], [bankR[bB]], [kTR[g]])

        def v_proj(pv):
            sl, slR = fm_slab(w_in, 2 * CW + 2048 + 512 + pv * 256)
            for blk in range(6):
                b = ring()
                for k in range(KC):
                    mm(banks[b][:, 0:256], xT[:, k, blk * 128:(blk + 1) * 128], sl[:, k, :], k == 0, k == KC - 1,
                       [slR, xTR[blk]], [bankR[b]])
                evac_a(Vt[:, blk, pv * 256:(pv + 1) * 256], banks[b][:, 0:256], [bankR[b]], [VR[blk]])

        extras = {1: lambda: k_proj(0), 3: lambda: k_proj(1), 5: lambda: v_proj(0), 7: lambda: v_proj(1)}

        for pc in range(NCH // 2):
            slV, slVR = fm_slab(w_in, pc * 256)
            slG, slGR = fm_slab(w_in, CW + pc * 256)
            if it == 0:
                sbk = ring()
            for j in range(2):
                c = 2 * pc + j
                bv = ring()
                for k in range(KC):
                    mm(banks[bv][:], slV[:, k, j * 128:(j + 1) * 128], xT[:, k, 144:656], k == 0, k == KC - 1,
                       [slVR] + allx, [bankR[bv]])
                if it == 0:
                    for k in range(KC):
                        mm(banks[sbk][:, (2 * j) * 32:(2 * j + 1) * 32], slV[:, k, j * 128:(j + 1) * 128], xT[:, k, 112:144],
                           k == 0, k == KC - 1, [slVR] + allx, [bankR[sbk]])
                bg = ring()
                for k in range(KC):
                    mm(banks[bg][:], slG[:, k, j * 128:(j + 1) * 128], xT[:, k, 144:656], k == 0, k == KC - 1,
                       [slGR] + allx, [bankR[bg]])
                if it == 0:
                    for k in range(KC):
                        mm(banks[sbk][:, (2 * j + 1) * 32:(2 * j + 2) * 32], slG[:, k, j * 128:(j + 1) * 128], xT[:, k, 112:144],
                           k == 0, k == KC - 1, [slGR] + allx, [bankR[sbk]])
                act(sig[j][:, 32:544], banks[bg][:], AF.Sigmoid, [bankR[bg]], [sigR[j]])
                if it == 0:
                    act(sig[j][:, 0:32], banks[sbk][:, (2 * j + 1) * 32:(2 * j + 2) * 32], AF.Sigmoid, [bankR[sbk]], [sigR[j]])
                    vtt(u[j][:, 0:32], banks[sbk][:, (2 * j) * 32:(2 * j + 1) * 32], sig[j][:, 0:32], ALU.mult,
                        [bankR[sbk], sigR[j]], [uR[j]])
                else:
                    act(u[j][:, 0:32], carry[:, c, :], AF.Copy, [carryR[c]], [uR[j]])
                vtt(u[j][:, 32:544], banks[bv][:], sig[j][:, 32:544], ALU.mult, [bankR[bv], sigR[j]], [uR[j]])
            if it < NT - 1:
                for j in range(2):
                    c = 2 * pc + j
                    act(carry[:, c, :], u[j][:, 512:544], AF.Copy, [uR[j]], [carryR[c]])
            q_proj(pc)
            if pc in extras:
                extras[pc]()
            for tap in range(CONV_K):
                for j in range(2):
                    c = 2 * pc + j
                    if tap == 0:
                        vts(hT[:, c, :], u[j][:, 1:513], cw[:, c, 0:1], cw[:, c, 31:32], ALU.mult, ALU.add,
                            [uR[j], cwR], [hTR[c]])
                    else:
                        vstt(hT[:, c, :], u[j][:, tap + 1:tap + 513], cw[:, c, tap:tap + 1], hT[:, c, :], ALU.mult, ALU.add,
                             [uR[j], cwR, hTR[c]], [hTR[c]])
            if pc > 0:
                conv_stats(pc - 1)
        conv_stats(NCH // 2 - 1)

        aM.switch()
        mixR = [aM.res() for _ in range(32)]
        mixT = Mt[:, 0:32 * 512].rearrange("p (c t) -> p c t", c=32)
        inv = 1.0 / CW
        vts(sp0, banks[SUMB][:], inv, None, ALU.mult, None, [bankR[SUMB]], [spR])
        vtt(sp1, sp0, sp0, ALU.mult, [spR], [spR])
        vstt(sp1, banks[SQB][:], inv, sp1, ALU.mult, ALU.subtract, [bankR[SQB], spR], [spR])
        act(sp1, sp1, AF.Ln, [spR, constR], [spR], bias=eps_t[:, 0:1])
        act(sp1, sp1, AF.Exp, [spR], [spR], scale=-0.5)
        for c in range(NCH):
            vtt(hT[:, c, :], hT[:, c, :], sp0, ALU.subtract, [hTR[c], spR], [hTR[c]])
            vtt(hT[:, c, :], hT[:, c, :], sp1, ALU.mult, [hTR[c], spR], [hTR[c]])
            act(mixT[:, c, :], hT[:, c, :], AF.Silu, [hTR[c], cwR], [mixR[c]], bias=cw[:, c, 33:34], scale=cw[:, c, 32:33])

        aXA.switch()
        BtR = [aXA.res() for _ in range(3)]
        PR = [aXA.res() for _ in range(8)]
        Pt = [Xt[:, 12288 + i * 512:12288 + (i + 1) * 512] for i in range(8)]
        for o in range(3):
            sdma(Xf[:, o * 2048:(o + 1) * 2048], bd_d[o], [BdR[o]], [BtR[o]], f"B{o}")
        st["nring"] = 8
        sp_seed = dict(spR.r)
        if spR.w is not None:
            sp_seed[("w", spR.w)] = spR.w
        sp0R = aR.res(sp_seed)
        sp1R = aR.res(sp_seed)
        iters = [(qb, g) for qb in range(4) for g in range(NKV)]
        Sb = {}
        Pl = {}

        def att_S(i):
            qb, g = iters[i]
            sb3 = [ring(), ring(), ring()]
            for o in range(3):
                kb = qb + o
                for hh in range(4):
                    h = 4 * g + hh
                    mm(banks[sb3[o]][:, hh * 128:(hh + 1) * 128], kT[:, g, kb * 128:(kb + 1) * 128],
                       qT[:, h, qb * 128:(qb + 1) * 128], True, True, [kTR[g], qTR[h]], [bankR[sb3[o]]])
            Sb[i] = sb3

        def att_sm(i):
            qb, g = iters[i]
            pl = []
            for o in range(3):
                kb = qb + o
                bk = Sb[i][o]
                pp = (3 * i + o) % 8
                vstt(banks[bk][:], banks[bk][:], SCALE, Xf[:, o * 2048 + g * 512:o * 2048 + (g + 1) * 512],
                     ALU.mult, ALU.add, [bankR[bk], BtR[o]], [bankR[bk]])
                gb_col = it * 4 + kb
                act(Pt[pp], banks[bk][:], AF.Exp, [bankR[bk], constR], [PR[pp]], bias=kbias[:, gb_col:gb_col + 1])
                pl.append(pp)
            Pl[i] = pl

        Pv = {}

        def att_pv(i):
            qb, g = iters[i]
            pl = Pl[i]
            bo = ring()
            bdn = ring()
            for o in range(3):
                kb = qb + o
                mm(banks[bo][:], Vt[:, kb, g * 128:(g + 1) * 128], Pt[pl[o]], o == 0, o == 2,
                   [VR[kb], PR[pl[o]]], [bankR[bo]])
            for o in range(3):
                mm(banks[bdn][:], ones_b[:], Pt[pl[o]], o == 0, o == 2, [constR, PR[pl[o]]], [bankR[bdn]])
            Pv[i] = (bo, bdn)

        def att_norm(i):
            qb, g = iters[i]
            bo, bdn = Pv[i]
            spb, spbR = (sp0, sp0R) if i % 2 == 0 else (sp1, sp1R)
            for hh in range(4):
                h = 4 * g + hh
                act(spb[:, hh * 128:(hh + 1) * 128], banks[bdn][:, hh * 128:(hh + 1) * 128], AF.Ln,
                    [bankR[bdn], eskR], [spbR], bias=esk[:, h:h + 1])
            act(spb, spb, AF.Exp, [spbR], [spbR], scale=-1.0)

        def att_out(i):
            qb, g = iters[i]
            bo, bdn = Pv[i]
            spb, spbR = (sp0, sp0R) if i % 2 == 0 else (sp1, sp1R)
            vtt(mixT[:, NCH + 4 * g:NCH + 4 * g + 4, qb * 128:(qb + 1) * 128],
                banks[bo][:].rearrange("p (h q) -> p h q", h=4), spb.rearrange("p (h q) -> p h q", h=4), ALU.mult,
                [bankR[bo], spbR], [mixR[NCH + 4 * g + hh] for hh in range(4)])

        att_S(0)
        att_sm(0)
        att_S(1)
        att_sm(1)
        for i in range(len(iters)):
            att_pv(i)
            if i + 2 < len(iters):
                att_S(i + 2)
            att_norm(i)
            if i + 2 < len(iters):
                att_sm(i + 2)
            att_out(i)
        st["nring"] = 6

        if DEBUG and it == 0:
            sdma(dbg_R, Rt[:], hTR + qTR + kTR + VR + [spR], [], "dbgR")
        aR.switch()
        residR = [[aR.res() for _ in range(8)] for _ in range(4)]
        for t in range(4):
            r0 = tp0 + HALO + t * 128
            sdma(resid[t], xpad[r0:r0 + 128, :], [], residR[t], f"res{t}")
        aXB.switch()
        gbR = [[aXB.res(), aXB.res()] for _ in range(2)]
        for n in range(8):
            bb = [ring() for _ in range(4)]
            for s in range(2):
                sl, slR = load_slab(w_out[s * 2048:(s + 1) * 2048, n * 512:(n + 1) * 512].rearrange("(k p) c -> p k c", p=128),
                                    (16, 512))
                for t in range(4):
                    for kk in range(16):
                        c = s * 16 + kk
                        mm(banks[bb[t]][:], mixT[:, c, t * 128:(t + 1) * 128], sl[:, kk, :], s == 0 and kk == 0,
                           s == 1 and kk == 15, [slR, mixR[c]], [bankR[bb[t]]])
            for t in range(4):
                rs = resid[t][:, n * 512:(n + 1) * 512]
                vstt(rs, rs, ALPHA, banks[bb[t]][:], ALU.mult, ALU.add, [bankR[bb[t]], residR[t][n]], [residR[t][n]])
                ln_stats(resid, residR, t, n)
        if DEBUG and it == 0:
            for t in range(4):
                sdma(dbg_r0[t * 128:(t + 1) * 128, :], resid[t], residR[t], [], f"dbg{t}")
            sdma(dbg_mix, Mt[:], mixR, [], "dbgm")
        aM.switch()
        x1TR = [aM.res() for _ in range(KC)]
        x1T = Mt[:, 0:KC * 512].rearrange("p (k t) -> p k t", k=KC)
        ln_norm_T(resid, residR, 0, x1T, x1TR)
        aff1 = ln_affine_items(resid, residR, 0, gbR, gb)

        aXA.switch()
        qxR = [aXA.res() for _ in range(XH)]
        PxR = [aXA.res() for _ in range(4)]
        oxR = [aXA.res() for _ in range(XH)]
        rdxR = [aXA.res() for _ in range(2)]
        qxT = Xt[:, 0:2048].rearrange("p (h t) -> p h t", h=XH)
        Px = [Xt[:, 2048 + i * 512:2048 + (i + 1) * 512] for i in range(4)]
        oxT = Xt[:, 4096:6144].rearrange("p (h t) -> p h t", h=XH)
        rdx = [Xf[:, 3072 + i * 512:3072 + (i + 1) * 512] for i in range(2)]
        for p in range(2):
            sl, slR = fm_slab(xq_w, p * 256)
            for j in range(2):
                h = 2 * p + j
                b = ring()
                for k in range(KC):
                    mm(banks[b][:], sl[:, k, j * 128:(j + 1) * 128], x1T[:, k, :], k == 0, k == KC - 1,
                       [slR, x1TR[k]], [bankR[b]])
                P.op("scalar", lambda e, o=qxT[:, h, :], i_=banks[b][:]: e.activation(o, i_, AF.Copy), [bankR[b]], [qxR[h]])
        for item in aff1:
            item()
        for h in range(XH):
            pidx = []
            for mb in range(2):
                b = ring()
                mm(banks[b][:], KxT[:, h, mb * 128:(mb + 1) * 128], qxT[:, h, :], True, True, [KxR, qxR[h]], [bankR[b]])
                pp = (2 * h + mb) % 4
                act(Px[pp], banks[b][:], AF.Exp, [bankR[b]], [PxR[pp]], scale=SCALE)
                pidx.append(pp)
            bo = ring()
            bdn = ring()
            for mb in range(2):
                mm(banks[bo][:], Vx[:, mb, h * 128:(h + 1) * 128], Px[pidx[mb]], mb == 0, mb == 1,
                   [VxR, PxR[pidx[mb]]], [bankR[bo]])
            for mb in range(2):
                mm(banks[bdn][:], ones_b[:], Px[pidx[mb]], mb == 0, mb == 1, [constR, PxR[pidx[mb]]], [bankR[bdn]])
            rr = h % 2
            act(rdx[rr], banks[bdn][:], AF.Ln, [bankR[bdn]], [rdxR[rr]])
            act(rdx[rr], rdx[rr], AF.Exp, [rdxR[rr]], [rdxR[rr]], scale=-1.0)
            vtt(oxT[:, h, :], banks[bo][:], rdx[rr], ALU.mult, [bankR[bo], rdxR[rr]], [oxR[h]])
        for s in range(2):
            sl, slR = load_slab(xo_w[:, s * 2048:(s + 1) * 2048].rearrange("(k p) c -> p k c", p=128), (4, 2048))
            for nn in range(4):
                n = s * 4 + nn
                bb = [ring() for _ in range(4)]
                for t in range(4):
                    for k in range(4):
                        mm(banks[bb[t]][:], oxT[:, k, t * 128:(t + 1) * 128], sl[:, k, nn * 512:(nn + 1) * 512], k == 0, k == 3,
                           [slR, oxR[k]], [bankR[bb[t]]])
                for t in range(4):
                    rs = resid[t][:, n * 512:(n + 1) * 512]
                    vstt(rs, rs, ALPHA, banks[bb[t]][:], ALU.mult, ALU.add, [bankR[bb[t]], residR[t][n]], [residR[t][n]])
                    ln_stats(resid, residR, t, n)
        aXA.switch()
        x2TR = [aXA.res() for _ in range(KC)]
        x2T = Xt[:, 0:KC * 512].rearrange("p (k t) -> p k t", k=KC)
        ln_norm_T(resid, residR, 1, x2T, x2TR)
        aff2 = ln_affine_items(resid, residR, 1, gbR, gb)

        aM.switch()
        hgR = [[aM.res() for _ in range(16)] for _ in range(2)]
        hg = [Mt[:, s * 8192:(s + 1) * 8192].rearrange("p (c t) -> p c t", c=16) for s in range(2)]
        aT4.switch()
        sgR = [aT4.res(), aT4.res()]
        sg = [T4[:, 0:512], T4[:, 512:1024]]
        ngroups = (NFC + 15) // 16
        si = 0
        for gi in range(ngroups):
            c_lo = gi * 16
            G = min(16, NFC - c_lo)
            slot = gi % 2
            for pc in range(G // 2):
                slG, slGR = fm_slab(w_gate, (c_lo + 2 * pc) * 128)
                slU, slUR = fm_slab(w_up, (c_lo + 2 * pc) * 128)
                for j in range(2):
                    cc = 2 * pc + j
                    bg = ring()
                    for k in range(KC):
                        mm(banks[bg][:], slG[:, k, j * 128:(j + 1) * 128], x2T[:, k, :], k == 0, k == KC - 1,
                           [slGR, x2TR[k]], [bankR[bg]])
                    bu = ring()
                    for k in range(KC):
                        mm(banks[bu][:], slU[:, k, j * 128:(j + 1) * 128], x2T[:, k, :], k == 0, k == KC - 1,
                           [slUR, x2TR[k]], [bankR[bu]])
                    s2 = si % 2
                    si += 1
                    act(sg[s2], banks[bg][:], AF.Silu, [bankR[bg]], [sgR[s2]])
                    vtt(hg[slot][:, cc, :], banks[bu][:], sg[s2], ALU.mult, [bankR[bu], sgR[s2]], [hgR[slot][cc]])
                for _ in range(2):
                    if aff2:
                        aff2.pop(0)()
            while gi == 0 and aff2:
                aff2.pop(0)()
            for n in range(8):
                sl, slR = load_slab(w_down[c_lo * 128:(c_lo + G) * 128, n * 512:(n + 1) * 512].rearrange("(k p) c -> p k c", p=128),
                                    (G, 512))
                bb = [ring() for _ in range(4)]
                for t in range(4):
                    for kk in range(G):
                        mm(banks[bb[t]][:], hg[slot][:, kk, t * 128:(t + 1) * 128], sl[:, kk, :], kk == 0, kk == G - 1,
                           [slR, hgR[slot][kk]], [bankR[bb[t]]])
                for t in range(4):
                    rs = resid[t][:, n * 512:(n + 1) * 512]
                    if gi == 0:
                        vstt(rs, rs, ALPHA, banks[bb[t]][:], ALU.mult, ALU.add, [bankR[bb[t]], residR[t][n]], [residR[t][n]])
                    else:
                        vtt(rs, rs, banks[bb[t]][:], ALU.add, [bankR[bb[t]], residR[t][n]], [residR[t][n]])
                    if gi == ngroups - 1:
                        ln_stats(resid, residR, t, n)
        if it < NT - 1:
            aM.switch()
            pre_stgR = [aM.res(), aM.res()]
            for blk in range(2):
                P.op("gpsimd", lambda e, o=stg[blk], i_=xpad[tp0 + T + blk * 128:tp0 + T + (blk + 1) * 128, :]:
                     e.dma_start(out=o, in_=i_), [], [pre_stgR[blk]], lane=f"stg{blk}")
        layer_norm(resid, residR, 2, gbR, gb)
        for t in range(4):
            r0 = it * T + t * 128
            P.op("scalar", lambda e, o=y[r0:r0 + 128, :], i_=resid[t]: e.dma_start(out=o, in_=i_), residR[t], [], lane=f"y{t}")

    P.emit(nc, es, y_lanes)
    es.close()
    return nc


def _t5_bucket(rel):
    nb = 16
    max_exact = 8
    ret = np.where(rel > 0, nb, 0)
    n = np.abs(rel)
    nf = np.maximum(n, 1).astype(np.float32)
    large = max_exact + (np.log(nf / max_exact) / math.log(128 / max_exact) * (nb - max_exact)).astype(np.int32)
    large = np.minimum(large, nb - 1)
    return ret + np.where(n < max_exact, n, large)


def _consts():
    ident = np.eye(128, dtype=np.float32)
    jmat = np.ascontiguousarray(ident[::-1, :])
    rel = np.arange(512) - 256
    bucket = _t5_bucket(rel)
    oh = np.zeros((33, 512), dtype=np.float32)
    inband = np.abs(rel) <= 128
    oh[bucket[inband], np.arange(512)[inband]] = 1.0
    oh[32, ~inband] = 1.0
    return ident, jmat, oh


_CACHE = {}


def kernel(x_prompt, x_sample, mem_prompt, mem_sample, rel_bias, w_in, conv_w, conv_b,
           conv_ln_g, conv_ln_b, sink, w_out, ln1_g, ln1_b, xq_w, xk_w, xv_w, xo_w,
           ln2_g, ln2_b, w_gate, w_up, w_down, ln3_g, ln3_b):
    f = lambda a: np.ascontiguousarray(np.asarray(a, dtype=np.float32))
    x_prompt, x_sample, mem_prompt, mem_sample = f(x_prompt), f(x_sample), f(mem_prompt), f(mem_sample)
    B, S, _ = x_prompt.shape
    DB, DS, _ = x_sample.shape
    assert DB == 1
    n_pc = B
    n_sc = N_CORES - n_pc
    tok = S
    assert DS == n_sc * tok and tok % T == 0
    NT = tok // T
    DFF = w_gate.shape[-1]
    key = (NT, DFF)
    if key not in _CACHE:
        _CACHE[key] = build_program(NT, DFF)
    nc = _CACHE[key]

    ident, jmat, oh = _consts()
    cpar = f(np.concatenate([conv_w[0], conv_b[0][None], conv_ln_g[0][None], conv_ln_b[0][None]], axis=0))
    lnp = f(np.stack([ln1_g[0], ln1_b[0], ln2_g[0], ln2_b[0], ln3_g[0], ln3_b[0]], axis=0))
    relb = f(np.concatenate([np.asarray(rel_bias, np.float32), np.full((1, NH), NEG, np.float32)], axis=0))
    shared = dict(w_in=f(w_in[0]), w_out=f(w_out[0]), xq_w=f(xq_w[0]), xk_w=f(xk_w[0]), xv_w=f(xv_w[0]),
                  xo_w=f(xo_w[0]), w_gate=f(w_gate[0]), w_up=f(w_up[0]), w_down=f(w_down[0]),
                  cpar=cpar, lnp=lnp, relb=relb, sink=f(np.asarray(sink, np.float32).reshape(1, NH)),
                  ident=ident, jmat=jmat, oh=oh)
    NB = NT * 4 + 2
    in_maps = []
    for c in range(N_CORES):
        xp = np.zeros((tok + 2 * HALO, D), np.float32)
        valid = np.zeros((tok + 2 * HALO,), bool)
        if c < n_pc:
            xp[HALO:HALO + tok] = x_prompt[c]
            valid[HALO:HALO + tok] = True
            mem = mem_prompt[c]
        else:
            j = c - n_pc
            lo = j * tok - HALO
            hi = (j + 1) * tok + HALO
            slo, shi = max(lo, 0), min(hi, DS)
            xp[slo - lo:shi - lo] = x_sample[0, slo:shi]
            valid[slo - lo:shi - lo] = True
            mem = mem_sample[0]
        kb = np.where(valid, 0.0, NEG).astype(np.float32).reshape(NB, 128).T
        m = dict(shared)
        m.update(xpad=xp, mem=f(mem), kbias=np.ascontiguousarray(kb))
        in_maps.append(m)
    res = run_bass_kernel_spmd(nc, in_maps, core_ids=list(range(N_CORES)))
    if DEBUG:
        global _DBG
        _DBG = res.results
    outs = [np.asarray(r["y"], dtype=np.float32) for r in res.results]
    y_prompt = np.stack(outs[:n_pc], axis=0)
    y_sample = np.concatenate(outs[n_pc:], axis=0)[None]
    return (y_prompt, y_sample)
```

```python
import math
from contextlib import ExitStack

import numpy as np
import concourse.bass as bass
import concourse.mybir as mybir
from concourse.bass_utils import run_bass_kernel_spmd

F32 = mybir.dt.float32
BF16 = mybir.dt.bfloat16
ALU = mybir.AluOpType
AF = mybir.ActivationFunctionType

D = 4096
KC = D // 128
CW = 2048
NCH = CW // 128
NH = 16
NKV = 4
XH = 4
MEM = 256
CONV_K = 31
T = 512
HALO = 128
TW = T + 2 * HALO
ALPHA = 2.0 ** 0.25
LN_EPS = 1e-5
SCALE = 128.0 ** -0.5
NEG = -30000.0
N_CORES = 8


class Res:
    __slots__ = ("w", "r")

    def __init__(self, seed=None):
        self.w = None
        self.r = dict(seed) if seed else {}


class Area:
    def __init__(self):
        self.cur = []

    def switch(self):
        seed = {}
        for res in self.cur:
            if res.w is not None:
                seed[("w", res.w)] = res.w
            for k, v in res.r.items():
                if k in seed:
                    seed[k] = max(seed[k], v)
                else:
                    seed[k] = v
        self.cur = []
        self.seed = seed
        return seed

    def res(self, extra_seed=None):
        s = dict(getattr(self, "seed", {}))
        if extra_seed:
            for k, v in extra_seed.items():
                s[k] = max(s.get(k, -1), v)
        r = Res(s)
        self.cur.append(r)
        return r

    def adopt(self, r):
        self.cur.append(r)


class Prog:
    def __init__(self):
        self.ins = []

    def op(self, eng, fn, reads=(), writes=(), lane=None):
        idx = len(self.ins)
        deps = set()
        for r in reads:
            if r.w is not None:
                deps.add(r.w)
        for w in writes:
            if w.w is not None:
                deps.add(w.w)
            deps.update(w.r.values())
        key = eng if lane is None else ("dma", idx)
        for r in reads:
            r.r[key] = idx
        for w in writes:
            w.w = idx
            w.r = {}
        self.ins.append([eng, fn, deps, lane, False, 0])
        return idx

    def emit(self, nc, es, final_wait_lanes):
        ins = self.ins
        engs = ("tensor", "vector", "scalar", "sync", "gpsimd")
        for rec in ins:
            for d in rec[2]:
                p = ins[d]
                if p[3] is None and not (p[0] == "tensor" and rec[0] == "tensor"):
                    p[4] = True
        cnt = {}
        lanes = {}
        for rec in ins:
            if rec[3] is not None:
                lanes[rec[3]] = lanes.get(rec[3], 0) + 16
                rec[5] = lanes[rec[3]]
            elif rec[4]:
                cnt[rec[0]] = cnt.get(rec[0], 0) + 1
                rec[5] = cnt[rec[0]]
        sem_e = {e: es.enter_context(nc.semaphore("se_" + e)) for e in ("tensor", "vector", "scalar", "gpsimd")}
        sem_l = {l: es.enter_context(nc.semaphore("sl_" + l)) for l in lanes}
        streams = {e: [] for e in engs}
        for i, rec in enumerate(ins):
            streams[rec[0]].append(i)
        block = es.enter_context(nc.Block())

        def run(e, name):
            waited = {}
            for i in streams[name]:
                rec = ins[i]
                need = {}
                for d in rec[2]:
                    p = ins[d]
                    if p[3] is not None:
                        k = ("l", p[3])
                    else:
                        if p[0] == "tensor" and name == "tensor":
                            continue
                        k = ("e", p[0])
                    if p[5] > need.get(k, 0):
                        need[k] = p[5]
                for k, v in need.items():
                    if waited.get(k, 0) >= v:
                        continue
                    waited[k] = v
                    e.wait_ge(sem_l[k[1]] if k[0] == "l" else sem_e[k[1]], v)
                inst = rec[1](e)
                if rec[3] is not None:
                    inst.then_inc(sem_l[rec[3]], 16)
                elif rec[4]:
                    inst.then_inc(sem_e[name], 1)
            if name == "sync":
                for l in final_wait_lanes:
                    if l in lanes:
                        e.wait_ge(sem_l[l], lanes[l])

        @block.tensor
        def _(e):
            run(e, "tensor")

        @block.vector
        def _(e):
            run(e, "vector")

        @block.scalar
        def _(e):
            run(e, "scalar")

        @block.gpsimd
        def _(e):
            run(e, "gpsimd")

        @block.sync
        def _(e):
            run(e, "sync")


DEBUG = False


def build_program(NT, DFF):
    NB = NT * 4 + 2
    NTOK = NT * T
    IN_COLS = 7168
    assert DFF % 256 == 0
    NFC = DFF // 128
    nc = bass.Bass("TRN2", target_bir_lowering=False)

    def din(name, shape):
        return nc.dram_tensor(name, list(shape), F32, kind="ExternalInput").ap()

    xpad = din("xpad", [NTOK + 2 * HALO, D])
    mem = din("mem", [MEM, D])
    kbias_d = din("kbias", [128, NB])
    w_in = din("w_in", [D, IN_COLS])
    w_out = din("w_out", [D, D])
    xq_w = din("xq_w", [D, 512])
    xk_w = din("xk_w", [D, 512])
    xv_w = din("xv_w", [D, 512])
    xo_w = din("xo_w", [512, D])
    w_gate = din("w_gate", [D, DFF])
    w_up = din("w_up", [D, DFF])
    w_down = din("w_down", [DFF, D])
    cpar_d = din("cpar", [34, CW])
    lnp = din("lnp", [6, D])
    relb_d = din("relb", [33, NH])
    sink_d = din("sink", [1, NH])
    ident_d = din("ident", [128, 128])
    jmat_d = din("jmat", [128, 128])
    oh_d = din("oh", [33, 512])
    y = nc.dram_tensor("y", [NTOK, D], F32, kind="ExternalOutput").ap()
    gtab_t = nc.dram_tensor("gtab", [16, 512], F32, kind="ExternalOutput" if DEBUG else "Internal")
    gtab_d = gtab_t.ap()
    bd_d = nc.dram_tensor("bd", [3, 128, 2048], F32, kind="ExternalOutput" if DEBUG else "Internal").ap()

    if DEBUG:
        dbg_mix = nc.dram_tensor("dbg_mix", [128, 16384], BF16, kind="ExternalOutput").ap()
        dbg_x1 = nc.dram_tensor("dbg_x1", [512, D], F32, kind="ExternalOutput").ap()
        dbg_x2 = nc.dram_tensor("dbg_x2", [512, D], F32, kind="ExternalOutput").ap()
        dbg_r0 = nc.dram_tensor("dbg_r0", [512, D], F32, kind="ExternalOutput").ap()
        dbg_R = nc.dram_tensor("dbg_R", [128, 16384], F32, kind="ExternalOutput").ap()
    P = Prog()
    es = ExitStack()
    E = es.enter_context

    Rt = E(nc.sbuf_tensor("Rt", [128, 16384], F32))
    Xt = E(nc.sbuf_tensor("Xt", [128, 24576], BF16))
    Mt = E(nc.sbuf_tensor("Mt", [128, 16384], BF16))
    slabs = [E(nc.sbuf_tensor(f"slab{i}", [128, 8192], BF16)) for i in range(3)]
    T4 = E(nc.sbuf_tensor("T4", [128, 1024], F32))
    ident = E(nc.sbuf_tensor("ident_s", [128, 128], F32))
    jmat = E(nc.sbuf_tensor("jmat_s", [128, 128], F32))
    ones_f = E(nc.sbuf_tensor("ones_f", [128, 128], F32))
    ones_b = E(nc.sbuf_tensor("ones_b", [128, 128], BF16))
    eps_t = E(nc.sbuf_tensor("eps_t", [128, 1], F32))
    cw = E(nc.sbuf_tensor("cw", [128, NCH, 34], F32))
    esk = E(nc.sbuf_tensor("esk", [128, NH], F32))
    kbias = E(nc.sbuf_tensor("kbias_s", [128, NB], F32))
    KxT = E(nc.sbuf_tensor("KxT", [128, XH, MEM], BF16))
    Vx = E(nc.sbuf_tensor("Vx", [128, 2, 512], BF16))
    stats = E(nc.sbuf_tensor("stats", [128, 4, 8, 6], F32))
    mv = E(nc.sbuf_tensor("mv", [128, 4, 2], F32))
    rstd = E(nc.sbuf_tensor("rstd", [128, 4], F32))
    nmr = E(nc.sbuf_tensor("nmr", [128, 4], F32))
    carry = E(nc.sbuf_tensor("carry", [128, NCH, 32], F32))
    gcol = E(nc.sbuf_tensor("gcol", [128, 128], F32))
    carryR = [Res() for _ in range(NCH)]
    banks = [E(nc.psum_tensor(f"bank{i}", [128, 512], F32)) for i in range(8)]

    Xf = Xt[:].bitcast(F32)
    Mf = Mt[:].bitcast(F32)
    Rb = Rt[:].bitcast(BF16)

    bankR = [Res() for _ in range(8)]
    slabR = [Res() for _ in range(3)]
    constR = Res()
    aR, aXA, aXB, aM, aT4 = Area(), Area(), Area(), Area(), Area()
    for a in (aR, aXA, aXB, aM, aT4):
        a.switch()

    st = {"bank": 0, "slab": 0, "ev": 0}

    st["nring"] = 6

    def ring():
        b = st["bank"] % st["nring"]
        st["bank"] = (b + 1) % st["nring"]
        return b

    def load_slab(src, shape3):
        s = st["slab"]
        st["slab"] = (s + 1) % 3
        a, b = shape3
        view = slabs[s][:, 0:a * b].rearrange("p (a b) -> p a b", a=a)
        P.op("gpsimd", lambda e, v=view, sr=src: e.dma_start(out=v, in_=sr), writes=[slabR[s]], lane=f"slab{s}")
        return view, slabR[s]

    def fm_slab(W, c0, ncols=256):
        return load_slab(W[:, c0:c0 + ncols].rearrange("(k p) c -> p k c", p=128), (KC, ncols))

    def mm(out, lhsT, rhs, start, stop, reads, writes):
        P.op("tensor", lambda e: e.matmul(out, lhsT, rhs, start=start, stop=stop), reads, writes)

    def tr(out, in_, idn, reads, writes):
        P.op("tensor", lambda e: e.transpose(out, in_, idn), reads, writes)

    def act(out, in_, func, reads, writes, bias=None, scale=None):
        kw = {}
        if bias is not None:
            kw["bias"] = bias
        if scale is not None:
            kw["scale"] = scale
        P.op("scalar", lambda e: e.activation(out, in_, func, **kw), reads, writes)

    def evac(out, in_, reads, writes):
        st["ev"] ^= 1
        if st["ev"]:
            P.op("scalar", lambda e: e.activation(out, in_, AF.Copy), reads, writes)
        else:
            P.op("vector", lambda e: e.tensor_copy(out=out, in_=in_), reads, writes)

    def vtt(out, in0, in1, op, reads, writes):
        P.op("vector", lambda e: e.tensor_tensor(out=out, in0=in0, in1=in1, op=op), reads, writes)

    def vts(out, in0, s1, s2, op0, op1, reads, writes):
        if op1 is None:
            P.op("vector", lambda e: e.tensor_scalar(out=out, in0=in0, scalar1=s1, scalar2=None, op0=op0), reads, writes)
        else:
            P.op("vector", lambda e: e.tensor_scalar(out=out, in0=in0, scalar1=s1, scalar2=s2, op0=op0, op1=op1), reads, writes)

    def vstt(out, in0, scalar, in1, op0, op1, reads, writes):
        P.op("vector", lambda e: e.scalar_tensor_tensor(out=out, in0=in0, scalar=scalar, in1=in1, op0=op0, op1=op1), reads, writes)

    def sdma(out, in_, reads, writes, lane):
        P.op("sync", lambda e: e.dma_start(out=out, in_=in_), reads, writes, lane=lane)

    sdma(ident[:], ident_d, [], [constR], "c0")
    sdma(jmat[:], jmat_d, [], [constR], "c1")
    sdma(kbias[:], kbias_d, [], [constR], "c2")
    P.op("vector", lambda e: e.memset(ones_f[:], 1.0), [], [constR])
    P.op("vector", lambda e: e.memset(ones_b[:], 1.0), [], [constR])
    P.op("vector", lambda e: e.memset(eps_t[:], LN_EPS), [], [constR])

    cparR = aM.res()
    cpar_s = Mf[0:34, 0:CW]
    sdma(cpar_s, cpar_d, [], [cparR], "c3")
    cwR = Res()
    for c in range(NCH):
        b = ring()
        tr(banks[b][:, 0:34], Mf[0:34, c * 128:(c + 1) * 128], ident[0:34, 0:34], [cparR, constR], [bankR[b]])
        evac(cw[:, c, :], banks[b][:, 0:34], [bankR[b]], [cwR])

    eskR = Res()
    sdma(esk[:], sink_d[0:1, :].broadcast_to([128, NH]), [], [eskR], "c4")
    act(esk[:], esk[:], AF.Exp, [eskR], [eskR])

    relbR = aM.res()
    relb_s = Mf[0:33, 4096:4096 + NH]
    oh_s = Mf[0:33, 4608:4608 + 512]
    sdma(relb_s, relb_d, [], [relbR], "c5")
    sdma(oh_s, oh_d, [], [relbR], "c6")
    b = ring()
    mm(banks[b][0:16, :], relb_s, oh_s, True, True, [relbR], [bankR[b]])
    gsbR = aM.res()
    gsb = Mf[0:16, 5120:5120 + 512]
    evac(gsb, banks[b][0:16, :], [bankR[b]], [gsbR])
    gtabR = Res()
    sdma(gtab_d, gsb, [gsbR], [gtabR], "c7")
    BdR = [Res() for _ in range(3)]

    def build_bias_tables():
      for o in (-1, 0, 1):
        hkR = aR.res()
        hk = Rt[:, (o + 1) * 2048:(o + 2) * 2048].rearrange("p (h k) -> p h k", h=NH)
        src = bass.AP(tensor=gtab_t, offset=128 * o + 129, ap=[[1, 128], [512, NH], [1, 128]])
        sdma(hk, src, [gtabR], [hkR], f"hk{o + 1}")
        btR = aR.res()
        bt = Rt[:, 6144 + (o + 1) * 2048:6144 + (o + 2) * 2048]
        for hq in range(4):
            b = ring()
            for hh in range(4):
                mm(banks[b][:, hh * 128:(hh + 1) * 128], hk[:, hq * 4 + hh, :], jmat[:], True, True,
                   [hkR, constR], [bankR[b]])
            evac(bt[:, hq * 512:(hq + 1) * 512], banks[b][:], [bankR[b]], [btR])
        sdma(bd_d[o + 1], bt, [btR], [BdR[o + 1]], f"bdw{o + 1}")

    gcolR = Res()
    for li in range(2):
        lncR = aM.res()
        lnc_s = Mf[0:64, 6144 + li * 128:6144 + (li + 1) * 128]
        sdma(lnc_s, lnp[2 * li:2 * li + 2, :].rearrange("a (k p) -> (a k) p", p=128), [], [lncR], f"c{8 + li}")
        b = ring()
        tr(banks[b][:, 0:64], lnc_s, ident[0:64, 0:64], [lncR, constR], [bankR[b]])
        evac(gcol[:, li * 64:(li + 1) * 64], banks[b][:, 0:64], [bankR[b]], [gcolR])

    aM.switch()
    stgR = [aM.res(), aM.res()]
    stg = [Mf[:, 0:4096], Mf[:, 4096:8192]]
    aXA.switch()
    memTR = aXA.res()
    memT = Xt[:, 0:KC * MEM].rearrange("p (k m) -> p k m", k=KC)
    for mb in range(2):
        sdma(stg[mb], mem[mb * 128:(mb + 1) * 128, :], [], [stgR[mb]], f"stg{mb}")
        for kq in range(8):
            b = ring()
            for kk in range(4):
                k = kq * 4 + kk
                tr(banks[b][:, kk * 128:(kk + 1) * 128], stg[mb][:, k * 128:(k + 1) * 128], ident[:],
                   [stgR[mb], constR], [bankR[b]])
            evac(memT[:, kq * 4:kq * 4 + 4, mb * 128:(mb + 1) * 128],
                 banks[b][:].rearrange("p (a b) -> p a b", a=4), [bankR[b]], [memTR])
    KxR = Res()
    VxR = Res()
    for p in range(2):
        sl, slR = fm_slab(xk_w, p * 256)
        for j in range(2):
            h = 2 * p + j
            b = ring()
            for k in range(KC):
                mm(banks[b][:, 0:MEM], sl[:, k, j * 128:(j + 1) * 128], memT[:, k, :], k == 0, k == KC - 1,
                   [slR, memTR], [bankR[b]])
            evac(KxT[:, h, :], banks[b][:, 0:MEM], [bankR[b]], [KxR])
    for p in range(2):
        sl, slR = fm_slab(xv_w, p * 256)
        for mb in range(2):
            b = ring()
            for k in range(KC):
                mm(banks[b][:, 0:256], memT[:, k, mb * 128:(mb + 1) * 128], sl[:, k, :], k == 0, k == KC - 1,
                   [slR, memTR], [bankR[b]])
            evac(Vx[:, mb, p * 256:(p + 1) * 256], banks[b][:, 0:256], [bankR[b]], [VxR])

    smallR = [Res() for _ in range(4)]

    def ln_stats(resid, residR, t, n):
        P.op("vector", lambda e, t=t, n=n: e.bn_stats(out=stats[:, t, n, :], in_=resid[t][:, n * 512:(n + 1) * 512]),
             [residR[t][n]], [smallR[t]])

    def ln_tail():
        for t in range(4):
            P.op("vector", lambda e, t=t: e.bn_aggr(out=mv[:, t, :], in_=stats[:, t].rearrange("p a b -> p (a b)")),
                 [smallR[t]], [smallR[t]])
        act(rstd[:, 0:4], mv[:, :, 1], AF.Ln, smallR + [constR], smallR, bias=eps_t[:, 0:1])
        act(rstd[:, 0:4], rstd[:, 0:4], AF.Exp, smallR, smallR, scale=-0.5)
        vstt(nmr[:, 0:4], mv[:, :, 0], -1.0, rstd[:, 0:4], ALU.mult, ALU.mult, smallR, smallR)

    def layer_norm(resid, residR, li, gbR, gb):
        ln_tail()
        for cg in range(4):
            bf = cg % 2
            c0 = cg * 1024
            sdma(gb[bf][0], lnp[2 * li:2 * li + 1, c0:c0 + 1024].broadcast_to([128, 1024]), [], [gbR[bf][0]], f"gbg{bf}")
            sdma(gb[bf][1], lnp[2 * li + 1:2 * li + 2, c0:c0 + 1024].broadcast_to([128, 1024]), [], [gbR[bf][1]], f"gbb{bf}")
            for t in range(4):
                rr = [residR[t][2 * cg], residR[t][2 * cg + 1]]
                sl_ = resid[t][:, c0:c0 + 1024]
                act(sl_, sl_, AF.Identity, rr + [smallR[t]], rr, bias=nmr[:, t:t + 1], scale=rstd[:, t:t + 1])
                vtt(sl_, sl_, gb[bf][0], ALU.mult, rr + [gbR[bf][0]], rr)
                vtt(sl_, sl_, gb[bf][1], ALU.add, rr + [gbR[bf][1]], rr)

    def ln_norm_T(resid, residR, li, dstT, dstR):
        ln_tail()
        for cg in range(4):
            c0 = cg * 1024
            for t in range(4):
                rr = [residR[t][2 * cg], residR[t][2 * cg + 1]]
                sl_ = resid[t][:, c0:c0 + 1024]
                act(sl_, sl_, AF.Identity, rr + [smallR[t]], rr, bias=nmr[:, t:t + 1], scale=rstd[:, t:t + 1])
        for k in range(KC):
            b = ring()
            for t in range(4):
                tr(banks[b][:, t * 128:(t + 1) * 128], resid[t][:, k * 128:(k + 1) * 128], ident[:],
                   [residR[t][k // 4], constR], [bankR[b]])
            gc = gcol[:, li * 64 + k:li * 64 + k + 1]
            bc = gcol[:, li * 64 + 32 + k:li * 64 + 32 + k + 1]
            vts(dstT[:, k, :], banks[b][:], gc, bc, ALU.mult, ALU.add, [bankR[b], gcolR], [dstR[k]])

    def ln_affine_items(resid, residR, li, gbR, gb):
        items = []
        for cg in range(4):
            bf = cg % 2
            c0 = cg * 1024

            def load(cg=cg, bf=bf, c0=c0):
                sdma(gb[bf][0], lnp[2 * li:2 * li + 1, c0:c0 + 1024].broadcast_to([128, 1024]), [], [gbR[bf][0]], f"gbg{bf}")
                sdma(gb[bf][1], lnp[2 * li + 1:2 * li + 2, c0:c0 + 1024].broadcast_to([128, 1024]), [], [gbR[bf][1]], f"gbb{bf}")
            for t in range(4):
                def item(cg=cg, bf=bf, c0=c0, t=t, load=load):
                    if t == 0:
                        load()
                    rr = [residR[t][2 * cg], residR[t][2 * cg + 1]]
                    sl_ = resid[t][:, c0:c0 + 1024]
                    vtt(sl_, sl_, gb[bf][0], ALU.mult, rr + [gbR[bf][0]], rr)
                    vtt(sl_, sl_, gb[bf][1], ALU.add, rr + [gbR[bf][1]], rr)
                items.append(item)
        return items

    resid = [Rt[:, t * 4096:(t + 1) * 4096] for t in range(4)]
    gb = [[Xf[:, 8192 + bf * 2048:8192 + bf * 2048 + 1024], Xf[:, 8192 + bf * 2048 + 1024:8192 + (bf + 1) * 2048]]
          for bf in range(2)]
    y_lanes = [f"y{t}" for t in range(4)]

    pre_stgR = None
    for it in range(NT):
        tp0 = it * T
        if pre_stgR is None:
            aM.switch()
            stgR = [aM.res(), aM.res()]
            npre = 0
        else:
            stgR = pre_stgR
            npre = 2
        sa = aXA.switch()
        sb_ = aXB.switch()
        xTR = []
        mseed = dict(sa)
        for k_, v_ in sb_.items():
            mseed[k_] = max(mseed.get(k_, -1), v_)
        for blk in range(6):
            r = Res(mseed)
            aXA.adopt(r)
            aXB.adopt(r)
            xTR.append(r)
        xT = Xt[:, 0:KC * TW].rearrange("p (k t) -> p k t", k=KC)
        for blk in range(6):
            s = blk % 2
            if blk >= npre:
                sdma(stg[s], xpad[tp0 + blk * 128:tp0 + (blk + 1) * 128, :], [], [stgR[s]], f"stg{s}")
            for kq in range(8):
                b = ring()
                for kk in range(4):
                    k = kq * 4 + kk
                    tr(banks[b][:, kk * 128:(kk + 1) * 128], stg[s][:, k * 128:(k + 1) * 128], ident[:],
                       [stgR[s], constR], [bankR[b]])
                evac(xT[:, kq * 4:kq * 4 + 4, blk * 128:(blk + 1) * 128],
                     banks[b][:].rearrange("p (a b) -> p a b", a=4), [bankR[b]], [xTR[blk]])

        if it == 0:
            build_bias_tables()

        aM.switch()
        sigR = [aM.res(), aM.res()]
        uR = [aM.res(), aM.res()]
        sqR = [aM.res() for _ in range(4)]
        sig = [Mf[:, i * 544:(i + 1) * 544] for i in range(2)]
        u = [Mf[:, 1088 + i * 544:1088 + (i + 1) * 544] for i in range(2)]
        sq = [Mf[:, 2176 + i * 512:2176 + (i + 1) * 512] for i in range(4)]
        aR.switch()
        hTR = [aR.res() for _ in range(NCH)]
        qTR = [aR.res() for _ in range(NH)]
        kTR = [aR.res() for _ in range(NKV)]
        VR = [aR.res() for _ in range(6)]
        spR = aR.res()
        hT = Rt[:, 0:8192].rearrange("p (c t) -> p c t", c=NCH)
        qT = Rb[:, 16384:24576].rearrange("p (h t) -> p h t", h=NH)
        kT = Rb[:, 24576:24576 + NKV * TW].rearrange("p (g t) -> p g t", g=NKV)
        Vt = Rb[:, 27648:27648 + 6 * 512].rearrange("p (b c) -> p b c", b=6)
        sp0 = Rt[:, 15360:15872]
        sp1 = Rt[:, 15872:16384]
        SUMB, SQB = 6, 7
        allx = xTR

        def conv_stats(pcs):
            for j in range(2):
                c = 2 * pcs + j
                s4 = c % 4
                act(sq[s4], hT[:, c, :], AF.Square, [hTR[c]], [sqR[s4]])
                mm(banks[SUMB][:], ones_f[:], hT[:, c, :], c == 0, c == NCH - 1, [constR, hTR[c]], [bankR[SUMB]])
                mm(banks[SQB][:], ones_f[:], sq[s4], c == 0, c == NCH - 1, [constR, sqR[s4]], [bankR[SQB]])

        def evac_a(out, in_, reads, writes):
            P.op("scalar", lambda e: e.activation(out, in_, AF.Copy), reads, writes)

        def q_proj(pq):
            sl, slR = fm_slab(w_in, 2 * CW + pq * 256)
            for j in range(2):
                h = 2 * pq + j
                b = ring()
                for k in range(KC):
                    mm(banks[b][:], sl[:, k, j * 128:(j + 1) * 128], xT[:, k, 128:640], k == 0, k == KC - 1,
                       [slR] + allx, [bankR[b]])
                evac_a(qT[:, h, :], banks[b][:], [bankR[b]], [qTR[h]])

        def k_proj(pk):
            sl, slR = fm_slab(w_in, 2 * CW + 2048 + pk * 256)
            for j in range(2):
                g = 2 * pk + j
                bA = ring()
                bB = ring()
                for k in range(KC):
                    mm(banks[bA][:], sl[:, k, j * 128:(j + 1) * 128], xT[:, k, 0:512], k == 0, k == KC - 1,
                       [slR] + allx, [bankR[bA]])
                for k in range(KC):
                    mm(banks[bB][:, 0:256], sl[:, k, j * 128:(j + 1) * 128], xT[:, k, 512:768], k == 0, k == KC - 1,
                       [slR] + allx, [bankR[bB]])
                evac_a(kT[:, g, 0:512], banks[bA][:], [bankR[bA]], [kTR[g]])
                evac_a(kT[:, g, 512:768], banks[bB][:, 0:256], [bankR[bB]], [kTR[g]])

        def v_proj(pv):
            sl, slR = fm_slab(w_in, 2 * CW + 2048 + 512 + pv * 256)
            for blk in range(6):
                b = ring()
                for k in range(KC):
                    mm(banks[b][:, 0:256], xT[:, k, blk * 128:(blk + 1) * 128], sl[:, k, :], k == 0, k == KC - 1,
                       [slR, xTR[blk]], [bankR[b]])
                evac_a(Vt[:, blk, pv * 256:(pv + 1) * 256], banks[b][:, 0:256], [bankR[b]], [VR[blk]])

        extras = {1: lambda: k_proj(0), 3: lambda: k_proj(1), 5: lambda: v_proj(0), 7: lambda: v_proj(1)}

        for pc in range(NCH // 2):
            slV, slVR = fm_slab(w_in, pc * 256)
            slG, slGR = fm_slab(w_in, CW + pc * 256)
            if it == 0:
                sbk = ring()
            for j in range(2):
                c = 2 * pc + j
                bv = ring()
                for k in range(KC):
                    mm(banks[bv][:], slV[:, k, j * 128:(j + 1) * 128], xT[:, k, 144:656], k == 0, k == KC - 1,
                       [slVR] + allx, [bankR[bv]])
                if it == 0:
                    for k in range(KC):
                        mm(banks[sbk][:, (2 * j) * 32:(2 * j + 1) * 32], slV[:, k, j * 128:(j + 1) * 128], xT[:, k, 112:144],
                           k == 0, k == KC - 1, [slVR] + allx, [bankR[sbk]])
                bg = ring()
                for k in range(KC):
                    mm(banks[bg][:], slG[:, k, j * 128:(j + 1) * 128], xT[:, k, 144:656], k == 0, k == KC - 1,
                       [slGR] + allx, [bankR[bg]])
                if it == 0:
                    for k in range(KC):
                        mm(banks[sbk][:, (2 * j + 1) * 32:(2 * j + 2) * 32], slG[:, k, j * 128:(j + 1) * 128], xT[:, k, 112:144],
                           k == 0, k == KC - 1, [slGR] + allx, [bankR[sbk]])
                act(sig[j][:, 32:544], banks[bg][:], AF.Sigmoid, [bankR[bg]], [sigR[j]])
                if it == 0:
                    act(sig[j][:, 0:32], banks[sbk][:, (2 * j + 1) * 32:(2 * j + 2) * 32], AF.Sigmoid, [bankR[sbk]], [sigR[j]])
                    vtt(u[j][:, 0:32], banks[sbk][:, (2 * j) * 32:(2 * j + 1) * 32], sig[j][:, 0:32], ALU.mult,
                        [bankR[sbk], sigR[j]], [uR[j]])
                else:
                    act(u[j][:, 0:32], carry[:, c, :], AF.Copy, [carryR[c]], [uR[j]])
                vtt(u[j][:, 32:544], banks[bv][:], sig[j][:, 32:544], ALU.mult, [bankR[bv], sigR[j]], [uR[j]])
            if it < NT - 1:
                for j in range(2):
                    c = 2 * pc + j
                    act(carry[:, c, :], u[j][:, 512:544], AF.Copy, [uR[j]], [carryR[c]])
            q_proj(pc)
            if pc in extras:
                extras[pc]()
            for tap in range(CONV_K):
                for j in range(2):
                    c = 2 * pc + j
                    if tap == 0:
                        vts(hT[:, c, :], u[j][:, 1:513], cw[:, c, 0:1], cw[:, c, 31:32], ALU.mult, ALU.add,
                            [uR[j], cwR], [hTR[c]])
                    else:
                        vstt(hT[:, c, :], u[j][:, tap + 1:tap + 513], cw[:, c, tap:tap + 1], hT[:, c, :], ALU.mult, ALU.add,
                             [uR[j], cwR, hTR[c]], [hTR[c]])
            if pc > 0:
                conv_stats(pc - 1)
        conv_stats(NCH // 2 - 1)

        aM.switch()
        mixR = [aM.res() for _ in range(32)]
        mixT = Mt[:, 0:32 * 512].rearrange("p (c t) -> p c t", c=32)
        inv = 1.0 / CW
        vts(sp0, banks[SUMB][:], inv, None, ALU.mult, None, [bankR[SUMB]], [spR])
        vtt(sp1, sp0, sp0, ALU.mult, [spR], [spR])
        vstt(sp1, banks[SQB][:], inv, sp1, ALU.mult, ALU.subtract, [bankR[SQB], spR], [spR])
        act(sp1, sp1, AF.Ln, [spR, constR], [spR], bias=eps_t[:, 0:1])
        act(sp1, sp1, AF.Exp, [spR], [spR], scale=-0.5)
        for c in range(NCH):
            vtt(hT[:, c, :], hT[:, c, :], sp0, ALU.subtract, [hTR[c], spR], [hTR[c]])
            vtt(hT[:, c, :], hT[:, c, :], sp1, ALU.mult, [hTR[c], spR], [hTR[c]])
            act(mixT[:, c, :], hT[:, c, :], AF.Silu, [hTR[c], cwR], [mixR[c]], bias=cw[:, c, 33:34], scale=cw[:, c, 32:33])

        aXA.switch()
        BtR = [aXA.res() for _ in range(3)]
        PR = [aXA.res() for _ in range(8)]
        Pt = [Xt[:, 12288 + i * 512:12288 + (i + 1) * 512] for i in range(8)]
        for o in range(3):
            sdma(Xf[:, o * 2048:(o + 1) * 2048], bd_d[o], [BdR[o]], [BtR[o]], f"B{o}")
        st["nring"] = 8
        sp_seed = dict(spR.r)
        if spR.w is not None:
            sp_seed[("w", spR.w)] = spR.w
        sp0R = aR.res(sp_seed)
        sp1R = aR.res(sp_seed)
        iters = [(qb, g) for qb in range(4) for g in range(NKV)]
        Sb = {}
        Pl = {}

        def att_S(i):
            qb, g = iters[i]
            sb3 = [ring(), ring(), ring()]
            for o in range(3):
                kb = qb + o
                for hh in range(4):
                    h = 4 * g + hh
                    mm(banks[sb3[o]][:, hh * 128:(hh + 1) * 128], kT[:, g, kb * 128:(kb + 1) * 128],
                       qT[:, h, qb * 128:(qb + 1) * 128], True, True, [kTR[g], qTR[h]], [bankR[sb3[o]]])
            Sb[i] = sb3

        def att_sm(i):
            qb, g = iters[i]
            pl = []
            for o in range(3):
                kb = qb + o
                bk = Sb[i][o]
                pp = (3 * i + o) % 8
                vstt(banks[bk][:], banks[bk][:], SCALE, Xf[:, o * 2048 + g * 512:o * 2048 + (g + 1) * 512],
                     ALU.mult, ALU.add, [bankR[bk], BtR[o]], [bankR[bk]])
                gb_col = it * 4 + kb
                act(Pt[pp], banks[bk][:], AF.Exp, [bankR[bk], constR], [PR[pp]], bias=kbias[:, gb_col:gb_col + 1])
                pl.append(pp)
            Pl[i] = pl

        Pv = {}

        def att_pv(i):
            qb, g = iters[i]
            pl = Pl[i]
            bo = ring()
            bdn = ring()
            for o in range(3):
                kb = qb + o
                mm(banks[bo][:], Vt[:, kb, g * 128:(g + 1) * 128], Pt[pl[o]], o == 0, o == 2,
                   [VR[kb], PR[pl[o]]], [bankR[bo]])
            for o in range(3):
                mm(banks[bdn][:], ones_b[:], Pt[pl[o]], o == 0, o == 2, [constR, PR[pl[o]]], [bankR[bdn]])
            Pv[i] = (bo, bdn)

        def att_norm(i):
            qb, g = iters[i]
            bo, bdn = Pv[i]
            spb, spbR = (sp0, sp0R) if i % 2 == 0 else (sp1, sp1R)
            for hh in range(4):
                h = 4 * g + hh
                act(spb[:, hh * 128:(hh + 1) * 128], banks[bdn][:, hh * 128:(hh + 1) * 128], AF.Ln,
                    [bankR[bdn], eskR], [spbR], bias=esk[:, h:h + 1])
            act(spb, spb, AF.Exp, [spbR], [spbR], scale=-1.0)

        def att_out(i):
            qb, g = iters[i]
            bo, bdn = Pv[i]
            spb, spbR = (sp0, sp0R) if i % 2 == 0 else (sp1, sp1R)
            vtt(mixT[:, NCH + 4 * g:NCH + 4 * g + 4, qb * 128:(qb + 1) * 128],
                banks[bo][:].rearrange("p (h q) -> p h q", h=4), spb.rearrange("p (h q) -> p h q", h=4), ALU.mult,
                [bankR[bo], spbR], [mixR[NCH + 4 * g + hh] for hh in range(4)])

        att_S(0)
        att_sm(0)
        att_S(1)
        att_sm(1)
        for i in range(len(iters)):
            att_pv(i)
            if i + 2 < len(iters):
                att_S(i + 2)
            att_norm(i)
            if i + 2 < len(iters):
                att_sm(i + 2)
            att_out(i)
        st["nring"] = 6

        if DEBUG and it == 0:
            sdma(dbg_R, Rt[:], hTR + qTR + kTR + VR + [spR], [], "dbgR")
        aR.switch()
        residR = [[aR.res() for _ in range(8)] for _ in range(4)]
        for t in range(4):
            r0 = tp0 + HALO + t * 128
            sdma(resid[t], xpad[r0:r0 + 128, :], [], residR[t], f"res{t}")
        aXB.switch()
        gbR = [[aXB.res(), aXB.res()] for _ in range(2)]
        for n in range(8):
            bb = [ring() for _ in range(4)]
            for s in range(2):
                sl, slR = load_slab(w_out[s * 2048:(s + 1) * 2048, n * 512:(n + 1) * 512].rearrange("(k p) c -> p k c", p=128),
                                    (16, 512))
                for t in range(4):
                    for kk in range(16):
                        c = s * 16 + kk
                        mm(banks[bb[t]][:], mixT[:, c, t * 128:(t + 1) * 128], sl[:, kk, :], s == 0 and kk == 0,
                           s == 1 and kk == 15, [slR, mixR[c]], [bankR[bb[t]]])
            for t in range(4):
                rs = resid[t][:, n * 512:(n + 1) * 512]
                vstt(rs, rs, ALPHA, banks[bb[t]][:], ALU.mult, ALU.add, [bankR[bb[t]], residR[t][n]], [residR[t][n]])
                ln_stats(resid, residR, t, n)
        if DEBUG and it == 0:
            for t in range(4):
                sdma(dbg_r0[t * 128:(t + 1) * 128, :], resid[t], residR[t], [], f"dbg{t}")
            sdma(dbg_mix, Mt[:], mixR, [], "dbgm")
        aM.switch()
        x1TR = [aM.res() for _ in range(KC)]
        x1T = Mt[:, 0:KC * 512].rearrange("p (k t) -> p k t", k=KC)
        ln_norm_T(resid, residR, 0, x1T, x1TR)
        aff1 = ln_affine_items(resid, residR, 0, gbR, gb)

        aXA.switch()
        qxR = [aXA.res() for _ in range(XH)]
        PxR = [aXA.res() for _ in range(4)]
        oxR = [aXA.res() for _ in range(XH)]
        rdxR = [aXA.res() for _ in range(2)]
        qxT = Xt[:, 0:2048].rearrange("p (h t) -> p h t", h=XH)
        Px = [Xt[:, 2048 + i * 512:2048 + (i + 1) * 512] for i in range(4)]
        oxT = Xt[:, 4096:6144].rearrange("p (h t) -> p h t", h=XH)
        rdx = [Xf[:, 3072 + i * 512:3072 + (i + 1) * 512] for i in range(2)]
        for p in range(2):
            sl, slR = fm_slab(xq_w, p * 256)
            for j in range(2):
                h = 2 * p + j
                b = ring()
                for k in range(KC):
                    mm(banks[b][:], sl[:, k, j * 128:(j + 1) * 128], x1T[:, k, :], k == 0, k == KC - 1,
                       [slR, x1TR[k]], [bankR[b]])
                P.op("scalar", lambda e, o=qxT[:, h, :], i_=banks[b][:]: e.activation(o, i_, AF.Copy), [bankR[b]], [qxR[h]])
        for item in aff1:
            item()
        for h in range(XH):
            pidx = []
            for mb in range(2):
                b = ring()
                mm(banks[b][:], KxT[:, h, mb * 128:(mb + 1) * 128], qxT[:, h, :], True, True, [KxR, qxR[h]], [bankR[b]])
                pp = (2 * h + mb) % 4
                act(Px[pp], banks[b][:], AF.Exp, [bankR[b]], [PxR[pp]], scale=SCALE)
                pidx.append(pp)
            bo = ring()
            bdn = ring()
            for mb in range(2):
                mm(banks[bo][:], Vx[:, mb, h * 128:(h + 1) * 128], Px[pidx[mb]], mb == 0, mb == 1,
                   [VxR, PxR[pidx[mb]]], [bankR[bo]])
            for mb in range(2):
                mm(banks[bdn][:], ones_b[:], Px[pidx[mb]], mb == 0, mb == 1, [constR, PxR[pidx[mb]]], [bankR[bdn]])
            rr = h % 2
            act(rdx[rr], banks[bdn][:], AF.Ln, [bankR[bdn]], [rdxR[rr]])
            act(rdx[rr], rdx[rr], AF.Exp, [rdxR[rr]], [rdxR[rr]], scale=-1.0)
            vtt(oxT[:, h, :], banks[bo][:], rdx[rr], ALU.mult, [bankR[bo], rdxR[rr]], [oxR[h]])
        for s in range(2):
            sl, slR = load_slab(xo_w[:, s * 2048:(s + 1) * 2048].rearrange("(k p) c -> p k c", p=128), (4, 2048))
            for nn in range(4):
                n = s * 4 + nn
                bb = [ring() for _ in range(4)]
                for t in range(4):
                    for k in range(4):
                        mm(banks[bb[t]][:], oxT[:, k, t * 128:(t + 1) * 128], sl[:, k, nn * 512:(nn + 1) * 512], k == 0, k == 3,
                           [slR, oxR[k]], [bankR[bb[t]]])
                for t in range(4):
                    rs = resid[t][:, n * 512:(n + 1) * 512]
                    vstt(rs, rs, ALPHA, banks[bb[t]][:], ALU.mult, ALU.add, [bankR[bb[t]], residR[t][n]], [residR[t][n]])
                    ln_stats(resid, residR, t, n)
        aXA.switch()
        x2TR = [aXA.res() for _ in range(KC)]
        x2T = Xt[:, 0:KC * 512].rearrange("p (k t) -> p k t", k=KC)
        ln_norm_T(resid, residR, 1, x2T, x2TR)
        aff2 = ln_affine_items(resid, residR, 1, gbR, gb)

        aM.switch()
        hgR = [[aM.res() for _ in range(16)] for _ in range(2)]
        hg = [Mt[:, s * 8192:(s + 1) * 8192].rearrange("p (c t) -> p c t", c=16) for s in range(2)]
        aT4.switch()
        sgR = [aT4.res(), aT4.res()]
        sg = [T4[:, 0:512], T4[:, 512:1024]]
        ngroups = (NFC + 15) // 16
        si = 0
        for gi in range(ngroups):
            c_lo = gi * 16
            G = min(16, NFC - c_lo)
            slot = gi % 2
            for pc in range(G // 2):
                slG, slGR = fm_slab(w_gate, (c_lo + 2 * pc) * 128)
                slU, slUR = fm_slab(w_up, (c_lo + 2 * pc) * 128)
                for j in range(2):
                    cc = 2 * pc + j
                    bg = ring()
                    for k in range(KC):
                        mm(banks[bg][:], slG[:, k, j * 128:(j + 1) * 128], x2T[:, k, :], k == 0, k == KC - 1,
                           [slGR, x2TR[k]], [bankR[bg]])
                    bu = ring()
                    for k in range(KC):
                        mm(banks[bu][:], slU[:, k, j * 128:(j + 1) * 128], x2T[:, k, :], k == 0, k == KC - 1,
                           [slUR, x2TR[k]], [bankR[bu]])
                    s2 = si % 2
                    si += 1
                    act(sg[s2], banks[bg][:], AF.Silu, [bankR[bg]], [sgR[s2]])
                    vtt(hg[slot][:, cc, :], banks[bu][:], sg[s2], ALU.mult, [bankR[bu], sgR[s2]], [hgR[slot][cc]])
                for _ in range(2):
                    if aff2:
                        aff2.pop(0)()
            while gi == 0 and aff2:
                aff2.pop(0)()
            for n in range(8):
                sl, slR = load_slab(w_down[c_lo * 128:(c_lo + G) * 128, n * 512:(n + 1) * 512].rearrange("(k p) c -> p k c", p=128),
                                    (G, 512))
                bb = [ring() for _ in range(4)]
                for t in range(4):
                    for kk in range(G):
                        mm(banks[bb[t]][:], hg[slot][:, kk, t * 128:(t + 1) * 128], sl[:, kk, :], kk == 0, kk == G - 1,
                           [slR, hgR[slot][kk]], [bankR[bb[t]]])
                for t in range(4):
                    rs = resid[t][:, n * 512:(n + 1) * 512]
                    if gi == 0:
                        vstt(rs, rs, ALPHA, banks[bb[t]][:], ALU.mult, ALU.add, [bankR[bb[t]], residR[t][n]], [residR[t][n]])
                    else:
                        vtt(rs, rs, banks[bb[t]][:], ALU.add, [bankR[bb[t]], residR[t][n]], [residR[t][n]])
                    if gi == ngroups - 1:
                        ln_stats(resid, residR, t, n)
        if it < NT - 1:
            aM.switch()
            pre_stgR = [aM.res(), aM.res()]
            for blk in range(2):
                P.op("gpsimd", lambda e, o=stg[blk], i_=xpad[tp0 + T + blk * 128:tp0 + T + (blk + 1) * 128, :]:
                     e.dma_start(out=o, in_=i_), [], [pre_stgR[blk]], lane=f"stg{blk}")
        layer_norm(resid, residR, 2, gbR, gb)
        for t in range(4):
            r0 = it * T + t * 128
            P.op("scalar", lambda e, o=y[r0:r0 + 128, :], i_=resid[t]: e.dma_start(out=o, in_=i_), residR[t], [], lane=f"y{t}")

    P.emit(nc, es, y_lanes)
    es.close()
    return nc


def _t5_bucket(rel):
    nb = 16
    max_exact = 8
    ret = np.where(rel > 0, nb, 0)
    n = np.abs(rel)
    nf = np.maximum(n, 1).astype(np.float32)
    large = max_exact + (np.log(nf / max_exact) / math.log(128 / max_exact) * (nb - max_exact)).astype(np.int32)
    large = np.minimum(large, nb - 1)
    return ret + np.where(n < max_exact, n, large)


def _consts():
    ident = np.eye(128, dtype=np.float32)
    jmat = np.ascontiguousarray(ident[::-1, :])
    rel = np.arange(512) - 256
    bucket = _t5_bucket(rel)
    oh = np.zeros((33, 512), dtype=np.float32)
    inband = np.abs(rel) <= 128
    oh[bucket[inband], np.arange(512)[inband]] = 1.0
    oh[32, ~inband] = 1.0
    return ident, jmat, oh


_CACHE = {}


def kernel(x_prompt, x_sample, mem_prompt, mem_sample, rel_bias, w_in, conv_w, conv_b,
           conv_ln_g, conv_ln_b, sink, w_out, ln1_g, ln1_b, xq_w, xk_w, xv_w, xo_w,
           ln2_g, ln2_b, w_gate, w_up, w_down, ln3_g, ln3_b):
    f = lambda a: np.ascontiguousarray(np.asarray(a, dtype=np.float32))
    x_prompt, x_sample, mem_prompt, mem_sample = f(x_prompt), f(x_sample), f(mem_prompt), f(mem_sample)
    B, S, _ = x_prompt.shape
    DB, DS, _ = x_sample.shape
    assert DB == 1
    n_pc = B
    n_sc = N_CORES - n_pc
    tok = S
    assert DS == n_sc * tok and tok % T == 0
    NT = tok // T
    DFF = w_gate.shape[-1]
    key = (NT, DFF)
    if key not in _CACHE:
        _CACHE[key] = build_program(NT, DFF)
    nc = _CACHE[key]

    ident, jmat, oh = _consts()
    cpar = f(np.concatenate([conv_w[0], conv_b[0][None], conv_ln_g[0][None], conv_ln_b[0][None]], axis=0))
    lnp = f(np.stack([ln1_g[0], ln1_b[0], ln2_g[0], ln2_b[0], ln3_g[0], ln3_b[0]], axis=0))
    relb = f(np.concatenate([np.asarray(rel_bias, np.float32), np.full((1, NH), NEG, np.float32)], axis=0))
    shared = dict(w_in=f(w_in[0]), w_out=f(w_out[0]), xq_w=f(xq_w[0]), xk_w=f(xk_w[0]), xv_w=f(xv_w[0]),
                  xo_w=f(xo_w[0]), w_gate=f(w_gate[0]), w_up=f(w_up[0]), w_down=f(w_down[0]),
                  cpar=cpar, lnp=lnp, relb=relb, sink=f(np.asarray(sink, np.float32).reshape(1, NH)),
                  ident=ident, jmat=jmat, oh=oh)
    NB = NT * 4 + 2
    in_maps = []
    for c in range(N_CORES):
        xp = np.zeros((tok + 2 * HALO, D), np.float32)
        valid = np.zeros((tok + 2 * HALO,), bool)
        if c < n_pc:
            xp[HALO:HALO + tok] = x_prompt[c]
            valid[HALO:HALO + tok] = True
            mem = mem_prompt[c]
        else:
            j = c - n_pc
            lo = j * tok - HALO
            hi = (j + 1) * tok + HALO
            slo, shi = max(lo, 0), min(hi, DS)
            xp[slo - lo:shi - lo] = x_sample[0, slo:shi]
            valid[slo - lo:shi - lo] = True
            mem = mem_sample[0]
        kb = np.where(valid, 0.0, NEG).astype(np.float32).reshape(NB, 128).T
        m = dict(shared)
        m.update(xpad=xp, mem=f(mem), kbias=np.ascontiguousarray(kb))
        in_maps.append(m)
    res = run_bass_kernel_spmd(nc, in_maps, core_ids=list(range(N_CORES)))
    if DEBUG:
        global _DBG
        _DBG = res.results
    outs = [np.asarray(r["y"], dtype=np.float32) for r in res.results]
    y_prompt = np.stack(outs[:n_pc], axis=0)
    y_sample = np.concatenate(outs[n_pc:], axis=0)[None]
    return (y_prompt, y_sample)
```

```python
import math
from contextlib import ExitStack

import numpy as np
import concourse.bass as bass
import concourse.mybir as mybir
from concourse.bass_utils import run_bass_kernel_spmd

F32 = mybir.dt.float32
BF16 = mybir.dt.bfloat16
ALU = mybir.AluOpType
AF = mybir.ActivationFunctionType

D = 4096
KC = D // 128
CW = 2048
NCH = CW // 128
NH = 16
NKV = 4
XH = 4
MEM = 256
CONV_K = 31
T = 512
HALO = 128
TW = T + 2 * HALO
ALPHA = 2.0 ** 0.25
LN_EPS = 1e-5
SCALE = 128.0 ** -0.5
NEG = -30000.0
N_CORES = 8


class Res:
    __slots__ = ("w", "r")

    def __init__(self, seed=None):
        self.w = None
        self.r = dict(seed) if seed else {}


class Area:
    def __init__(self):
        self.cur = []

    def switch(self):
        seed = {}
        for res in self.cur:
            if res.w is not None:
                seed[("w", res.w)] = res.w
            for k, v in res.r.items():
                if k in seed:
                    seed[k] = max(seed[k], v)
                else:
                    seed[k] = v
        self.cur = []
        self.seed = seed
        return seed

    def res(self, extra_seed=None):
        s = dict(getattr(self, "seed", {}))
        if extra_seed:
            for k, v in extra_seed.items():
                s[k] = max(s.get(k, -1), v)
        r = Res(s)
        self.cur.append(r)
        return r

    def adopt(self, r):
        self.cur.append(r)


class Prog:
    def __init__(self):
        self.ins = []

    def op(self, eng, fn, reads=(), writes=(), lane=None):
        idx = len(self.ins)
        deps = set()
        for r in reads:
            if r.w is not None:
                deps.add(r.w)
        for w in writes:
            if w.w is not None:
                deps.add(w.w)
            deps.update(w.r.values())
        key = eng if lane is None else ("dma", idx)
        for r in reads:
            r.r[key] = idx
        for w in writes:
            w.w = idx
            w.r = {}
        self.ins.append([eng, fn, deps, lane, False, 0])
        return idx

    def emit(self, nc, es, final_wait_lanes):
        ins = self.ins
        engs = ("tensor", "vector", "scalar", "sync", "gpsimd")
        for rec in ins:
            for d in rec[2]:
                p = ins[d]
                if p[3] is None and not (p[0] == "tensor" and rec[0] == "tensor"):
                    p[4] = True
        cnt = {}
        lanes = {}
        for rec in ins:
            if rec[3] is not None:
                lanes[rec[3]] = lanes.get(rec[3], 0) + 16
                rec[5] = lanes[rec[3]]
            elif rec[4]:
                cnt[rec[0]] = cnt.get(rec[0], 0) + 1
                rec[5] = cnt[rec[0]]
        sem_e = {e: es.enter_context(nc.semaphore("se_" + e)) for e in ("tensor", "vector", "scalar", "gpsimd")}
        sem_l = {l: es.enter_context(nc.semaphore("sl_" + l)) for l in lanes}
        streams = {e: [] for e in engs}
        for i, rec in enumerate(ins):
            streams[rec[0]].append(i)
        block = es.enter_context(nc.Block())

        def run(e, name):
            waited = {}
            for i in streams[name]:
                rec = ins[i]
                need = {}
                for d in rec[2]:
                    p = ins[d]
                    if p[3] is not None:
                        k = ("l", p[3])
                    else:
                        if p[0] == "tensor" and name == "tensor":
                            continue
                        k = ("e", p[0])
                    if p[5] > need.get(k, 0):
                        need[k] = p[5]
                for k, v in need.items():
                    if waited.get(k, 0) >= v:
                        continue
                    waited[k] = v
                    e.wait_ge(sem_l[k[1]] if k[0] == "l" else sem_e[k[1]], v)
                inst = rec[1](e)
                if rec[3] is not None:
                    inst.then_inc(sem_l[rec[3]], 16)
                elif rec[4]:
                    inst.then_inc(sem_e[name], 1)
            if name == "sync":
                for l in final_wait_lanes:
                    if l in lanes:
                        e.wait_ge(sem_l[l], lanes[l])

        @block.tensor
        def _(e):
            run(e, "tensor")

        @block.vector
        def _(e):
            run(e, "vector")

        @block.scalar
        def _(e):
            run(e, "scalar")

        @block.gpsimd
        def _(e):
            run(e, "gpsimd")

        @block.sync
        def _(e):
            run(e, "sync")


DEBUG = False


def build_program(NT, DFF):
    NB = NT * 4 + 2
    NTOK = NT * T
    IN_COLS = 7168
    assert DFF % 256 == 0
    NFC = DFF // 128
    nc = bass.Bass("TRN2", target_bir_lowering=False)

    def din(name, shape):
        return nc.dram_tensor(name, list(shape), F32, kind="ExternalInput").ap()

    xpad = din("xpad", [NTOK + 2 * HALO, D])
    mem = din("mem", [MEM, D])
    kbias_d = din("kbias", [128, NB])
    w_in = din("w_in", [D, IN_COLS])
    w_out = din("w_out", [D, D])
    xq_w = din("xq_w", [D, 512])
    xk_w = din("xk_w", [D, 512])
    xv_w = din("xv_w", [D, 512])
    xo_w = din("xo_w", [512, D])
    w_gate = din("w_gate", [D, DFF])
    w_up = din("w_up", [D, DFF])
    w_down = din("w_down", [DFF, D])
    cpar_d = din("cpar", [34, CW])
    lnp = din("lnp", [6, D])
    relb_d = din("relb", [33, NH])
    sink_d = din("sink", [1, NH])
    ident_d = din("ident", [128, 128])
    jmat_d = din("jmat", [128, 128])
    oh_d = din("oh", [33, 512])
    y = nc.dram_tensor("y", [NTOK, D], F32, kind="ExternalOutput").ap()
    gtab_t = nc.dram_tensor("gtab", [16, 512], F32, kind="ExternalOutput" if DEBUG else "Internal")
    gtab_d = gtab_t.ap()
    bd_d = nc.dram_tensor("bd", [3, 128, 2048], F32, kind="ExternalOutput" if DEBUG else "Internal").ap()

    if DEBUG:
        dbg_mix = nc.dram_tensor("dbg_mix", [128, 16384], BF16, kind="ExternalOutput").ap()
        dbg_x1 = nc.dram_tensor("dbg_x1", [512, D], F32, kind="ExternalOutput").ap()
        dbg_x2 = nc.dram_tensor("dbg_x2", [512, D], F32, kind="ExternalOutput").ap()
        dbg_r0 = nc.dram_tensor("dbg_r0", [512, D], F32, kind="ExternalOutput").ap()
        dbg_R = nc.dram_tensor("dbg_R", [128, 16384], F32, kind="ExternalOutput").ap()
    P = Prog()
    es = ExitStack()
    E = es.enter_context

    Rt = E(nc.sbuf_tensor("Rt", [128, 16384], F32))
    Xt = E(nc.sbuf_tensor("Xt", [128, 24576], BF16))
    Mt = E(nc.sbuf_tensor("Mt", [128, 16384], BF16))
    slabs = [E(nc.sbuf_tensor(f"slab{i}", [128, 8192], BF16)) for i in range(3)]
    T4 = E(nc.sbuf_tensor("T4", [128, 1024], F32))
    ident = E(nc.sbuf_tensor("ident_s", [128, 128], F32))
    jmat = E(nc.sbuf_tensor("jmat_s", [128, 128], F32))
    ones_f = E(nc.sbuf_tensor("ones_f", [128, 128], F32))
    ones_b = E(nc.sbuf_tensor("ones_b", [128, 128], BF16))
    eps_t = E(nc.sbuf_tensor("eps_t", [128, 1], F32))
    cw = E(nc.sbuf_tensor("cw", [128, NCH, 34], F32))
    esk = E(nc.sbuf_tensor("esk", [128, NH], F32))
    kbias = E(nc.sbuf_tensor("kbias_s", [128, NB], F32))
    KxT = E(nc.sbuf_tensor("KxT", [128, XH, MEM], BF16))
    Vx = E(nc.sbuf_tensor("Vx", [128, 2, 512], BF16))
    stats = E(nc.sbuf_tensor("stats", [128, 4, 8, 6], F32))
    mv = E(nc.sbuf_tensor("mv", [128, 4, 2], F32))
    rstd = E(nc.sbuf_tensor("rstd", [128, 4], F32))
    nmr = E(nc.sbuf_tensor("nmr", [128, 4], F32))
    carry = E(nc.sbuf_tensor("carry", [128, NCH, 32], F32))
    gcol = E(nc.sbuf_tensor("gcol", [128, 128], F32))
    carryR = [Res() for _ in range(NCH)]
    banks = [E(nc.psum_tensor(f"bank{i}", [128, 512], F32)) for i in range(8)]

    Xf = Xt[:].bitcast(F32)
    Mf = Mt[:].bitcast(F32)
    Rb = Rt[:].bitcast(BF16)

    bankR = [Res() for _ in range(8)]
    slabR = [Res() for _ in range(3)]
    constR = Res()
    aR, aXA, aXB, aM, aT4 = Area(), Area(), Area(), Area(), Area()
    for a in (aR, aXA, aXB, aM, aT4):
        a.switch()

    st = {"bank": 0, "slab": 0, "ev": 0}

    st["nring"] = 6

    def ring():
        b = st["bank"] % st["nring"]
        st["bank"] = (b + 1) % st["nring"]
        return b

    def load_slab(src, shape3):
        s = st["slab"]
        st["slab"] = (s + 1) % 3
        a, b = shape3
        view = slabs[s][:, 0:a * b].rearrange("p (a b) -> p a b", a=a)
        P.op("gpsimd", lambda e, v=view, sr=src: e.dma_start(out=v, in_=sr), writes=[slabR[s]], lane=f"slab{s}")
        return view, slabR[s]

    def fm_slab(W, c0, ncols=256):
        return load_slab(W[:, c0:c0 + ncols].rearrange("(k p) c -> p k c", p=128), (KC, ncols))

    def mm(out, lhsT, rhs, start, stop, reads, writes):
        P.op("tensor", lambda e: e.matmul(out, lhsT, rhs, start=start, stop=stop), reads, writes)

    def tr(out, in_, idn, reads, writes):
        P.op("tensor", lambda e: e.transpose(out, in_, idn), reads, writes)

    def act(out, in_, func, reads, writes, bias=None, scale=None):
        kw = {}
        if bias is not None:
            kw["bias"] = bias
        if scale is not None:
            kw["scale"] = scale
        P.op("scalar", lambda e: e.activation(out, in_, func, **kw), reads, writes)

    def evac(out, in_, reads, writes):
        st["ev"] ^= 1
        if st["ev"]:
            P.op("scalar", lambda e: e.activation(out, in_, AF.Copy), reads, writes)
        else:
            P.op("vector", lambda e: e.tensor_copy(out=out, in_=in_), reads, writes)

    def evac_act(out, in_, reads, writes):
        P.op("scalar", lambda e: e.activation(out, in_, AF.Copy), reads, writes)

    def vtt(out, in0, in1, op, reads, writes):
        P.op("vector", lambda e: e.tensor_tensor(out=out, in0=in0, in1=in1, op=op), reads, writes)

    def vts(out, in0, s1, s2, op0, op1, reads, writes):
        if op1 is None:
            P.op("vector", lambda e: e.tensor_scalar(out=out, in0=in0, scalar1=s1, scalar2=None, op0=op0), reads, writes)
        else:
            P.op("vector", lambda e: e.tensor_scalar(out=out, in0=in0, scalar1=s1, scalar2=s2, op0=op0, op1=op1), reads, writes)

    def vstt(out, in0, scalar, in1, op0, op1, reads, writes):
        P.op("vector", lambda e: e.scalar_tensor_tensor(out=out, in0=in0, scalar=scalar, in1=in1, op0=op0, op1=op1), reads, writes)

    def sdma(out, in_, reads, writes, lane):
        P.op("sync", lambda e: e.dma_start(out=out, in_=in_), reads, writes, lane=lane)

    sdma(ident[:], ident_d, [], [constR], "c0")
    sdma(jmat[:], jmat_d, [], [constR], "c1")
    sdma(kbias[:], kbias_d, [], [constR], "c2")
    P.op("vector", lambda e: e.memset(ones_f[:], 1.0), [], [constR])
    P.op("vector", lambda e: e.memset(ones_b[:], 1.0), [], [constR])
    P.op("vector", lambda e: e.memset(eps_t[:], LN_EPS), [], [constR])

    cparR = aM.res()
    cpar_s = Mf[0:34, 0:CW]
    sdma(cpar_s, cpar_d, [], [cparR], "c3")
    cwR = Res()
    for c in range(NCH):
        b = ring()
        tr(banks[b][:, 0:34], Mf[0:34, c * 128:(c + 1) * 128], ident[0:34, 0:34], [cparR, constR], [bankR[b]])
        evac(cw[:, c, :], banks[b][:, 0:34], [bankR[b]], [cwR])

    eskR = Res()
    sdma(esk[:], sink_d[0:1, :].broadcast_to([128, NH]), [], [eskR], "c4")
    act(esk[:], esk[:], AF.Exp, [eskR], [eskR])

    relbR = aM.res()
    relb_s = Mf[0:33, 4096:4096 + NH]
    oh_s = Mf[0:33, 4608:4608 + 512]
    sdma(relb_s, relb_d, [], [relbR], "c5")
    sdma(oh_s, oh_d, [], [relbR], "c6")
    b = ring()
    mm(banks[b][0:16, :], relb_s, oh_s, True, True, [relbR], [bankR[b]])
    gsbR = aM.res()
    gsb = Mf[0:16, 5120:5120 + 512]
    evac(gsb, banks[b][0:16, :], [bankR[b]], [gsbR])
    gtabR = Res()
    sdma(gtab_d, gsb, [gsbR], [gtabR], "c7")
    BdR = [Res() for _ in range(3)]

    def build_bias_tables():
      for o in (-1, 0, 1):
        hkR = aR.res()
        hk = Rt[:, (o + 1) * 2048:(o + 2) * 2048].rearrange("p (h k) -> p h k", h=NH)
        src = bass.AP(tensor=gtab_t, offset=128 * o + 129, ap=[[1, 128], [512, NH], [1, 128]])
        sdma(hk, src, [gtabR], [hkR], f"hk{o + 1}")
        btR = aR.res()
        bt = Rt[:, 6144 + (o + 1) * 2048:6144 + (o + 2) * 2048]
        for hq in range(4):
            b = ring()
            for hh in range(4):
                mm(banks[b][:, hh * 128:(hh + 1) * 128], hk[:, hq * 4 + hh, :], jmat[:], True, True,
                   [hkR, constR], [bankR[b]])
            evac(bt[:, hq * 512:(hq + 1) * 512], banks[b][:], [bankR[b]], [btR])
        sdma(bd_d[o + 1], bt, [btR], [BdR[o + 1]], f"bdw{o + 1}")

    gcolR = Res()
    for li in range(2):
        lncR = aM.res()
        lnc_s = Mf[0:64, 6144 + li * 128:6144 + (li + 1) * 128]
        sdma(lnc_s, lnp[2 * li:2 * li + 2, :].rearrange("a (k p) -> (a k) p", p=128), [], [lncR], f"c{8 + li}")
        b = ring()
        tr(banks[b][:, 0:64], lnc_s, ident[0:64, 0:64], [lncR, constR], [bankR[b]])
        evac(gcol[:, li * 64:(li + 1) * 64], banks[b][:, 0:64], [bankR[b]], [gcolR])

    aM.switch()
    stgR = [aM.res(), aM.res()]
    stg = [Mf[:, 0:4096], Mf[:, 4096:8192]]
    aXA.switch()
    memTR = aXA.res()
    memT = Xt[:, 0:KC * MEM].rearrange("p (k m) -> p k m", k=KC)
    for mb in range(2):
        sdma(stg[mb], mem[mb * 128:(mb + 1) * 128, :], [], [stgR[mb]], f"stg{mb}")
        for kq in range(8):
            b = ring()
            for kk in range(4):
                k = kq * 4 + kk
                tr(banks[b][:, kk * 128:(kk + 1) * 128], stg[mb][:, k * 128:(k + 1) * 128], ident[:],
                   [stgR[mb], constR], [bankR[b]])
            evac(memT[:, kq * 4:kq * 4 + 4, mb * 128:(mb + 1) * 128],
                 banks[b][:].rearrange("p (a b) -> p a b", a=4), [bankR[b]], [memTR])
    KxR = Res()
    VxR = Res()
    for p in range(2):
        sl, slR = fm_slab(xk_w, p * 256)
        for j in range(2):
            h = 2 * p + j
            b = ring()
            for k in range(KC):
                mm(banks[b][:, 0:MEM], sl[:, k, j * 128:(j + 1) * 128], memT[:, k, :], k == 0, k == KC - 1,
                   [slR, memTR], [bankR[b]])
            evac(KxT[:, h, :], banks[b][:, 0:MEM], [bankR[b]], [KxR])
    for p in range(2):
        sl, slR = fm_slab(xv_w, p * 256)
        for mb in range(2):
            b = ring()
            for k in range(KC):
                mm(banks[b][:, 0:256], memT[:, k, mb * 128:(mb + 1) * 128], sl[:, k, :], k == 0, k == KC - 1,
                   [slR, memTR], [bankR[b]])
            evac(Vx[:, mb, p * 256:(p + 1) * 256], banks[b][:, 0:256], [bankR[b]], [VxR])

    smallR = [Res() for _ in range(4)]

    def ln_stats(resid, residR, t, n):
        P.op("vector", lambda e, t=t, n=n: e.bn_stats(out=stats[:, t, n, :], in_=resid[t][:, n * 512:(n + 1) * 512]),
             [residR[t][n]], [smallR[t]])

    def ln_tail():
        for t in range(4):
            P.op("vector", lambda e, t=t: e.bn_aggr(out=mv[:, t, :], in_=stats[:, t].rearrange("p a b -> p (a b)")),
                 [smallR[t]], [smallR[t]])
        act(rstd[:, 0:4], mv[:, :, 1], AF.Ln, smallR + [constR], smallR, bias=eps_t[:, 0:1])
        act(rstd[:, 0:4], rstd[:, 0:4], AF.Exp, smallR, smallR, scale=-0.5)
        vstt(nmr[:, 0:4], mv[:, :, 0], -1.0, rstd[:, 0:4], ALU.mult, ALU.mult, smallR, smallR)

    def layer_norm(resid, residR, li, gbR, gb):
        ln_tail()
        for cg in range(4):
            bf = cg % 2
            c0 = cg * 1024
            sdma(gb[bf][0], lnp[2 * li:2 * li + 1, c0:c0 + 1024].broadcast_to([128, 1024]), [], [gbR[bf][0]], f"gbg{bf}")
            sdma(gb[bf][1], lnp[2 * li + 1:2 * li + 2, c0:c0 + 1024].broadcast_to([128, 1024]), [], [gbR[bf][1]], f"gbb{bf}")
            for t in range(4):
                rr = [residR[t][2 * cg], residR[t][2 * cg + 1]]
                sl_ = resid[t][:, c0:c0 + 1024]
                act(sl_, sl_, AF.Identity, rr + [smallR[t]], rr, bias=nmr[:, t:t + 1], scale=rstd[:, t:t + 1])
                vtt(sl_, sl_, gb[bf][0], ALU.mult, rr + [gbR[bf][0]], rr)
                vtt(sl_, sl_, gb[bf][1], ALU.add, rr + [gbR[bf][1]], rr)

    def ln_norm_T(resid, residR, li, dstT, dstR):
        ln_tail()
        for cg in range(4):
            c0 = cg * 1024
            for t in range(4):
                rr = [residR[t][2 * cg], residR[t][2 * cg + 1]]
                sl_ = resid[t][:, c0:c0 + 1024]
                act(sl_, sl_, AF.Identity, rr + [smallR[t]], rr, bias=nmr[:, t:t + 1], scale=rstd[:, t:t + 1])
        for k in range(KC):
            b = ring()
            for t in range(4):
                tr(banks[b][:, t * 128:(t + 1) * 128], resid[t][:, k * 128:(k + 1) * 128], ident[:],
                   [residR[t][k // 4], constR], [bankR[b]])
            gc = gcol[:, li * 64 + k:li * 64 + k + 1]
            bc = gcol[:, li * 64 + 32 + k:li * 64 + 32 + k + 1]
            vts(dstT[:, k, :], banks[b][:], gc, bc, ALU.mult, ALU.add, [bankR[b], gcolR], [dstR[k]])

    def ln_affine_items(resid, residR, li, gbR, gb):
        items = []
        for cg in range(4):
            bf = cg % 2
            c0 = cg * 1024

            def load(cg=cg, bf=bf, c0=c0):
                sdma(gb[bf][0], lnp[2 * li:2 * li + 1, c0:c0 + 1024].broadcast_to([128, 1024]), [], [gbR[bf][0]], f"gbg{bf}")
                sdma(gb[bf][1], lnp[2 * li + 1:2 * li + 2, c0:c0 + 1024].broadcast_to([128, 1024]), [], [gbR[bf][1]], f"gbb{bf}")
            for t in range(4):
                def item(cg=cg, bf=bf, c0=c0, t=t, load=load):
                    if t == 0:
                        load()
                    rr = [residR[t][2 * cg], residR[t][2 * cg + 1]]
                    sl_ = resid[t][:, c0:c0 + 1024]
                    vtt(sl_, sl_, gb[bf][0], ALU.mult, rr + [gbR[bf][0]], rr)
                    vtt(sl_, sl_, gb[bf][1], ALU.add, rr + [gbR[bf][1]], rr)
                items.append(item)
        return items

    resid = [Rt[:, t * 4096:(t + 1) * 4096] for t in range(4)]
    gb = [[Xf[:, 8192 + bf * 2048:8192 + bf * 2048 + 1024], Xf[:, 8192 + bf * 2048 + 1024:8192 + (bf + 1) * 2048]]
          for bf in range(2)]
    y_lanes = [f"y{t}" for t in range(4)]

    def emit_y(it_, residR_):
        for t in range(4):
            r0 = it_ * T + t * 128
            P.op("scalar", lambda e, o=y[r0:r0 + 128, :], i_=resid[t]: e.dma_start(out=o, in_=i_), residR_[t], [], lane=f"y{t}")

    pending_y = None
    pre_stgR = None
    for it in range(NT):
        tp0 = it * T
        if pre_stgR is None:
            aM.switch()
            stgR = [aM.res(), aM.res()]
            npre = 0
        else:
            stgR = pre_stgR
            npre = 2
        sa = aXA.switch()
        sb_ = aXB.switch()
        xTR = []
        mseed = dict(sa)
        for k_, v_ in sb_.items():
            mseed[k_] = max(mseed.get(k_, -1), v_)
        for blk in range(6):
            r = Res(mseed)
            aXA.adopt(r)
            aXB.adopt(r)
            xTR.append(r)
        xT = Xt[:, 0:KC * TW].rearrange("p (k t) -> p k t", k=KC)
        for blk in range(6):
            s = blk % 2
            if blk >= npre:
                sdma(stg[s], xpad[tp0 + blk * 128:tp0 + (blk + 1) * 128, :], [], [stgR[s]], f"stg{s}")
            for kq in range(8):
                b = ring()
                for kk in range(4):
                    k = kq * 4 + kk
                    tr(banks[b][:, kk * 128:(kk + 1) * 128], stg[s][:, k * 128:(k + 1) * 128], ident[:],
                       [stgR[s], constR], [bankR[b]])
                (evac_act if blk < npre else evac)(xT[:, kq * 4:kq * 4 + 4, blk * 128:(blk + 1) * 128],
                     banks[b][:].rearrange("p (a b) -> p a b", a=4), [bankR[b]], [xTR[blk]])

        if pending_y is not None:
            emit_y(*pending_y)
            pending_y = None
        if it == 0:
            build_bias_tables()

        aM.switch()
        sigR = [aM.res(), aM.res()]
        uR = [aM.res(), aM.res()]
        sqR = [aM.res() for _ in range(4)]
        sig = [Mf[:, i * 544:(i + 1) * 544] for i in range(2)]
        u = [Mf[:, 1088 + i * 544:1088 + (i + 1) * 544] for i in range(2)]
        sq = [Mf[:, 2176 + i * 512:2176 + (i + 1) * 512] for i in range(4)]
        aR.switch()
        hTR = [aR.res() for _ in range(NCH)]
        qTR = [aR.res() for _ in range(NH)]
        kTR = [aR.res() for _ in range(NKV)]
        VR = [aR.res() for _ in range(6)]
        spR = aR.res()
        hT = Rt[:, 0:8192].rearrange("p (c t) -> p c t", c=NCH)
        qT = Rb[:, 16384:24576].rearrange("p (h t) -> p h t", h=NH)
        kT = Rb[:, 24576:24576 + NKV * TW].rearrange("p (g t) -> p g t", g=NKV)
        Vt = Rb[:, 27648:27648 + 6 * 512].rearrange("p (b c) -> p b c", b=6)
        sp0 = Rt[:, 15360:15872]
        sp1 = Rt[:, 15872:16384]
        SUMB, SQB = 6, 7
        allx = xTR

        def conv_stats(pcs):
            for j in range(2):
                c = 2 * pcs + j
                s4 = c % 4
                act(sq[s4], hT[:, c, :], AF.Square, [hTR[c]], [sqR[s4]])
                mm(banks[SUMB][:], ones_f[:], hT[:, c, :], c == 0, c == NCH - 1, [constR, hTR[c]], [bankR[SUMB]])
                mm(banks[SQB][:], ones_f[:], sq[s4], c == 0, c == NCH - 1, [constR, sqR[s4]], [bankR[SQB]])

        def evac_a(out, in_, reads, writes):
            P.op("scalar", lambda e: e.activation(out, in_, AF.Copy), reads, writes)

        def q_proj(pq):
            sl, slR = fm_slab(w_in, 2 * CW + pq * 256)
            for j in range(2):
                h = 2 * pq + j
                b = ring()
                for k in range(KC):
                    mm(banks[b][:], sl[:, k, j * 128:(j + 1) * 128], xT[:, k, 128:640], k == 0, k == KC - 1,
                       [slR] + allx, [bankR[b]])
                evac_a(qT[:, h, :], banks[b][:], [bankR[b]], [qTR[h]])

        def k_proj(pk):
            sl, slR = fm_slab(w_in, 2 * CW + 2048 + pk * 256)
            for j in range(2):
                g = 2 * pk + j
                bA = ring()
                bB = ring()
                for k in range(KC):
                    mm(banks[bA][:], sl[:, k, j * 128:(j + 1) * 128], xT[:, k, 0:512], k == 0, k == KC - 1,
                       [slR] + allx, [bankR[bA]])
                for k in range(KC):
                    mm(banks[bB][:, 0:256], sl[:, k, j * 128:(j + 1) * 128], xT[:, k, 512:768], k == 0, k == KC - 1,
                       [slR] + allx, [bankR[bB]])
                evac_a(kT[:, g, 0:512], banks[bA][:], [bankR[bA]], [kTR[g]])
                evac_a(kT[:, g, 512:768], banks[bB][:, 0:256], [bankR[bB]], [kTR[g]])

        def v_proj(pv):
            sl, slR = fm_slab(w_in, 2 * CW + 2048 + 512 + pv * 256)
            for blk in range(6):
                b = ring()
                for k in range(KC):
                    mm(banks[b][:, 0:256], xT[:, k, blk * 128:(blk + 1) * 128], sl[:, k, :], k == 0, k == KC - 1,
                       [slR, xTR[blk]], [bankR[b]])
                evac_a(Vt[:, blk, pv * 256:(pv + 1) * 256], banks[b][:, 0:256], [bankR[b]], [VR[blk]])

        extras = {1: lambda: k_proj(0), 3: lambda: k_proj(1), 5: lambda: v_proj(0), 7: lambda: v_proj(1)}

        for pc in range(NCH // 2):
            slV, slVR = fm_slab(w_in, pc * 256)
            slG, slGR = fm_slab(w_in, CW + pc * 256)
            if it == 0:
                sbk = ring()
            for j in range(2):
                c = 2 * pc + j
                bv = ring()
                for k in range(KC):
                    mm(banks[bv][:], slV[:, k, j * 128:(j + 1) * 128], xT[:, k, 144:656], k == 0, k == KC - 1,
                       [slVR] + allx, [bankR[bv]])
                if it == 0:
                    for k in range(KC):
                        mm(banks[sbk][:, (2 * j) * 32:(2 * j + 1) * 32], slV[:, k, j * 128:(j + 1) * 128], xT[:, k, 112:144],
                           k == 0, k == KC - 1, [slVR] + allx, [bankR[sbk]])
                bg = ring()
                for k in range(KC):
                    mm(banks[bg][:], slG[:, k, j * 128:(j + 1) * 128], xT[:, k, 144:656], k == 0, k == KC - 1,
                       [slGR] + allx, [bankR[bg]])
                if it == 0:
                    for k in range(KC):
                        mm(banks[sbk][:, (2 * j + 1) * 32:(2 * j + 2) * 32], slG[:, k, j * 128:(j + 1) * 128], xT[:, k, 112:144],
                           k == 0, k == KC - 1, [slGR] + allx, [bankR[sbk]])
                act(sig[j][:, 32:544], banks[bg][:], AF.Sigmoid, [bankR[bg]], [sigR[j]])
                if it == 0:
                    act(sig[j][:, 0:32], banks[sbk][:, (2 * j + 1) * 32:(2 * j + 2) * 32], AF.Sigmoid, [bankR[sbk]], [sigR[j]])
                    vtt(u[j][:, 0:32], banks[sbk][:, (2 * j) * 32:(2 * j + 1) * 32], sig[j][:, 0:32], ALU.mult,
                        [bankR[sbk], sigR[j]], [uR[j]])
                else:
                    act(u[j][:, 0:32], carry[:, c, :], AF.Copy, [carryR[c]], [uR[j]])
                vtt(u[j][:, 32:544], banks[bv][:], sig[j][:, 32:544], ALU.mult, [bankR[bv], sigR[j]], [uR[j]])
            if it < NT - 1:
                for j in range(2):
                    c = 2 * pc + j
                    act(carry[:, c, :], u[j][:, 512:544], AF.Copy, [uR[j]], [carryR[c]])
            q_proj(pc)
            if pc in extras:
                extras[pc]()
            for tap in range(CONV_K):
                for j in range(2):
                    c = 2 * pc + j
                    if tap == 0:
                        vts(hT[:, c, :], u[j][:, 1:513], cw[:, c, 0:1], cw[:, c, 31:32], ALU.mult, ALU.add,
                            [uR[j], cwR], [hTR[c]])
                    else:
                        vstt(hT[:, c, :], u[j][:, tap + 1:tap + 513], cw[:, c, tap:tap + 1], hT[:, c, :], ALU.mult, ALU.add,
                             [uR[j], cwR, hTR[c]], [hTR[c]])
            if pc > 0:
                conv_stats(pc - 1)
        conv_stats(NCH // 2 - 1)

        aM.switch()
        mixR = [aM.res() for _ in range(32)]
        mixT = Mt[:, 0:32 * 512].rearrange("p (c t) -> p c t", c=32)
        inv = 1.0 / CW
        vts(sp0, banks[SUMB][:], inv, None, ALU.mult, None, [bankR[SUMB]], [spR])
        vtt(sp1, sp0, sp0, ALU.mult, [spR], [spR])
        vstt(sp1, banks[SQB][:], inv, sp1, ALU.mult, ALU.subtract, [bankR[SQB], spR], [spR])
        act(sp1, sp1, AF.Ln, [spR, constR], [spR], bias=eps_t[:, 0:1])
        act(sp1, sp1, AF.Exp, [spR], [spR], scale=-0.5)
        for c in range(NCH):
            vtt(hT[:, c, :], hT[:, c, :], sp0, ALU.subtract, [hTR[c], spR], [hTR[c]])
            vtt(hT[:, c, :], hT[:, c, :], sp1, ALU.mult, [hTR[c], spR], [hTR[c]])
            act(mixT[:, c, :], hT[:, c, :], AF.Silu, [hTR[c], cwR], [mixR[c]], bias=cw[:, c, 33:34], scale=cw[:, c, 32:33])

        aXA.switch()
        BtR = [aXA.res() for _ in range(3)]
        PR = [aXA.res() for _ in range(8)]
        Pt = [Xt[:, 12288 + i * 512:12288 + (i + 1) * 512] for i in range(8)]
        for o in range(3):
            sdma(Xf[:, o * 2048:(o + 1) * 2048], bd_d[o], [BdR[o]], [BtR[o]], f"B{o}")
        st["nring"] = 8
        sp_seed = dict(spR.r)
        if spR.w is not None:
            sp_seed[("w", spR.w)] = spR.w
        sp0R = aR.res(sp_seed)
        sp1R = aR.res(sp_seed)
        iters = [(qb, g) for qb in range(4) for g in range(NKV)]
        Sb = {}
        Pl = {}

        def att_S(i):
            qb, g = iters[i]
            sb3 = [ring(), ring(), ring()]
            for o in range(3):
                kb = qb + o
                for hh in range(4):
                    h = 4 * g + hh
                    mm(banks[sb3[o]][:, hh * 128:(hh + 1) * 128], kT[:, g, kb * 128:(kb + 1) * 128],
                       qT[:, h, qb * 128:(qb + 1) * 128], True, True, [kTR[g], qTR[h]], [bankR[sb3[o]]])
            Sb[i] = sb3

        def att_sm(i):
            qb, g = iters[i]
            pl = []
            for o in range(3):
                kb = qb + o
                bk = Sb[i][o]
                pp = (3 * i + o) % 8
                vstt(banks[bk][:], banks[bk][:], SCALE, Xf[:, o * 2048 + g * 512:o * 2048 + (g + 1) * 512],
                     ALU.mult, ALU.add, [bankR[bk], BtR[o]], [bankR[bk]])
                gb_col = it * 4 + kb
                act(Pt[pp], banks[bk][:], AF.Exp, [bankR[bk], constR], [PR[pp]], bias=kbias[:, gb_col:gb_col + 1])
                pl.append(pp)
            Pl[i] = pl

        Pv = {}

        def att_pv(i):
            qb, g = iters[i]
            pl = Pl[i]
            bo = ring()
            bdn = ring()
            for o in range(3):
                kb = qb + o
                mm(banks[bo][:], Vt[:, kb, g * 128:(g + 1) * 128], Pt[pl[o]], o == 0, o == 2,
                   [VR[kb], PR[pl[o]]], [bankR[bo]])
            for o in range(3):
                mm(banks[bdn][:], ones_b[:], Pt[pl[o]], o == 0, o == 2, [constR, PR[pl[o]]], [bankR[bdn]])
            Pv[i] = (bo, bdn)

        def att_norm(i):
            qb, g = iters[i]
            bo, bdn = Pv[i]
            spb, spbR = (sp0, sp0R) if i % 2 == 0 else (sp1, sp1R)
            for hh in range(4):
                h = 4 * g + hh
                act(spb[:, hh * 128:(hh + 1) * 128], banks[bdn][:, hh * 128:(hh + 1) * 128], AF.Ln,
                    [bankR[bdn], eskR], [spbR], bias=esk[:, h:h + 1])
            act(spb, spb, AF.Exp, [spbR], [spbR], scale=-1.0)

        def att_out(i):
            qb, g = iters[i]
            bo, bdn = Pv[i]
            spb, spbR = (sp0, sp0R) if i % 2 == 0 else (sp1, sp1R)
            vtt(mixT[:, NCH + 4 * g:NCH + 4 * g + 4, qb * 128:(qb + 1) * 128],
                banks[bo][:].rearrange("p (h q) -> p h q", h=4), spb.rearrange("p (h q) -> p h q", h=4), ALU.mult,
                [bankR[bo], spbR], [mixR[NCH + 4 * g + hh] for hh in range(4)])

        att_S(0)
        att_sm(0)
        att_S(1)
        att_sm(1)
        for i in range(len(iters)):
            att_pv(i)
            if i + 2 < len(iters):
                att_S(i + 2)
            att_norm(i)
            if i + 2 < len(iters):
                att_sm(i + 2)
            att_out(i)
        st["nring"] = 6

        if DEBUG and it == 0:
            sdma(dbg_R, Rt[:], hTR + qTR + kTR + VR + [spR], [], "dbgR")
        aR.switch()
        residR = [[aR.res() for _ in range(8)] for _ in range(4)]
        for t in range(4):
            r0 = tp0 + HALO + t * 128
            sdma(resid[t], xpad[r0:r0 + 128, :], [], residR[t], f"res{t}")
        aXB.switch()
        gbR = [[aXB.res(), aXB.res()] for _ in range(2)]
        for n in range(8):
            bb = [ring() for _ in range(4)]
            for s in range(2):
                sl, slR = load_slab(w_out[s * 2048:(s + 1) * 2048, n * 512:(n + 1) * 512].rearrange("(k p) c -> p k c", p=128),
                                    (16, 512))
                for t in range(4):
                    for kk in range(16):
                        c = s * 16 + kk
                        mm(banks[bb[t]][:], mixT[:, c, t * 128:(t + 1) * 128], sl[:, kk, :], s == 0 and kk == 0,
                           s == 1 and kk == 15, [slR, mixR[c]], [bankR[bb[t]]])
            for t in range(4):
                rs = resid[t][:, n * 512:(n + 1) * 512]
                vstt(rs, rs, ALPHA, banks[bb[t]][:], ALU.mult, ALU.add, [bankR[bb[t]], residR[t][n]], [residR[t][n]])
                ln_stats(resid, residR, t, n)
        if DEBUG and it == 0:
            for t in range(4):
                sdma(dbg_r0[t * 128:(t + 1) * 128, :], resid[t], residR[t], [], f"dbg{t}")
            sdma(dbg_mix, Mt[:], mixR, [], "dbgm")
        aM.switch()
        x1TR = [aM.res() for _ in range(KC)]
        x1T = Mt[:, 0:KC * 512].rearrange("p (k t) -> p k t", k=KC)
        ln_norm_T(resid, residR, 0, x1T, x1TR)
        aff1 = ln_affine_items(resid, residR, 0, gbR, gb)

        aXA.switch()
        qxR = [aXA.res() for _ in range(XH)]
        PxR = [aXA.res() for _ in range(4)]
        oxR = [aXA.res() for _ in range(XH)]
        rdxR = [aXA.res() for _ in range(2)]
        qxT = Xt[:, 0:2048].rearrange("p (h t) -> p h t", h=XH)
        Px = [Xt[:, 2048 + i * 512:2048 + (i + 1) * 512] for i in range(4)]
        oxT = Xt[:, 4096:6144].rearrange("p (h t) -> p h t", h=XH)
        rdx = [Xf[:, 3072 + i * 512:3072 + (i + 1) * 512] for i in range(2)]
        for p in range(2):
            sl, slR = fm_slab(xq_w, p * 256)
            for j in range(2):
                h = 2 * p + j
                b = ring()
                for k in range(KC):
                    mm(banks[b][:], sl[:, k, j * 128:(j + 1) * 128], x1T[:, k, :], k == 0, k == KC - 1,
                       [slR, x1TR[k]], [bankR[b]])
                P.op("scalar", lambda e, o=qxT[:, h, :], i_=banks[b][:]: e.activation(o, i_, AF.Copy), [bankR[b]], [qxR[h]])
        for item in aff1:
            item()
        for h in range(XH):
            pidx = []
            for mb in range(2):
                b = ring()
                mm(banks[b][:], KxT[:, h, mb * 128:(mb + 1) * 128], qxT[:, h, :], True, True, [KxR, qxR[h]], [bankR[b]])
                pp = (2 * h + mb) % 4
                act(Px[pp], banks[b][:], AF.Exp, [bankR[b]], [PxR[pp]], scale=SCALE)
                pidx.append(pp)
            bo = ring()
            bdn = ring()
            for mb in range(2):
                mm(banks[bo][:], Vx[:, mb, h * 128:(h + 1) * 128], Px[pidx[mb]], mb == 0, mb == 1,
                   [VxR, PxR[pidx[mb]]], [bankR[bo]])
            for mb in range(2):
                mm(banks[bdn][:], ones_b[:], Px[pidx[mb]], mb == 0, mb == 1, [constR, PxR[pidx[mb]]], [bankR[bdn]])
            rr = h % 2
            act(rdx[rr], banks[bdn][:], AF.Ln, [bankR[bdn]], [rdxR[rr]])
            act(rdx[rr], rdx[rr], AF.Exp, [rdxR[rr]], [rdxR[rr]], scale=-1.0)
            vtt(oxT[:, h, :], banks[bo][:], rdx[rr], ALU.mult, [bankR[bo], rdxR[rr]], [oxR[h]])
        for s in range(2):
            sl, slR = load_slab(xo_w[:, s * 2048:(s + 1) * 2048].rearrange("(k p) c -> p k c", p=128), (4, 2048))
            for nn in range(4):
                n = s * 4 + nn
                bb = [ring() for _ in range(4)]
                for t in range(4):
                    for k in range(4):
                        mm(banks[bb[t]][:], oxT[:, k, t * 128:(t + 1) * 128], sl[:, k, nn * 512:(nn + 1) * 512], k == 0, k == 3,
                           [slR, oxR[k]], [bankR[bb[t]]])
                for t in range(4):
                    rs = resid[t][:, n * 512:(n + 1) * 512]
                    vstt(rs, rs, ALPHA, banks[bb[t]][:], ALU.mult, ALU.add, [bankR[bb[t]], residR[t][n]], [residR[t][n]])
                    ln_stats(resid, residR, t, n)
        aXA.switch()
        x2TR = [aXA.res() for _ in range(KC)]
        x2T = Xt[:, 0:KC * 512].rearrange("p (k t) -> p k t", k=KC)
        ln_norm_T(resid, residR, 1, x2T, x2TR)
        aff2 = ln_affine_items(resid, residR, 1, gbR, gb)

        aM.switch()
        hgR = [[aM.res() for _ in range(16)] for _ in range(2)]
        hg = [Mt[:, s * 8192:(s + 1) * 8192].rearrange("p (c t) -> p c t", c=16) for s in range(2)]
        aT4.switch()
        sgR = [aT4.res(), aT4.res()]
        sg = [T4[:, 0:512], T4[:, 512:1024]]
        ngroups = (NFC + 15) // 16
        si = 0
        for gi in range(ngroups):
            c_lo = gi * 16
            G = min(16, NFC - c_lo)
            slot = gi % 2
            for pc in range(G // 2):
                slG, slGR = fm_slab(w_gate, (c_lo + 2 * pc) * 128)
                slU, slUR = fm_slab(w_up, (c_lo + 2 * pc) * 128)
                for j in range(2):
                    cc = 2 * pc + j
                    bg = ring()
                    for k in range(KC):
                        mm(banks[bg][:], slG[:, k, j * 128:(j + 1) * 128], x2T[:, k, :], k == 0, k == KC - 1,
                           [slGR, x2TR[k]], [bankR[bg]])
                    bu = ring()
                    for k in range(KC):
                        mm(banks[bu][:], slU[:, k, j * 128:(j + 1) * 128], x2T[:, k, :], k == 0, k == KC - 1,
                           [slUR, x2TR[k]], [bankR[bu]])
                    s2 = si % 2
                    si += 1
                    act(sg[s2], banks[bg][:], AF.Silu, [bankR[bg]], [sgR[s2]])
                    vtt(hg[slot][:, cc, :], banks[bu][:], sg[s2], ALU.mult, [bankR[bu], sgR[s2]], [hgR[slot][cc]])
                for _ in range(2):
                    if aff2:
                        aff2.pop(0)()
            while gi == 0 and aff2:
                aff2.pop(0)()
            for n in range(8):
                sl, slR = load_slab(w_down[c_lo * 128:(c_lo + G) * 128, n * 512:(n + 1) * 512].rearrange("(k p) c -> p k c", p=128),
                                    (G, 512))
                bb = [ring() for _ in range(4)]
                for t in range(4):
                    for kk in range(G):
                        mm(banks[bb[t]][:], hg[slot][:, kk, t * 128:(t + 1) * 128], sl[:, kk, :], kk == 0, kk == G - 1,
                           [slR, hgR[slot][kk]], [bankR[bb[t]]])
                for t in range(4):
                    rs = resid[t][:, n * 512:(n + 1) * 512]
                    if gi == 0:
                        vstt(rs, rs, ALPHA, banks[bb[t]][:], ALU.mult, ALU.add, [bankR[bb[t]], residR[t][n]], [residR[t][n]])
                    else:
                        vtt(rs, rs, banks[bb[t]][:], ALU.add, [bankR[bb[t]], residR[t][n]], [residR[t][n]])
                    if gi == ngroups - 1:
                        ln_stats(resid, residR, t, n)
        if it < NT - 1:
            aM.switch()
            pre_stgR = [aM.res(), aM.res()]
            for blk in range(2):
                P.op("gpsimd", lambda e, o=stg[blk], i_=xpad[tp0 + T + blk * 128:tp0 + T + (blk + 1) * 128, :]:
                     e.dma_start(out=o, in_=i_), [], [pre_stgR[blk]], lane=f"stg{blk}")
        layer_norm(resid, residR, 2, gbR, gb)
        if it < NT - 1:
            pending_y = (it, residR)
        else:
            emit_y(it, residR)

    P.emit(nc, es, y_lanes)
    es.close()
    return nc


def _t5_bucket(rel):
    nb = 16
    max_exact = 8
    ret = np.where(rel > 0, nb, 0)
    n = np.abs(rel)
    nf = np.maximum(n, 1).astype(np.float32)
    large = max_exact + (np.log(nf / max_exact) / math.log(128 / max_exact) * (nb - max_exact)).astype(np.int32)
    large = np.minimum(large, nb - 1)
    return ret + np.where(n < max_exact, n, large)


def _consts():
    ident = np.eye(128, dtype=np.float32)
    jmat = np.ascontiguousarray(ident[::-1, :])
    rel = np.arange(512) - 256
    bucket = _t5_bucket(rel)
    oh = np.zeros((33, 512), dtype=np.float32)
    inband = np.abs(rel) <= 128
    oh[bucket[inband], np.arange(512)[inband]] = 1.0
    oh[32, ~inband] = 1.0
    return ident, jmat, oh


_CACHE = {}


def kernel(x_prompt, x_sample, mem_prompt, mem_sample, rel_bias, w_in, conv_w, conv_b,
           conv_ln_g, conv_ln_b, sink, w_out, ln1_g, ln1_b, xq_w, xk_w, xv_w, xo_w,
           ln2_g, ln2_b, w_gate, w_up, w_down, ln3_g, ln3_b):
    f = lambda a: np.ascontiguousarray(np.asarray(a, dtype=np.float32))
    x_prompt, x_sample, mem_prompt, mem_sample = f(x_prompt), f(x_sample), f(mem_prompt), f(mem_sample)
    B, S, _ = x_prompt.shape
    DB, DS, _ = x_sample.shape
    assert DB == 1
    n_pc = B
    n_sc = N_CORES - n_pc
    tok = S
    assert DS == n_sc * tok and tok % T == 0
    NT = tok // T
    DFF = w_gate.shape[-1]
    key = (NT, DFF)
    if key not in _CACHE:
        _CACHE[key] = build_program(NT, DFF)
    nc = _CACHE[key]

    ident, jmat, oh = _consts()
    cpar = f(np.concatenate([conv_w[0], conv_b[0][None], conv_ln_g[0][None], conv_ln_b[0][None]], axis=0))
    lnp = f(np.stack([ln1_g[0], ln1_b[0], ln2_g[0], ln2_b[0], ln3_g[0], ln3_b[0]], axis=0))
    relb = f(np.concatenate([np.asarray(rel_bias, np.float32), np.full((1, NH), NEG, np.float32)], axis=0))
    shared = dict(w_in=f(w_in[0]), w_out=f(w_out[0]), xq_w=f(xq_w[0]), xk_w=f(xk_w[0]), xv_w=f(xv_w[0]),
                  xo_w=f(xo_w[0]), w_gate=f(w_gate[0]), w_up=f(w_up[0]), w_down=f(w_down[0]),
                  cpar=cpar, lnp=lnp, relb=relb, sink=f(np.asarray(sink, np.float32).reshape(1, NH)),
                  ident=ident, jmat=jmat, oh=oh)
    NB = NT * 4 + 2
    in_maps = []
    for c in range(N_CORES):
        xp = np.zeros((tok + 2 * HALO, D), np.float32)
        valid = np.zeros((tok + 2 * HALO,), bool)
        if c < n_pc:
            xp[HALO:HALO + tok] = x_prompt[c]
            valid[HALO:HALO + tok] = True
            mem = mem_prompt[c]
        else:
            j = c - n_pc
            lo = j * tok - HALO
            hi = (j + 1) * tok + HALO
            slo, shi = max(lo, 0), min(hi, DS)
            xp[slo - lo:shi - lo] = x_sample[0, slo:shi]
            valid[slo - lo:shi - lo] = True
            mem = mem_sample[0]
        kb = np.where(valid, 0.0, NEG).astype(np.float32).reshape(NB, 128).T
        m = dict(shared)
        m.update(xpad=xp, mem=f(mem), kbias=np.ascontiguousarray(kb))
        in_maps.append(m)
    res = run_bass_kernel_spmd(nc, in_maps, core_ids=list(range(N_CORES)))
    if DEBUG:
        global _DBG
        _DBG = res.results
    outs = [np.asarray(r["y"], dtype=np.float32) for r in res.results]
    y_prompt = np.stack(outs[:n_pc], axis=0)
    y_sample = np.concatenate(outs[n_pc:], axis=0)[None]
    return (y_prompt, y_sample)
```

```python
import math
from contextlib import ExitStack

import numpy as np
import concourse.bass as bass
import concourse.mybir as mybir
from concourse.bass_utils import run_bass_kernel_spmd

F32 = mybir.dt.float32
BF16 = mybir.dt.bfloat16
ALU = mybir.AluOpType
AF = mybir.ActivationFunctionType

D = 4096
KC = D // 128
CW = 2048
NCH = CW // 128
NH = 16
NKV = 4
XH = 4
MEM = 256
CONV_K = 31
T = 512
HALO = 128
TW = T + 2 * HALO
ALPHA = 2.0 ** 0.25
LN_EPS = 1e-5
SCALE = 128.0 ** -0.5
NEG = -30000.0
N_CORES = 8


class Res:
    __slots__ = ("w", "r")

    def __init__(self, seed=None):
        self.w = None
        self.r = dict(seed) if seed else {}


class Area:
    def __init__(self):
        self.cur = []

    def switch(self):
        seed = {}
        for res in self.cur:
            if res.w is not None:
                seed[("w", res.w)] = res.w
            for k, v in res.r.items():
                if k in seed:
                    seed[k] = max(seed[k], v)
                else:
                    seed[k] = v
        self.cur = []
        self.seed = seed
        return seed

    def res(self, extra_seed=None):
        s = dict(getattr(self, "seed", {}))
        if extra_seed:
            for k, v in extra_seed.items():
                s[k] = max(s.get(k, -1), v)
        r = Res(s)
        self.cur.append(r)
        return r

    def adopt(self, r):
        self.cur.append(r)


class Prog:
    def __init__(self):
        self.ins = []

    def op(self, eng, fn, reads=(), writes=(), lane=None):
        idx = len(self.ins)
        deps = set()
        for r in reads:
            if r.w is not None:
                deps.add(r.w)
        for w in writes:
            if w.w is not None:
                deps.add(w.w)
            deps.update(w.r.values())
        key = eng if lane is None else ("dma", idx)
        for r in reads:
            r.r[key] = idx
        for w in writes:
            w.w = idx
            w.r = {}
        self.ins.append([eng, fn, deps, lane, False, 0])
        return idx

    def emit(self, nc, es, final_wait_lanes):
        ins = self.ins
        engs = ("tensor", "vector", "scalar", "sync", "gpsimd")
        for rec in ins:
            for d in rec[2]:
                p = ins[d]
                if p[3] is None and not (p[0] == "tensor" and rec[0] == "tensor"):
                    p[4] = True
        cnt = {}
        lanes = {}
        for rec in ins:
            if rec[3] is not None:
                lanes[rec[3]] = lanes.get(rec[3], 0) + 16
                rec[5] = lanes[rec[3]]
            elif rec[4]:
                cnt[rec[0]] = cnt.get(rec[0], 0) + 1
                rec[5] = cnt[rec[0]]
        sem_e = {e: es.enter_context(nc.semaphore("se_" + e)) for e in ("tensor", "vector", "scalar", "gpsimd")}
        sem_l = {l: es.enter_context(nc.semaphore("sl_" + l)) for l in lanes}
        streams = {e: [] for e in engs}
        for i, rec in enumerate(ins):
            streams[rec[0]].append(i)
        block = es.enter_context(nc.Block())

        def run(e, name):
            waited = {}
            for i in streams[name]:
                rec = ins[i]
                need = {}
                for d in rec[2]:
                    p = ins[d]
                    if p[3] is not None:
                        k = ("l", p[3])
                    else:
                        if p[0] == "tensor" and name == "tensor":
                            continue
                        k = ("e", p[0])
                    if p[5] > need.get(k, 0):
                        need[k] = p[5]
                for k, v in need.items():
                    if waited.get(k, 0) >= v:
                        continue
                    waited[k] = v
                    e.wait_ge(sem_l[k[1]] if k[0] == "l" else sem_e[k[1]], v)
                inst = rec[1](e)
                if rec[3] is not None:
                    inst.then_inc(sem_l[rec[3]], 16)
                elif rec[4]:
                    inst.then_inc(sem_e[name], 1)
            if name == "sync":
                for l in final_wait_lanes:
                    if l in lanes:
                        e.wait_ge(sem_l[l], lanes[l])

        @block.tensor
        def _(e):
            run(e, "tensor")

        @block.vector
        def _(e):
            run(e, "vector")

        @block.scalar
        def _(e):
            run(e, "scalar")

        @block.gpsimd
        def _(e):
            run(e, "gpsimd")

        @block.sync
        def _(e):
            run(e, "sync")


DEBUG = False


def build_program(NT, DFF):
    NB = NT * 4 + 2
    NTOK = NT * T
    IN_COLS = 7168
    assert DFF % 256 == 0
    NFC = DFF // 128
    nc = bass.Bass("TRN2", target_bir_lowering=False)

    def din(name, shape):
        return nc.dram_tensor(name, list(shape), F32, kind="ExternalInput").ap()

    xpad = din("xpad", [NTOK + 2 * HALO, D])
    mem = din("mem", [MEM, D])
    kbias_d = din("kbias", [128, NB])
    w_in = din("w_in", [D, IN_COLS])
    w_out = din("w_out", [D, D])
    xq_w = din("xq_w", [D, 512])
    xk_w = din("xk_w", [D, 512])
    xv_w = din("xv_w", [D, 512])
    xo_w = din("xo_w", [512, D])
    w_gate = din("w_gate", [D, DFF])
    w_up = din("w_up", [D, DFF])
    w_down = din("w_down", [DFF, D])
    cpar_d = din("cpar", [34, CW])
    lnp = din("lnp", [6, D])
    relb_d = din("relb", [33, NH])
    sink_d = din("sink", [1, NH])
    ident_d = din("ident", [128, 128])
    jmat_d = din("jmat", [128, 128])
    oh_d = din("oh", [33, 512])
    y = nc.dram_tensor("y", [NTOK, D], F32, kind="ExternalOutput").ap()
    gtab_t = nc.dram_tensor("gtab", [16, 512], F32, kind="ExternalOutput" if DEBUG else "Internal")
    gtab_d = gtab_t.ap()
    bd_d = nc.dram_tensor("bd", [3, 128, 2048], F32, kind="ExternalOutput" if DEBUG else "Internal").ap()

    if DEBUG:
        dbg_mix = nc.dram_tensor("dbg_mix", [128, 16384], BF16, kind="ExternalOutput").ap()
        dbg_x1 = nc.dram_tensor("dbg_x1", [512, D], F32, kind="ExternalOutput").ap()
        dbg_x2 = nc.dram_tensor("dbg_x2", [512, D], F32, kind="ExternalOutput").ap()
        dbg_r0 = nc.dram_tensor("dbg_r0", [512, D], F32, kind="ExternalOutput").ap()
        dbg_R = nc.dram_tensor("dbg_R", [128, 16384], F32, kind="ExternalOutput").ap()
    P = Prog()
    es = ExitStack()
    E = es.enter_context

    Rt = E(nc.sbuf_tensor("Rt", [128, 16384], F32))
    Xt = E(nc.sbuf_tensor("Xt", [128, 24576], BF16))
    Mt = E(nc.sbuf_tensor("Mt", [128, 16384], BF16))
    slabs = [E(nc.sbuf_tensor(f"slab{i}", [128, 8192], BF16)) for i in range(3)]
    T4 = E(nc.sbuf_tensor("T4", [128, 1024], F32))
    ident = E(nc.sbuf_tensor("ident_s", [128, 128], F32))
    jmat = E(nc.sbuf_tensor("jmat_s", [128, 128], F32))
    ones_f = E(nc.sbuf_tensor("ones_f", [128, 128], F32))
    ones_b = E(nc.sbuf_tensor("ones_b", [128, 128], BF16))
    eps_t = E(nc.sbuf_tensor("eps_t", [128, 1], F32))
    cw = E(nc.sbuf_tensor("cw", [128, NCH, 34], F32))
    esk = E(nc.sbuf_tensor("esk", [128, NH], F32))
    kbias = E(nc.sbuf_tensor("kbias_s", [128, NB], F32))
    KxT = E(nc.sbuf_tensor("KxT", [128, XH, MEM], BF16))
    Vx = E(nc.sbuf_tensor("Vx", [128, 2, 512], BF16))
    stats = E(nc.sbuf_tensor("stats", [128, 4, 8, 6], F32))
    mv = E(nc.sbuf_tensor("mv", [128, 4, 2], F32))
    rstd = E(nc.sbuf_tensor("rstd", [128, 4], F32))
    nmr = E(nc.sbuf_tensor("nmr", [128, 4], F32))
    carry = E(nc.sbuf_tensor("carry", [128, NCH, 32], F32))
    gcol = E(nc.sbuf_tensor("gcol", [128, 128], F32))
    carryR = [Res() for _ in range(NCH)]
    banks = [E(nc.psum_tensor(f"bank{i}", [128, 512], F32)) for i in range(8)]

    Xf = Xt[:].bitcast(F32)
    Mf = Mt[:].bitcast(F32)
    Rb = Rt[:].bitcast(BF16)

    bankR = [Res() for _ in range(8)]
    slabR = [Res() for _ in range(3)]
    constR = Res()
    aR, aXA, aXB, aM, aT4 = Area(), Area(), Area(), Area(), Area()
    for a in (aR, aXA, aXB, aM, aT4):
        a.switch()

    st = {"bank": 0, "slab": 0, "ev": 0}

    st["nring"] = 6

    def ring():
        b = st["bank"] % st["nring"]
        st["bank"] = (b + 1) % st["nring"]
        return b

    def load_slab(src, shape3):
        s = st["slab"]
        st["slab"] = (s + 1) % 3
        a, b = shape3
        view = slabs[s][:, 0:a * b].rearrange("p (a b) -> p a b", a=a)
        P.op("gpsimd", lambda e, v=view, sr=src: e.dma_start(out=v, in_=sr), writes=[slabR[s]], lane=f"slab{s}")
        return view, slabR[s]

    def fm_slab(W, c0, ncols=256):
        return load_slab(W[:, c0:c0 + ncols].rearrange("(k p) c -> p k c", p=128), (KC, ncols))

    def mm(out, lhsT, rhs, start, stop, reads, writes):
        P.op("tensor", lambda e: e.matmul(out, lhsT, rhs, start=start, stop=stop), reads, writes)

    def tr(out, in_, idn, reads, writes):
        P.op("tensor", lambda e: e.transpose(out, in_, idn), reads, writes)

    def act(out, in_, func, reads, writes, bias=None, scale=None):
        kw = {}
        if bias is not None:
            kw["bias"] = bias
        if scale is not None:
            kw["scale"] = scale
        P.op("scalar", lambda e: e.activation(out, in_, func, **kw), reads, writes)

    def evac(out, in_, reads, writes):
        st["ev"] ^= 1
        if st["ev"]:
            P.op("scalar", lambda e: e.activation(out, in_, AF.Copy), reads, writes)
        else:
            P.op("vector", lambda e: e.tensor_copy(out=out, in_=in_), reads, writes)

    def evac_act(out, in_, reads, writes):
        P.op("scalar", lambda e: e.activation(out, in_, AF.Copy), reads, writes)

    def vtt(out, in0, in1, op, reads, writes):
        P.op("vector", lambda e: e.tensor_tensor(out=out, in0=in0, in1=in1, op=op), reads, writes)

    def vts(out, in0, s1, s2, op0, op1, reads, writes):
        if op1 is None:
            P.op("vector", lambda e: e.tensor_scalar(out=out, in0=in0, scalar1=s1, scalar2=None, op0=op0), reads, writes)
        else:
            P.op("vector", lambda e: e.tensor_scalar(out=out, in0=in0, scalar1=s1, scalar2=s2, op0=op0, op1=op1), reads, writes)

    def vstt(out, in0, scalar, in1, op0, op1, reads, writes):
        P.op("vector", lambda e: e.scalar_tensor_tensor(out=out, in0=in0, scalar=scalar, in1=in1, op0=op0, op1=op1), reads, writes)

    def sdma(out, in_, reads, writes, lane):
        P.op("sync", lambda e: e.dma_start(out=out, in_=in_), reads, writes, lane=lane)

    sdma(ident[:], ident_d, [], [constR], "c0")
    sdma(jmat[:], jmat_d, [], [constR], "c1")
    sdma(kbias[:], kbias_d, [], [constR], "c2")
    P.op("vector", lambda e: e.memset(ones_f[:], 1.0), [], [constR])
    P.op("vector", lambda e: e.memset(ones_b[:], 1.0), [], [constR])
    P.op("vector", lambda e: e.memset(eps_t[:], LN_EPS), [], [constR])

    cparR = aM.res()
    cpar_s = Mf[0:34, 0:CW]
    sdma(cpar_s, cpar_d, [], [cparR], "c3")
    cwR = Res()
    for c in range(NCH):
        b = ring()
        tr(banks[b][:, 0:34], Mf[0:34, c * 128:(c + 1) * 128], ident[0:34, 0:34], [cparR, constR], [bankR[b]])
        evac(cw[:, c, :], banks[b][:, 0:34], [bankR[b]], [cwR])

    eskR = Res()
    sdma(esk[:], sink_d[0:1, :].broadcast_to([128, NH]), [], [eskR], "c4")
    act(esk[:], esk[:], AF.Exp, [eskR], [eskR])

    relbR = aM.res()
    relb_s = Mf[0:33, 4096:4096 + NH]
    oh_s = Mf[0:33, 4608:4608 + 512]
    sdma(relb_s, relb_d, [], [relbR], "c5")
    sdma(oh_s, oh_d, [], [relbR], "c6")
    b = ring()
    mm(banks[b][0:16, :], relb_s, oh_s, True, True, [relbR], [bankR[b]])
    gsbR = aM.res()
    gsb = Mf[0:16, 5120:5120 + 512]
    evac(gsb, banks[b][0:16, :], [bankR[b]], [gsbR])
    gtabR = Res()
    sdma(gtab_d, gsb, [gsbR], [gtabR], "c7")
    BdR = [Res() for _ in range(3)]

    def build_bias_tables():
      for o in (-1, 0, 1):
        hkR = aR.res()
        hk = Rt[:, (o + 1) * 2048:(o + 2) * 2048].rearrange("p (h k) -> p h k", h=NH)
        src = bass.AP(tensor=gtab_t, offset=128 * o + 129, ap=[[1, 128], [512, NH], [1, 128]])
        sdma(hk, src, [gtabR], [hkR], f"hk{o + 1}")
        btR = aR.res()
        bt = Rt[:, 6144 + (o + 1) * 2048:6144 + (o + 2) * 2048]
        for hq in range(4):
            b = ring()
            for hh in range(4):
                mm(banks[b][:, hh * 128:(hh + 1) * 128], hk[:, hq * 4 + hh, :], jmat[:], True, True,
                   [hkR, constR], [bankR[b]])
            evac(bt[:, hq * 512:(hq + 1) * 512], banks[b][:], [bankR[b]], [btR])
        sdma(bd_d[o + 1], bt, [btR], [BdR[o + 1]], f"bdw{o + 1}")

    gcolR = Res()
    for li in range(2):
        lncR = aM.res()
        lnc_s = Mf[0:64, 6144 + li * 128:6144 + (li + 1) * 128]
        sdma(lnc_s, lnp[2 * li:2 * li + 2, :].rearrange("a (k p) -> (a k) p", p=128), [], [lncR], f"c{8 + li}")
        b = ring()
        tr(banks[b][:, 0:64], lnc_s, ident[0:64, 0:64], [lncR, constR], [bankR[b]])
        evac(gcol[:, li * 64:(li + 1) * 64], banks[b][:, 0:64], [bankR[b]], [gcolR])

    aM.switch()
    stgR = [aM.res(), aM.res()]
    stg = [Mf[:, 0:4096], Mf[:, 4096:8192]]
    aXA.switch()
    memTR = aXA.res()
    memT = Xt[:, 0:KC * MEM].rearrange("p (k m) -> p k m", k=KC)
    for mb in range(2):
        sdma(stg[mb], mem[mb * 128:(mb + 1) * 128, :], [], [stgR[mb]], f"stg{mb}")
        for kq in range(8):
            b = ring()
            for kk in range(4):
                k = kq * 4 + kk
                tr(banks[b][:, kk * 128:(kk + 1) * 128], stg[mb][:, k * 128:(k + 1) * 128], ident[:],
                   [stgR[mb], constR], [bankR[b]])
            evac(memT[:, kq * 4:kq * 4 + 4, mb * 128:(mb + 1) * 128],
                 banks[b][:].rearrange("p (a b) -> p a b", a=4), [bankR[b]], [memTR])
    KxR = Res()
    VxR = Res()
    for p in range(2):
        sl, slR = fm_slab(xk_w, p * 256)
        for j in range(2):
            h = 2 * p + j
            b = ring()
            for k in range(KC):
                mm(banks[b][:, 0:MEM], sl[:, k, j * 128:(j + 1) * 128], memT[:, k, :], k == 0, k == KC - 1,
                   [slR, memTR], [bankR[b]])
            evac(KxT[:, h, :], banks[b][:, 0:MEM], [bankR[b]], [KxR])
    for p in range(2):
        sl, slR = fm_slab(xv_w, p * 256)
        for mb in range(2):
            b = ring()
            for k in range(KC):
                mm(banks[b][:, 0:256], memT[:, k, mb * 128:(mb + 1) * 128], sl[:, k, :], k == 0, k == KC - 1,
                   [slR, memTR], [bankR[b]])
            evac(Vx[:, mb, p * 256:(p + 1) * 256], banks[b][:, 0:256], [bankR[b]], [VxR])

    smallR = [Res() for _ in range(4)]

    def ln_stats(resid, residR, t, n):
        P.op("vector", lambda e, t=t, n=n: e.bn_stats(out=stats[:, t, n, :], in_=resid[t][:, n * 512:(n + 1) * 512]),
             [residR[t][n]], [smallR[t]])

    def ln_tail():
        for t in range(4):
            P.op("vector", lambda e, t=t: e.bn_aggr(out=mv[:, t, :], in_=stats[:, t].rearrange("p a b -> p (a b)")),
                 [smallR[t]], [smallR[t]])
        act(rstd[:, 0:4], mv[:, :, 1], AF.Ln, smallR + [constR], smallR, bias=eps_t[:, 0:1])
        act(rstd[:, 0:4], rstd[:, 0:4], AF.Exp, smallR, smallR, scale=-0.5)
        vstt(nmr[:, 0:4], mv[:, :, 0], -1.0, rstd[:, 0:4], ALU.mult, ALU.mult, smallR, smallR)

    def layer_norm(resid, residR, li, gbR, gb):
        ln_tail()
        for cg in range(4):
            bf = cg % 2
            c0 = cg * 1024
            sdma(gb[bf][0], lnp[2 * li:2 * li + 1, c0:c0 + 1024].broadcast_to([128, 1024]), [], [gbR[bf][0]], f"gbg{bf}")
            sdma(gb[bf][1], lnp[2 * li + 1:2 * li + 2, c0:c0 + 1024].broadcast_to([128, 1024]), [], [gbR[bf][1]], f"gbb{bf}")
            for t in range(4):
                rr = [residR[t][2 * cg], residR[t][2 * cg + 1]]
                sl_ = resid[t][:, c0:c0 + 1024]
                act(sl_, sl_, AF.Identity, rr + [smallR[t]], rr, bias=nmr[:, t:t + 1], scale=rstd[:, t:t + 1])
                vtt(sl_, sl_, gb[bf][0], ALU.mult, rr + [gbR[bf][0]], rr)
                vtt(sl_, sl_, gb[bf][1], ALU.add, rr + [gbR[bf][1]], rr)

    def ln_norm_T(resid, residR, li, dstT, dstR):
        ln_tail()
        for cg in range(4):
            c0 = cg * 1024
            for t in range(4):
                rr = [residR[t][2 * cg], residR[t][2 * cg + 1]]
                sl_ = resid[t][:, c0:c0 + 1024]
                act(sl_, sl_, AF.Identity, rr + [smallR[t]], rr, bias=nmr[:, t:t + 1], scale=rstd[:, t:t + 1])
        for k in range(KC):
            b = ring()
            for t in range(4):
                tr(banks[b][:, t * 128:(t + 1) * 128], resid[t][:, k * 128:(k + 1) * 128], ident[:],
                   [residR[t][k // 4], constR], [bankR[b]])
            gc = gcol[:, li * 64 + k:li * 64 + k + 1]
            bc = gcol[:, li * 64 + 32 + k:li * 64 + 32 + k + 1]
            vts(dstT[:, k, :], banks[b][:], gc, bc, ALU.mult, ALU.add, [bankR[b], gcolR], [dstR[k]])

    def ln_affine_items(resid, residR, li, gbR, gb):
        items = []
        for cg in range(4):
            bf = cg % 2
            c0 = cg * 1024

            def load(cg=cg, bf=bf, c0=c0):
                sdma(gb[bf][0], lnp[2 * li:2 * li + 1, c0:c0 + 1024].broadcast_to([128, 1024]), [], [gbR[bf][0]], f"gbg{bf}")
                sdma(gb[bf][1], lnp[2 * li + 1:2 * li + 2, c0:c0 + 1024].broadcast_to([128, 1024]), [], [gbR[bf][1]], f"gbb{bf}")
            for t in range(4):
                def item(cg=cg, bf=bf, c0=c0, t=t, load=load):
                    if t == 0:
                        load()
                    rr = [residR[t][2 * cg], residR[t][2 * cg + 1]]
                    sl_ = resid[t][:, c0:c0 + 1024]
                    vtt(sl_, sl_, gb[bf][0], ALU.mult, rr + [gbR[bf][0]], rr)
                    vtt(sl_, sl_, gb[bf][1], ALU.add, rr + [gbR[bf][1]], rr)
                items.append(item)
        return items

    resid = [Rt[:, t * 4096:(t + 1) * 4096] for t in range(4)]
    gb = [[Xf[:, 8192 + bf * 2048:8192 + bf * 2048 + 1024], Xf[:, 8192 + bf * 2048 + 1024:8192 + (bf + 1) * 2048]]
          for bf in range(2)]
    y_lanes = [f"y{t}" for t in range(4)]

    def emit_y(it_, residR_):
        for t in range(4):
            r0 = it_ * T + t * 128
            P.op("scalar", lambda e, o=y[r0:r0 + 128, :], i_=resid[t]: e.dma_start(out=o, in_=i_), residR_[t], [], lane=f"y{t}")

    pending_y = None
    pre_stgR = None
    for it in range(NT):
        tp0 = it * T
        if pre_stgR is None:
            aM.switch()
            stgR = [aM.res(), aM.res()]
            npre = 0
        else:
            stgR = pre_stgR
            npre = 2
        sa = aXA.switch()
        sb_ = aXB.switch()
        xTR = []
        mseed = dict(sa)
        for k_, v_ in sb_.items():
            mseed[k_] = max(mseed.get(k_, -1), v_)
        for blk in range(6):
            r = Res(mseed)
            aXA.adopt(r)
            aXB.adopt(r)
            xTR.append(r)
        xT = Xt[:, 0:KC * TW].rearrange("p (k t) -> p k t", k=KC)
        for blk in range(6):
            s = blk % 2
            if blk >= npre:
                sdma(stg[s], xpad[tp0 + blk * 128:tp0 + (blk + 1) * 128, :], [], [stgR[s]], f"stg{s}")
            for kq in range(8):
                b = ring()
                for kk in range(4):
                    k = kq * 4 + kk
                    tr(banks[b][:, kk * 128:(kk + 1) * 128], stg[s][:, k * 128:(k + 1) * 128], ident[:],
                       [stgR[s], constR], [bankR[b]])
                (evac_act if blk < npre else evac)(xT[:, kq * 4:kq * 4 + 4, blk * 128:(blk + 1) * 128],
                     banks[b][:].rearrange("p (a b) -> p a b", a=4), [bankR[b]], [xTR[blk]])

        if pending_y is not None:
            emit_y(*pending_y)
            pending_y = None
        if it == 0:
            build_bias_tables()

        aM.switch()
        sigR = [aM.res(), aM.res()]
        uR = [aM.res(), aM.res()]
        sqR = [aM.res() for _ in range(4)]
        sig = [Mf[:, i * 544:(i + 1) * 544] for i in range(2)]
        u = [Mf[:, 1088 + i * 544:1088 + (i + 1) * 544] for i in range(2)]
        sq = [Mf[:, 2176 + i * 512:2176 + (i + 1) * 512] for i in range(4)]
        aR.switch()
        hTR = [aR.res() for _ in range(NCH)]
        qTR = [aR.res() for _ in range(NH)]
        kTR = [aR.res() for _ in range(NKV)]
        VR = [aR.res() for _ in range(6)]
        spR = aR.res()
        hT = Rt[:, 0:8192].rearrange("p (c t) -> p c t", c=NCH)
        qT = Rb[:, 16384:24576].rearrange("p (h t) -> p h t", h=NH)
        kT = Rb[:, 24576:24576 + NKV * TW].rearrange("p (g t) -> p g t", g=NKV)
        Vt = Rb[:, 27648:27648 + 6 * 512].rearrange("p (b c) -> p b c", b=6)
        sp0 = Rt[:, 15360:15872]
        sp1 = Rt[:, 15872:16384]
        SUMB, SQB = 6, 7
        allx = xTR

        def conv_stats(pcs):
            for j in range(2):
                c = 2 * pcs + j
                s4 = c % 4
                act(sq[s4], hT[:, c, :], AF.Square, [hTR[c]], [sqR[s4]])
                mm(banks[SUMB][:], ones_f[:], hT[:, c, :], c == 0, c == NCH - 1, [constR, hTR[c]], [bankR[SUMB]])
                mm(banks[SQB][:], ones_f[:], sq[s4], c == 0, c == NCH - 1, [constR, sqR[s4]], [bankR[SQB]])

        def evac_a(out, in_, reads, writes):
            P.op("scalar", lambda e: e.activation(out, in_, AF.Copy), reads, writes)

        def q_proj(pq):
            sl, slR = fm_slab(w_in, 2 * CW + pq * 256)
            for j in range(2):
                h = 2 * pq + j
                b = ring()
                for k in range(KC):
                    mm(banks[b][:], sl[:, k, j * 128:(j + 1) * 128], xT[:, k, 128:640], k == 0, k == KC - 1,
                       [slR] + allx, [bankR[b]])
                evac_a(qT[:, h, :], banks[b][:], [bankR[b]], [qTR[h]])

        def k_proj(pk):
            sl, slR = fm_slab(w_in, 2 * CW + 2048 + pk * 256)
            for j in range(2):
                g = 2 * pk + j
                bA = ring()
                bB = ring()
                for k in range(KC):
                    mm(banks[bA][:], sl[:, k, j * 128:(j + 1) * 128], xT[:, k, 0:512], k == 0, k == KC - 1,
                       [slR] + allx, [bankR[bA]])
                for k in range(KC):
                    mm(banks[bB][:, 0:256], sl[:, k, j * 128:(j + 1) * 128], xT[:, k, 512:768], k == 0, k == KC - 1,
                       [slR] + allx, [bankR[bB]])
                evac_a(kT[:, g, 0:512], banks[bA][:], [bankR[bA]], [kTR[g]])
                evac_a(kT[:, g, 512:768], banks[bB][:, 0:256], [bankR[bB]], [kTR[g]])

        def v_proj(pv):
            sl, slR = fm_slab(w_in, 2 * CW + 2048 + 512 + pv * 256)
            for blk in range(6):
                b = ring()
                for k in range(KC):
                    mm(banks[b][:, 0:256], xT[:, k, blk * 128:(blk + 1) * 128], sl[:, k, :], k == 0, k == KC - 1,
                       [slR, xTR[blk]], [bankR[b]])
                evac_a(Vt[:, blk, pv * 256:(pv + 1) * 256], banks[b][:, 0:256], [bankR[b]], [VR[blk]])

        extras = {1: lambda: k_proj(0), 3: lambda: k_proj(1), 5: lambda: v_proj(0), 7: lambda: v_proj(1)}

        for pc in range(NCH // 2):
            slV, slVR = fm_slab(w_in, pc * 256)
            slG, slGR = fm_slab(w_in, CW + pc * 256)
            if it == 0:
                sbk = ring()
            for j in range(2):
                c = 2 * pc + j
                bv = ring()
                for k in range(KC):
                    mm(banks[bv][:], slV[:, k, j * 128:(j + 1) * 128], xT[:, k, 144:656], k == 0, k == KC - 1,
                       [slVR] + allx, [bankR[bv]])
                if it == 0:
                    for k in range(KC):
                        mm(banks[sbk][:, (2 * j) * 32:(2 * j + 1) * 32], slV[:, k, j * 128:(j + 1) * 128], xT[:, k, 112:144],
                           k == 0, k == KC - 1, [slVR] + allx, [bankR[sbk]])
                bg = ring()
                for k in range(KC):
                    mm(banks[bg][:], slG[:, k, j * 128:(j + 1) * 128], xT[:, k, 144:656], k == 0, k == KC - 1,
                       [slGR] + allx, [bankR[bg]])
                if it == 0:
                    for k in range(KC):
                        mm(banks[sbk][:, (2 * j + 1) * 32:(2 * j + 2) * 32], slG[:, k, j * 128:(j + 1) * 128], xT[:, k, 112:144],
                           k == 0, k == KC - 1, [slGR] + allx, [bankR[sbk]])
                act(sig[j][:, 32:544], banks[bg][:], AF.Sigmoid, [bankR[bg]], [sigR[j]])
                if it == 0:
                    act(sig[j][:, 0:32], banks[sbk][:, (2 * j + 1) * 32:(2 * j + 2) * 32], AF.Sigmoid, [bankR[sbk]], [sigR[j]])
                    vtt(u[j][:, 0:32], banks[sbk][:, (2 * j) * 32:(2 * j + 1) * 32], sig[j][:, 0:32], ALU.mult,
                        [bankR[sbk], sigR[j]], [uR[j]])
                else:
                    act(u[j][:, 0:32], carry[:, c, :], AF.Copy, [carryR[c]], [uR[j]])
                vtt(u[j][:, 32:544], banks[bv][:], sig[j][:, 32:544], ALU.mult, [bankR[bv], sigR[j]], [uR[j]])
            if it < NT - 1:
                for j in range(2):
                    c = 2 * pc + j
                    act(carry[:, c, :], u[j][:, 512:544], AF.Copy, [uR[j]], [carryR[c]])
            q_proj(pc)
            if pc in extras:
                extras[pc]()
            for tap in range(CONV_K):
                for j in range(2):
                    c = 2 * pc + j
                    if tap == 0:
                        vts(hT[:, c, :], u[j][:, 1:513], cw[:, c, 0:1], cw[:, c, 31:32], ALU.mult, ALU.add,
                            [uR[j], cwR], [hTR[c]])
                    else:
                        vstt(hT[:, c, :], u[j][:, tap + 1:tap + 513], cw[:, c, tap:tap + 1], hT[:, c, :], ALU.mult, ALU.add,
                             [uR[j], cwR, hTR[c]], [hTR[c]])
            if pc > 0:
                conv_stats(pc - 1)
        conv_stats(NCH // 2 - 1)

        aM.switch()
        mixR = [aM.res() for _ in range(32)]
        mixT = Mt[:, 0:32 * 512].rearrange("p (c t) -> p c t", c=32)
        inv = 1.0 / CW
        vts(sp0, banks[SUMB][:], inv, None, ALU.mult, None, [bankR[SUMB]], [spR])
        vtt(sp1, sp0, sp0, ALU.mult, [spR], [spR])
        vstt(sp1, banks[SQB][:], inv, sp1, ALU.mult, ALU.subtract, [bankR[SQB], spR], [spR])
        act(sp1, sp1, AF.Ln, [spR, constR], [spR], bias=eps_t[:, 0:1])
        act(sp1, sp1, AF.Exp, [spR], [spR], scale=-0.5)
        for c in range(NCH):
            vtt(hT[:, c, :], hT[:, c, :], sp0, ALU.subtract, [hTR[c], spR], [hTR[c]])
            vtt(hT[:, c, :], hT[:, c, :], sp1, ALU.mult, [hTR[c], spR], [hTR[c]])
            act(mixT[:, c, :], hT[:, c, :], AF.Silu, [hTR[c], cwR], [mixR[c]], bias=cw[:, c, 33:34], scale=cw[:, c, 32:33])

        aXA.switch()
        BtR = [aXA.res() for _ in range(3)]
        PR = [aXA.res() for _ in range(8)]
        Pt = [Xt[:, 12288 + i * 512:12288 + (i + 1) * 512] for i in range(8)]
        for o in range(3):
            sdma(Xf[:, o * 2048:(o + 1) * 2048], bd_d[o], [BdR[o]], [BtR[o]], f"B{o}")
        st["nring"] = 8
        sp_seed = dict(spR.r)
        if spR.w is not None:
            sp_seed[("w", spR.w)] = spR.w
        sp0R = aR.res(sp_seed)
        sp1R = aR.res(sp_seed)
        iters = [(qb, g) for qb in range(4) for g in range(NKV)]
        Sb = {}
        Pl = {}

        def att_S(i):
            qb, g = iters[i]
            sb3 = [ring(), ring(), ring()]
            for o in range(3):
                kb = qb + o
                for hh in range(4):
                    h = 4 * g + hh
                    mm(banks[sb3[o]][:, hh * 128:(hh + 1) * 128], kT[:, g, kb * 128:(kb + 1) * 128],
                       qT[:, h, qb * 128:(qb + 1) * 128], True, True, [kTR[g], qTR[h]], [bankR[sb3[o]]])
            Sb[i] = sb3

        def att_sm(i):
            qb, g = iters[i]
            pl = []
            for o in range(3):
                kb = qb + o
                bk = Sb[i][o]
                pp = (3 * i + o) % 8
                vstt(banks[bk][:], banks[bk][:], SCALE, Xf[:, o * 2048 + g * 512:o * 2048 + (g + 1) * 512],
                     ALU.mult, ALU.add, [bankR[bk], BtR[o]], [bankR[bk]])
                gb_col = it * 4 + kb
                act(Pt[pp], banks[bk][:], AF.Exp, [bankR[bk], constR], [PR[pp]], bias=kbias[:, gb_col:gb_col + 1])
                pl.append(pp)
            Pl[i] = pl

        Pv = {}

        def att_pv(i):
            qb, g = iters[i]
            pl = Pl[i]
            bo = ring()
            bdn = ring()
            for o in range(3):
                kb = qb + o
                mm(banks[bo][:], Vt[:, kb, g * 128:(g + 1) * 128], Pt[pl[o]], o == 0, o == 2,
                   [VR[kb], PR[pl[o]]], [bankR[bo]])
            for o in range(3):
                mm(banks[bdn][:], ones_b[:], Pt[pl[o]], o == 0, o == 2, [constR, PR[pl[o]]], [bankR[bdn]])
            Pv[i] = (bo, bdn)

        def att_norm(i):
            qb, g = iters[i]
            bo, bdn = Pv[i]
            spb, spbR = (sp0, sp0R) if i % 2 == 0 else (sp1, sp1R)
            for hh in range(4):
                h = 4 * g + hh
                act(spb[:, hh * 128:(hh + 1) * 128], banks[bdn][:, hh * 128:(hh + 1) * 128], AF.Ln,
                    [bankR[bdn], eskR], [spbR], bias=esk[:, h:h + 1])
            act(spb, spb, AF.Exp, [spbR], [spbR], scale=-1.0)

        def att_out(i):
            qb, g = iters[i]
            bo, bdn = Pv[i]
            spb, spbR = (sp0, sp0R) if i % 2 == 0 else (sp1, sp1R)
            vtt(mixT[:, NCH + 4 * g:NCH + 4 * g + 4, qb * 128:(qb + 1) * 128],
                banks[bo][:].rearrange("p (h q) -> p h q", h=4), spb.rearrange("p (h q) -> p h q", h=4), ALU.mult,
                [bankR[bo], spbR], [mixR[NCH + 4 * g + hh] for hh in range(4)])

        att_S(0)
        att_sm(0)
        att_S(1)
        att_sm(1)
        for i in range(len(iters)):
            att_pv(i)
            if i + 2 < len(iters):
                att_S(i + 2)
            att_norm(i)
            if i + 2 < len(iters):
                att_sm(i + 2)
            att_out(i)
        st["nring"] = 6

        if DEBUG and it == 0:
            sdma(dbg_R, Rt[:], hTR + qTR + kTR + VR + [spR], [], "dbgR")
        aR.switch()
        residR = [[aR.res() for _ in range(8)] for _ in range(4)]
        for t in range(4):
            r0 = tp0 + HALO + t * 128
            sdma(resid[t], xpad[r0:r0 + 128, :], [], residR[t], f"res{t}")
        aXB.switch()
        gbR = [[aXB.res(), aXB.res()] for _ in range(2)]
        for n in range(8):
            bb = [ring() for _ in range(4)]
            for s in range(2):
                sl, slR = load_slab(w_out[s * 2048:(s + 1) * 2048, n * 512:(n + 1) * 512].rearrange("(k p) c -> p k c", p=128),
                                    (16, 512))
                for t in range(4):
                    for kk in range(16):
                        c = s * 16 + kk
                        mm(banks[bb[t]][:], mixT[:, c, t * 128:(t + 1) * 128], sl[:, kk, :], s == 0 and kk == 0,
                           s == 1 and kk == 15, [slR, mixR[c]], [bankR[bb[t]]])
            for t in range(4):
                rs = resid[t][:, n * 512:(n + 1) * 512]
                vstt(rs, rs, ALPHA, banks[bb[t]][:], ALU.mult, ALU.add, [bankR[bb[t]], residR[t][n]], [residR[t][n]])
                ln_stats(resid, residR, t, n)
        if DEBUG and it == 0:
            for t in range(4):
                sdma(dbg_r0[t * 128:(t + 1) * 128, :], resid[t], residR[t], [], f"dbg{t}")
            sdma(dbg_mix, Mt[:], mixR, [], "dbgm")
        aM.switch()
        x1TR = [aM.res() for _ in range(KC)]
        x1T = Mt[:, 0:KC * 512].rearrange("p (k t) -> p k t", k=KC)
        ln_norm_T(resid, residR, 0, x1T, x1TR)
        aff1 = ln_affine_items(resid, residR, 0, gbR, gb)

        aXA.switch()
        qxR = [aXA.res() for _ in range(XH)]
        PxR = [aXA.res() for _ in range(4)]
        oxR = [aXA.res() for _ in range(XH)]
        rdxR = [aXA.res() for _ in range(2)]
        qxT = Xt[:, 0:2048].rearrange("p (h t) -> p h t", h=XH)
        Px = [Xt[:, 2048 + i * 512:2048 + (i + 1) * 512] for i in range(4)]
        oxT = Xt[:, 4096:6144].rearrange("p (h t) -> p h t", h=XH)
        rdx = [Xf[:, 3072 + i * 512:3072 + (i + 1) * 512] for i in range(2)]
        for p in range(2):
            sl, slR = fm_slab(xq_w, p * 256)
            for j in range(2):
                h = 2 * p + j
                b = ring()
                for k in range(KC):
                    mm(banks[b][:], sl[:, k, j * 128:(j + 1) * 128], x1T[:, k, :], k == 0, k == KC - 1,
                       [slR, x1TR[k]], [bankR[b]])
                P.op("scalar", lambda e, o=qxT[:, h, :], i_=banks[b][:]: e.activation(o, i_, AF.Copy), [bankR[b]], [qxR[h]])
        for item in aff1:
            item()
        xpidx = {}

        def x_scores(h):
            pidx = []
            for mb in range(2):
                b = ring()
                mm(banks[b][:], KxT[:, h, mb * 128:(mb + 1) * 128], qxT[:, h, :], True, True, [KxR, qxR[h]], [bankR[b]])
                pp = (2 * h + mb) % 4
                act(Px[pp], banks[b][:], AF.Exp, [bankR[b]], [PxR[pp]], scale=SCALE)
                pidx.append(pp)
            xpidx[h] = pidx

        x_scores(0)
        for h in range(XH):
            if h + 1 < XH:
                x_scores(h + 1)
            pidx = xpidx[h]
            bo = ring()
            bdn = ring()
            for mb in range(2):
                mm(banks[bo][:], Vx[:, mb, h * 128:(h + 1) * 128], Px[pidx[mb]], mb == 0, mb == 1,
                   [VxR, PxR[pidx[mb]]], [bankR[bo]])
            for mb in range(2):
                mm(banks[bdn][:], ones_b[:], Px[pidx[mb]], mb == 0, mb == 1, [constR, PxR[pidx[mb]]], [bankR[bdn]])
            rr = h % 2
            act(rdx[rr], banks[bdn][:], AF.Ln, [bankR[bdn]], [rdxR[rr]])
            act(rdx[rr], rdx[rr], AF.Exp, [rdxR[rr]], [rdxR[rr]], scale=-1.0)
            vtt(oxT[:, h, :], banks[bo][:], rdx[rr], ALU.mult, [bankR[bo], rdxR[rr]], [oxR[h]])
        for s in range(2):
            sl, slR = load_slab(xo_w[:, s * 2048:(s + 1) * 2048].rearrange("(k p) c -> p k c", p=128), (4, 2048))
            for nn in range(4):
                n = s * 4 + nn
                bb = [ring() for _ in range(4)]
                for t in range(4):
                    for k in range(4):
                        mm(banks[bb[t]][:], oxT[:, k, t * 128:(t + 1) * 128], sl[:, k, nn * 512:(nn + 1) * 512], k == 0, k == 3,
                           [slR, oxR[k]], [bankR[bb[t]]])
                for t in range(4):
                    rs = resid[t][:, n * 512:(n + 1) * 512]
                    vstt(rs, rs, ALPHA, banks[bb[t]][:], ALU.mult, ALU.add, [bankR[bb[t]], residR[t][n]], [residR[t][n]])
                    ln_stats(resid, residR, t, n)
        aXA.switch()
        x2TR = [aXA.res() for _ in range(KC)]
        x2T = Xt[:, 0:KC * 512].rearrange("p (k t) -> p k t", k=KC)
        ln_norm_T(resid, residR, 1, x2T, x2TR)
        aff2 = ln_affine_items(resid, residR, 1, gbR, gb)

        aM.switch()
        hgR = [[aM.res() for _ in range(16)] for _ in range(2)]
        hg = [Mt[:, s * 8192:(s + 1) * 8192].rearrange("p (c t) -> p c t", c=16) for s in range(2)]
        aT4.switch()
        sgR = [aT4.res(), aT4.res()]
        sg = [T4[:, 0:512], T4[:, 512:1024]]
        ngroups = (NFC + 15) // 16
        si = 0
        for gi in range(ngroups):
            c_lo = gi * 16
            G = min(16, NFC - c_lo)
            slot = gi % 2
            for pc in range(G // 2):
                slG, slGR = fm_slab(w_gate, (c_lo + 2 * pc) * 128)
                slU, slUR = fm_slab(w_up, (c_lo + 2 * pc) * 128)
                sgs = []
                for j in range(2):
                    bg = ring()
                    for k in range(KC):
                        mm(banks[bg][:], slG[:, k, j * 128:(j + 1) * 128], x2T[:, k, :], k == 0, k == KC - 1,
                           [slGR, x2TR[k]], [bankR[bg]])
                    s2 = si % 2
                    si += 1
                    act(sg[s2], banks[bg][:], AF.Silu, [bankR[bg]], [sgR[s2]])
                    sgs.append(s2)
                for j in range(2):
                    cc = 2 * pc + j
                    bu = ring()
                    for k in range(KC):
                        mm(banks[bu][:], slU[:, k, j * 128:(j + 1) * 128], x2T[:, k, :], k == 0, k == KC - 1,
                           [slUR, x2TR[k]], [bankR[bu]])
                    s2 = sgs[j]
                    vtt(hg[slot][:, cc, :], banks[bu][:], sg[s2], ALU.mult, [bankR[bu], sgR[s2]], [hgR[slot][cc]])
                for _ in range(2):
                    if aff2:
                        aff2.pop(0)()
            while gi == 0 and aff2:
                aff2.pop(0)()
            for n in range(8):
                sl, slR = load_slab(w_down[c_lo * 128:(c_lo + G) * 128, n * 512:(n + 1) * 512].rearrange("(k p) c -> p k c", p=128),
                                    (G, 512))
                bb = [ring() for _ in range(4)]
                for t in range(4):
                    for kk in range(G):
                        mm(banks[bb[t]][:], hg[slot][:, kk, t * 128:(t + 1) * 128], sl[:, kk, :], kk == 0, kk == G - 1,
                           [slR, hgR[slot][kk]], [bankR[bb[t]]])
                for t in range(4):
                    rs = resid[t][:, n * 512:(n + 1) * 512]
                    if gi == 0:
                        vstt(rs, rs, ALPHA, banks[bb[t]][:], ALU.mult, ALU.add, [bankR[bb[t]], residR[t][n]], [residR[t][n]])
                    else:
                        vtt(rs, rs, banks[bb[t]][:], ALU.add, [bankR[bb[t]], residR[t][n]], [residR[t][n]])
                    if gi == ngroups - 1:
                        ln_stats(resid, residR, t, n)
        if it < NT - 1:
            aM.switch()
            pre_stgR = [aM.res(), aM.res()]
            for blk in range(2):
                P.op("gpsimd", lambda e, o=stg[blk], i_=xpad[tp0 + T + blk * 128:tp0 + T + (blk + 1) * 128, :]:
                     e.dma_start(out=o, in_=i_), [], [pre_stgR[blk]], lane=f"stg{blk}")
        layer_norm(resid, residR, 2, gbR, gb)
        if it < NT - 1:
            pending_y = (it, residR)
        else:
            emit_y(it, residR)

    P.emit(nc, es, y_lanes)
    es.close()
    return nc


def _t5_bucket(rel):
    nb = 16
    max_exact = 8
    ret = np.where(rel > 0, nb, 0)
    n = np.abs(rel)
    nf = np.maximum(n, 1).astype(np.float32)
    large = max_exact + (np.log(nf / max_exact) / math.log(128 / max_exact) * (nb - max_exact)).astype(np.int32)
    large = np.minimum(large, nb - 1)
    return ret + np.where(n < max_exact, n, large)


def _consts():
    ident = np.eye(128, dtype=np.float32)
    jmat = np.ascontiguousarray(ident[::-1, :])
    rel = np.arange(512) - 256
    bucket = _t5_bucket(rel)
    oh = np.zeros((33, 512), dtype=np.float32)
    inband = np.abs(rel) <= 128
    oh[bucket[inband], np.arange(512)[inband]] = 1.0
    oh[32, ~inband] = 1.0
    return ident, jmat, oh


_CACHE = {}


def kernel(x_prompt, x_sample, mem_prompt, mem_sample, rel_bias, w_in, conv_w, conv_b,
           conv_ln_g, conv_ln_b, sink, w_out, ln1_g, ln1_b, xq_w, xk_w, xv_w, xo_w,
           ln2_g, ln2_b, w_gate, w_up, w_down, ln3_g, ln3_b):
    f = lambda a: np.ascontiguousarray(np.asarray(a, dtype=np.float32))
    x_prompt, x_sample, mem_prompt, mem_sample = f(x_prompt), f(x_sample), f(mem_prompt), f(mem_sample)
    B, S, _ = x_prompt.shape
    DB, DS, _ = x_sample.shape
    assert DB == 1
    n_pc = B
    n_sc = N_CORES - n_pc
    tok = S
    assert DS == n_sc * tok and tok % T == 0
    NT = tok // T
    DFF = w_gate.shape[-1]
    key = (NT, DFF)
    if key not in _CACHE:
        _CACHE[key] = build_program(NT, DFF)
    nc = _CACHE[key]

    ident, jmat, oh = _consts()
    cpar = f(np.concatenate([conv_w[0], conv_b[0][None], conv_ln_g[0][None], conv_ln_b[0][None]], axis=0))
    lnp = f(np.stack([ln1_g[0], ln1_b[0], ln2_g[0], ln2_b[0], ln3_g[0], ln3_b[0]], axis=0))
    relb = f(np.concatenate([np.asarray(rel_bias, np.float32), np.full((1, NH), NEG, np.float32)], axis=0))
    shared = dict(w_in=f(w_in[0]), w_out=f(w_out[0]), xq_w=f(xq_w[0]), xk_w=f(xk_w[0]), xv_w=f(xv_w[0]),
                  xo_w=f(xo_w[0]), w_gate=f(w_gate[0]), w_up=f(w_up[0]), w_down=f(w_down[0]),
                  cpar=cpar, lnp=lnp, relb=relb, sink=f(np.asarray(sink, np.float32).reshape(1, NH)),
                  ident=ident, jmat=jmat, oh=oh)
    NB = NT * 4 + 2
    in_maps = []
    for c in range(N_CORES):
        xp = np.zeros((tok + 2 * HALO, D), np.float32)
        valid = np.zeros((tok + 2 * HALO,), bool)
        if c < n_pc:
            xp[HALO:HALO + tok] = x_prompt[c]
            valid[HALO:HALO + tok] = True
            mem = mem_prompt[c]
        else:
            j = c - n_pc
            lo = j * tok - HALO
            hi = (j + 1) * tok + HALO
            slo, shi = max(lo, 0), min(hi, DS)
            xp[slo - lo:shi - lo] = x_sample[0, slo:shi]
            valid[slo - lo:shi - lo] = True
            mem = mem_sample[0]
        kb = np.where(valid, 0.0, NEG).astype(np.float32).reshape(NB, 128).T
        m = dict(shared)
        m.update(xpad=xp, mem=f(mem), kbias=np.ascontiguousarray(kb))
        in_maps.append(m)
    res = run_bass_kernel_spmd(nc, in_maps, core_ids=list(range(N_CORES)))
    if DEBUG:
        global _DBG
        _DBG = res.results
    outs = [np.asarray(r["y"], dtype=np.float32) for r in res.results]
    y_prompt = np.stack(outs[:n_pc], axis=0)
    y_sample = np.concatenate(outs[n_pc:], axis=0)[None]
    return (y_prompt, y_sample)
```
